# Optimizing a Trainium2 kernel written in Bass

```python
import jax, jax.numpy as jnp
from jax import lax
import numpy as np

D_MODEL = 1024
BATCH = 4
SEQ = 8192
DEPTH = 2

D_INNER = 2 * D_MODEL
D_MLSTM = D_INNER // 2
D_RET = D_INNER - D_MLSTM
N_HEADS_MLSTM = 4
N_HEADS_RET = 4
HD_M = D_MLSTM // N_HEADS_MLSTM
HD_R = D_RET // N_HEADS_RET
CONV_K = 4
CHUNK = 64
EPS = 1e-6
ROPE_BASE = 10000.0
D_IN_PROJ = 5 * D_MLSTM + 2 * N_HEADS_MLSTM + 4 * D_RET

kernel_name = 'hybrid_mlstm_retention_parallel_heads'


def rmsnorm(x, g):
    xf = x.astype(jnp.float32)
    y = xf * lax.rsqrt(jnp.mean(xf * xf, axis=-1, keepdims=True) + EPS)
    return (y * g.astype(jnp.float32)).astype(x.dtype)


def head_layernorm(h, g, n_heads):
    B, S, W = h.shape
    hf = h.astype(jnp.float32).reshape(B, S, n_heads, W // n_heads)
    mu = jnp.mean(hf, axis=-1, keepdims=True)
    var = jnp.mean(jnp.square(hf - mu), axis=-1, keepdims=True)
    y = ((hf - mu) * lax.rsqrt(var + EPS)).reshape(B, S, W)
    return (y * g.astype(jnp.float32)).astype(h.dtype)


def causal_dwconv(x, w, b):
    S = x.shape[1]
    xp = jnp.pad(x, ((0, 0), (CONV_K - 1, 0), (0, 0)))
    y = xp[:, 0:S] * w[0]
    for t in range(1, CONV_K):
        y = y + xp[:, t:t + S] * w[t]
    return y + b


def split_cols(p):
    sizes = [D_MLSTM] * 5 + [N_HEADS_MLSTM] * 2 + [D_RET] * 4
    idx = np.cumsum(sizes)[:-1].tolist()
    return jnp.split(p, idx, axis=-1)


def to_chunks(t):
    B, S, H, d = t.shape
    return t.reshape(B, S // CHUNK, CHUNK, H, d).transpose(1, 0, 3, 2, 4)


def from_chunks(t):
    NC, B, H, L, d = t.shape
    return t.transpose(1, 0, 3, 2, 4).reshape(B, NC * L, H * d)


def gate_chunks(t):
    B, S, H = t.shape
    return t.reshape(B, S // CHUNK, CHUNK, H).transpose(1, 0, 3, 2)


def rotary(t, positions):
    d = t.shape[-1]
    inv_freq = 1.0 / (ROPE_BASE ** jnp.linspace(0.0, 1.0, d // 2, dtype=jnp.float32))
    ang = positions.astype(jnp.float32)[..., None] * inv_freq
    cos = jnp.cos(ang)[:, :, None, :].astype(t.dtype)
    sin = jnp.sin(ang)[:, :, None, :].astype(t.dtype)
    t1, t2 = jnp.split(t, 2, axis=-1)
    return jnp.concatenate([t1 * cos - t2 * sin, t2 * cos + t1 * sin], axis=-1)


def mlstm_chunkwise(q, k, v, i_pre, f_pre):
    B, S, H, d = q.shape
    k = k * (d ** -0.5)
    li = gate_chunks(i_pre.astype(jnp.float32))
    lf = gate_chunks(jax.nn.log_sigmoid(f_pre.astype(jnp.float32)))
    causal = jnp.tril(jnp.ones((CHUNK, CHUNK), dtype=bool))

    def step(carry, inp):
        C, n, m = carry
        qj, kj, vj, lij, lfj = inp
        b = jnp.cumsum(lfj, axis=-1)
        a = b + m[..., None]
        g = b[..., :, None] - b[..., None, :] + lij[..., None, :]
        g = jnp.where(causal, g, -jnp.inf)
        m_row = jnp.maximum(a, jnp.max(g, axis=-1))
        w_intra = jnp.exp(g - m_row[..., None])
        w_inter = jnp.exp(a - m_row)
        s = jnp.einsum('bhld,bhsd->bhls', qj, kj) * w_intra
        num = (jnp.einsum('bhls,bhsv->bhlv', s, vj)
               + w_inter[..., None] * jnp.einsum('bhvk,bhlk->bhlv', C, qj))
        den = jnp.sum(s, axis=-1) + w_inter * jnp.einsum('bhk,bhlk->bhl', n, qj)
        h = num / jnp.maximum(jnp.abs(den), jnp.exp(-m_row))[..., None]
        bL = b[..., -1]
        g_end = bL[..., None] - b + lij
        m_new = jnp.maximum(bL + m, jnp.max(g_end, axis=-1))
        w_end = jnp.exp(g_end - m_new[..., None])
        decay = jnp.exp(bL + m - m_new)
        C_new = decay[..., None, None] * C + jnp.einsum('bhs,bhsv,bhsk->bhvk', w_end, vj, kj)
        n_new = decay[..., None] * n + jnp.einsum('bhs,bhsk->bhk', w_end, kj)
        return (C_new, n_new, m_new), h.astype(q.dtype)

    init = (jnp.zeros((B, H, d, d), jnp.float32),
            jnp.zeros((B, H, d), jnp.float32),
            jnp.zeros((B, H), jnp.float32))
    _, hs = lax.scan(step, init, (to_chunks(q), to_chunks(k), to_chunks(v), li, lf))
    return from_chunks(hs)


def retention_chunkwise(q, k, v):
    B, S, H, d = q.shape
    k = k * (d ** -0.5)
    log_gamma = jnp.log(1.0 - 2.0 ** (-5.0 - jnp.arange(H, dtype=jnp.float32)))
    pos = jnp.arange(CHUNK, dtype=jnp.float32)
    diff = pos[:, None] - pos[None, :]
    Dmat = jnp.where(diff >= 0, jnp.exp(log_gamma[:, None, None] * jnp.maximum(diff, 0.0)), 0.0)
    xi = jnp.exp(log_gamma[:, None] * (pos + 1.0))
    zeta = jnp.exp(log_gamma[:, None] * (CHUNK - 1.0 - pos))
    g_chunk = jnp.exp(log_gamma * CHUNK)

    def step(R, inp):
        qj, kj, vj = inp
        s = jnp.einsum('bhld,bhsd->bhls', qj, kj) * Dmat
        o = (jnp.einsum('bhls,bhsv->bhlv', s, vj)
             + xi[..., None] * jnp.einsum('bhlk,bhkv->bhlv', qj, R))
        R_new = g_chunk[:, None, None] * R + jnp.einsum('bhsk,bhsv->bhkv', kj * zeta[..., None], vj)
        return R_new, o.astype(q.dtype)

    init = jnp.zeros((B, H, d, d), jnp.float32)
    _, os_ = lax.scan(step, init, (to_chunks(q), to_chunks(k), to_chunks(v)))
    return from_chunks(os_)


def setup_inputs(seed: int = 0) -> dict:
    key = jax.random.key(seed)
    ks = jax.random.split(key, 16)
    f32 = jnp.float32
    x = jax.random.normal(ks[0], (BATCH, SEQ, D_MODEL), f32)
    c = jax.random.normal(ks[1], (BATCH, D_MODEL), f32)
    offset = jax.random.randint(ks[2], (BATCH, 1), 0, 4096, dtype=jnp.int32)
    positions = offset + jnp.arange(SEQ, dtype=jnp.int32)[None, :]
    w_ada = jax.random.normal(ks[3], (DEPTH, D_MODEL, 3 * D_MODEL), f32) * (0.5 * D_MODEL ** -0.5)
    b_ada = jax.random.normal(ks[4], (DEPTH, 3 * D_MODEL), f32) * 0.02
    norm_g = 1.0 + 0.02 * jax.random.normal(ks[5], (DEPTH, D_MODEL), f32)
    w_in = jax.random.normal(ks[6], (DEPTH, D_MODEL, D_IN_PROJ), f32) * (D_MODEL ** -0.5)
    conv_w = jax.random.normal(ks[7], (DEPTH, CONV_K, 2 * D_MLSTM), f32) * (CONV_K ** -0.5)
    conv_b = jax.random.normal(ks[8], (DEPTH, 2 * D_MLSTM), f32) * 0.02
    b_igate = jax.random.normal(ks[9], (DEPTH, N_HEADS_MLSTM), f32) * 0.1
    b_fgate = (jnp.linspace(3.0, 6.0, N_HEADS_MLSTM, dtype=f32)[None, :]
               + 0.1 * jax.random.normal(ks[10], (DEPTH, N_HEADS_MLSTM), f32))
    gn_m = 1.0 + 0.02 * jax.random.normal(ks[11], (DEPTH, D_MLSTM), f32)
    gn_r = 1.0 + 0.02 * jax.random.normal(ks[12], (DEPTH, D_RET), f32)
    w_out = jax.random.normal(ks[13], (DEPTH, D_INNER, D_MODEL), f32) * (D_INNER ** -0.5)
    final_g = 1.0 + 0.02 * jax.random.normal(ks[14], (D_MODEL,), f32)
    return {'x': x, 'c': c, 'positions': positions, 'w_ada': w_ada, 'b_ada': b_ada,
            'norm_g': norm_g, 'w_in': w_in, 'conv_w': conv_w, 'conv_b': conv_b,
            'b_igate': b_igate, 'b_fgate': b_fgate, 'gn_m': gn_m, 'gn_r': gn_r,
            'w_out': w_out, 'final_g': final_g}


def reference(x, c, positions, w_ada, b_ada, norm_g, w_in, conv_w, conv_b,
              b_igate, b_fgate, gn_m, gn_r, w_out, final_g):
    B, S, _ = x.shape
    c_act = jax.nn.silu(c)
    for l in range(DEPTH):
        ada = c_act @ w_ada[l] + b_ada[l]
        shift, scale, gate = jnp.split(ada, 3, axis=-1)
        h = rmsnorm(x, norm_g[l]) * (1.0 + scale[:, None, :]) + shift[:, None, :]
        p = h @ w_in[l]
        mq, mk, mv, mo, mz, mi, mf, rq, rk, rv, rz = split_cols(p)

        qk = jax.nn.silu(causal_dwconv(jnp.concatenate([mq, mk], axis=-1), conv_w[l], conv_b[l]))
        mq_c, mk_c = jnp.split(qk, 2, axis=-1)
        cell = mlstm_chunkwise(mq_c.reshape(B, S, N_HEADS_MLSTM, HD_M),
                               mk_c.reshape(B, S, N_HEADS_MLSTM, HD_M),
                               mv.reshape(B, S, N_HEADS_MLSTM, HD_M),
                               mi + b_igate[l], mf + b_fgate[l])
        y_m = jax.nn.sigmoid(mo) * head_layernorm(cell, gn_m[l], N_HEADS_MLSTM) * jax.nn.silu(mz)

        rq_h = rotary(rq.reshape(B, S, N_HEADS_RET, HD_R), positions)
        rk_h = rotary(rk.reshape(B, S, N_HEADS_RET, HD_R), positions)
        ret = retention_chunkwise(rq_h, rk_h, rv.reshape(B, S, N_HEADS_RET, HD_R))
        y_r = head_layernorm(ret, gn_r[l], N_HEADS_RET) * jax.nn.silu(rz)

        y = jnp.concatenate([y_m, y_r], axis=-1) @ w_out[l]
        x = x + gate[:, None, :] * y
    return rmsnorm(x, final_g)
```

```python
import contextlib
import math
import numpy as np
import ml_dtypes
import concourse.bass as bass
import concourse.mybir as mybir
from concourse.bass_utils import run_bass_kernel_spmd

F32 = mybir.dt.float32
BF16 = mybir.dt.bfloat16
I32 = mybir.dt.int32
ALU = mybir.AluOpType
AF = mybir.ActivationFunctionType
AX = mybir.AxisListType

D = 1024
DEPTH = 2
EPS = 1e-6
TT = 512
NDMA = 24
EPOCH = 30000
LN16 = math.log(16.0)
TWO_PI = 2.0 * math.pi
CW1 = 6.28125
CW2 = TWO_PI - 6.28125


class T:
    def __init__(self, t):
        self.t = t
        self.lw = {}
        self.rd = {}

    def __getitem__(self, idx):
        return self.t[idx]


class TV(T):
    def __init__(self, base, ap):
        self.t = ap
        self.lw = base.lw
        self.rd = base.rd


class Sched:
    def __init__(self, nc, es):
        self.nc = nc
        self.es = es
        self.eng = {'pe': nc.tensor, 'act': nc.scalar, 'dve': nc.vector, 'pool': nc.gpsimd, 'sp': nc.sync}
        self.cnt = {e: 0 for e in self.eng}
        self.sems = {e: [] for e in self.eng}
        self.waited = {e: {} for e in self.eng}
        self.dma_sems = [es.enter_context(nc.semaphore("dma%d" % i)) for i in range(NDMA)]
        self.dma_val = [0] * NDMA
        self.dma_rr = 0
        self.cc_sem = es.enter_context(nc.semaphore("ccs"))
        self.cc_val = 0
        self.nwait = 0

    def _sem(self, e, ep):
        while len(self.sems[e]) <= ep:
            self.sems[e].append(self.es.enter_context(self.nc.semaphore("s_%s_%d" % (e, len(self.sems[e])))))
        return self.sems[e][ep]

    def _wait(self, e, key, val):
        w = self.waited[e]
        if w.get(key, 0) >= val:
            return
        w[key] = val
        self.nwait += 1
        if key[0] == 'E':
            ep = (val - 1) // EPOCH
            self.eng[e].wait_ge(self._sem(key[1], ep), val - ep * EPOCH)
        elif key[0] == 'D':
            self.eng[e].wait_ge(self.dma_sems[key[1]], val)
        else:
            self.eng[e].wait_ge(self.cc_sem, val)

    def op(self, e, reads, writes, fn, kind='c'):
        deps = {}
        for t in reads:
            for k, v in t.lw.items():
                deps[k] = max(deps.get(k, 0), v)
        for t in writes:
            for k, v in t.lw.items():
                deps[k] = max(deps.get(k, 0), v)
            for k, v in t.rd.items():
                deps[k] = max(deps.get(k, 0), v)
        for k, v in deps.items():
            if k == ('E', 'pe') and e == 'pe':
                continue
            self._wait(e, k, v)
        if kind == 'dma':
            i = self.dma_rr
            self.dma_rr = (i + 1) % NDMA
            if self.dma_val[i] > 0:
                self._wait(e, ('D', i), self.dma_val[i])
            inst = fn()
            self.dma_val[i] += 16
            inst.then_inc(self.dma_sems[i], 16)
            tok = (('D', i), self.dma_val[i])
        elif kind == 'cc':
            inst = fn()
            self.cc_val += 1
            inst.then_inc(self.cc_sem, 1)
            tok = (('C',), self.cc_val)
        else:
            inst = fn()
            self.cnt[e] += 1
            n = self.cnt[e]
            ep = (n - 1) // EPOCH
            inst.then_inc(self._sem(e, ep), 1)
            tok = (('E', e), n)
        for t in reads:
            t.rd[tok[0]] = max(t.rd.get(tok[0], 0), tok[1])
        for t in writes:
            t.lw.clear()
            t.lw[tok[0]] = tok[1]
            t.rd.clear()
        return tok

    def finish(self, e):
        for src in self.eng:
            if self.cnt[src] > 0 and not (src == e == 'pe'):
                self._wait(e, ('E', src), self.cnt[src])
        for i in range(NDMA):
            if self.dma_val[i] > 0:
                self._wait(e, ('D', i), self.dma_val[i])
        if self.cc_val > 0:
            self._wait(e, ('C',), self.cc_val)


def build(S=8192, debug=False):
    NT = S // TT
    nc = bass.Bass("TRN2", target_bir_lowering=False)
    es = contextlib.ExitStack()

    def din(name, shape, dt=F32):
        return nc.dram_tensor(name, list(shape), dt, kind="ExternalInput")

    x_d = din("x", [S, D])
    pos_d = din("pos", [1, S], I32)
    c_d = din("c_fm", [128, 8])
    wada_d = din("wada", [DEPTH, D, 3 * D])
    badafm_d = din("bada_fm", [DEPTH, 128, 16])
    badag_d = din("bada_g", [DEPTH, 1, D])
    normg_d = din("normg_fm", [DEPTH, 128, 8])
    wf_d = din("wf", [DEPTH, D, 2048])
    wt_d = din("wt", [DEPTH, D, 2560])
    wg_d = din("wg", [DEPTH, D, 4])
    cw_d = din("cw", [DEPTH, 128, 32])
    cb_d = din("cb", [DEPTH, 128, 8])
    bg_d = din("bg", [DEPTH, 2, 2, 1])
    gn_d = din("gn_fm", [DEPTH, 128, 8])
    wo_d = din("wo", [DEPTH, D, D])
    fg_d = din("finalg", [1, D])
    identb_d = din("ident_b", [128, 128], BF16)
    identf_d = din("ident_f", [128, 128])
    mask_d = din("mask01", [128, 128])
    mrt_d = din("mrt", [128, 256])
    tabs_d = din("tabs", [128, 8])
    bmask_d = din("bmask", [2, 8])
    out_d = nc.dram_tensor("out", [NT * 256, D], F32, kind="ExternalOutput")
    part_d = [nc.dram_tensor("part%d" % l, [NT, TT, D], F32) for l in range(DEPTH)]
    xs0_d = nc.dram_tensor("xs0", [NT, TT, D], F32)
    xs1_d = nc.dram_tensor("xs1", [NT, 256, D], F32)
    dbg = {}
    if debug:
        dbg['hT'] = nc.dram_tensor("dbg_hT", [128, 8, TT], BF16, kind="ExternalOutput")
        dbg['qT'] = nc.dram_tensor("dbg_qT", [128, 4, TT], BF16, kind="ExternalOutput")
        dbg['kT'] = nc.dram_tensor("dbg_kT", [128, 4, TT], BF16, kind="ExternalOutput")
        dbg['rqT'] = nc.dram_tensor("dbg_rqT", [128, 4, TT], BF16, kind="ExternalOutput")
        dbg['rkT'] = nc.dram_tensor("dbg_rkT", [128, 4, TT], BF16, kind="ExternalOutput")
        dbg['ytok'] = nc.dram_tensor("dbg_ytok", [128, 1024], BF16, kind="ExternalOutput")
        dbg['tg'] = nc.dram_tensor("dbg_tg", [128, 16], F32, kind="ExternalOutput")
        dbg['hbuf'] = nc.dram_tensor("dbg_hbuf", [128, 4 * 257], F32, kind="ExternalOutput")
        dbg['xo'] = nc.dram_tensor("dbg_xo", [128, 1024], F32, kind="ExternalOutput")

    with es:
        K = Sched(nc, es)

        def sb(name, shape, dt=F32):
            return T(es.enter_context(nc.sbuf_tensor("sb_" + name, list(shape), dt)))

        def dr(t):
            return T(t)

        PE, ACT, DVE, POOL, SP = 'pe', 'act', 'dve', 'pool', 'sp'
        E = K.eng

        def dma(out_t, out_ap, in_t, in_ap, q=SP):
            reads = [in_t] if in_t is not None else []
            writes = [out_t] if out_t is not None else []
            return K.op(q, reads, writes, lambda: E[q].dma_start(out=out_ap, in_=in_ap), kind='dma')

        banks = [es.enter_context(nc.psum_tensor("psb%d" % i, [128, 512], F32)) for i in range(8)]
        psA = [T(banks[0]), T(banks[1])]
        ps_s = T(banks[2])
        ps_num = T(banks[3])
        ps_u = [T(banks[4]), T(banks[5])]
        ps_tr = [T(banks[6]), T(banks[6])]
        ps_yt = T(banks[6])
        ps_sm = T(banks[7])
        ps_kt = ps_sm
        bank6_bf = banks[6][:].bitcast(BF16)
        kt_bf = banks[7][:, 128:256].bitcast(BF16)
        abank = [0]

        def next_ps():
            abank[0] ^= 1
            return psA[abank[0]]

        ident_b = sb("ident_b", [128, 128], BF16)
        ident_f = sb("ident_f", [128, 128])
        mask01 = sb("mask01", [128, 128])
        mrt = sb("mrt", [128, 2, 128])
        tabs = sb("tabs", [128, 8])
        bmask = sb("bmask", [2, 2, 4])
        c_fm = sb("c_fm", [128, 8])
        c_act = sb("c_act", [128, 8])
        onesK = sb("onesK", [2, 128])
        mhalf = sb("mhalf", [128, 4])
        dma(ident_b, ident_b[:], None, identb_d[:, :])
        dma(ident_f, ident_f[:], None, identf_d[:, :])
        dma(mask01, mask01[:], None, mask_d[:, :])
        dma(mrt, mrt[:], None, mrt_d.ap().rearrange("p (h l) -> p h l", h=2))
        dma(tabs, tabs[:], None, tabs_d[:, :])
        dma(bmask, bmask[:], None, bmask_d.ap().rearrange("p (h c) -> p h c", h=2))
        dma(c_fm, c_fm[:], None, c_d[:, :])
        K.op(DVE, [], [onesK], lambda: E[DVE].memset(onesK[:], 1.0))
        K.op(DVE, [], [mhalf], lambda: E[DVE].memset(mhalf[:], -0.5))
        K.op(ACT, [c_fm], [c_act], lambda: E[ACT].activation(out=c_act[:], in_=c_fm[:], func=AF.Silu))
        gl = lambda j: tabs[:, j:j + 1]
        zt = lambda j: tabs[:, 2 + j:3 + j]
        g128 = lambda j: tabs[:, 4 + j:5 + j]
        invf = tabs[:, 6:7]

        wf_bf = sb("wf_bf", [128, 8, 2048], BF16)
        wt_bf = sb("wt_bf", [128, 8, 2560], BF16)
        wg_bf = sb("wg_bf", [128, 8, 4], BF16)
        wo_bf = sb("wo_bf", [128, 8, D], BF16)
        sti = [0]
        cw = sb("cw", [128, 8, 4])
        cb = sb("cb", [128, 8])
        bgi = sb("bgi", [2, 1])
        bgf = sb("bgf", [2, 1])
        nbf = sb("nbf", [2, 1])
        gn_fm = sb("gn_fm", [128, 8])
        normg = sb("normg", [128, 8])
        bada_fm = sb("bada_fm", [128, 16])
        gsc = sb("gsc", [128, 8])
        shf = sb("shf", [128, 8])
        gate_bc = sb("gate_bc", [128, D])

        xin0 = sb("xin0", [128, D])
        xin = [xin0, xin0]
        ss = sb("ss", [128, 1])
        rstd = sb("rstd", [128, 1])
        xn = sb("xn", [128, D], BF16)
        hTs = [sb("hT%d" % i, [128, 8, TT], BF16) for i in range(2)]
        f1 = sb("f1", [128, TT])
        f2 = sb("f2", [128, TT])
        f3 = sb("f3", [128, TT])
        cosT = sb("cosT", [128, TT])
        sinT = sb("sinT", [128, TT])
        carry = sb("carry", [128, 8, 3])
        qTs = [[sb("qT%d_%d" % (j, i), [128, TT], BF16) for i in range(4)] for j in range(2)]
        kTs = [[sb("kT%d_%d" % (j, i), [128, TT], BF16) for i in range(4)] for j in range(2)]
        rqTs = [[sb("rqT%d_%d" % (j, i), [128, TT], BF16) for i in range(4)] for j in range(2)]
        rkTs = [[sb("rkT%d_%d" % (j, i), [128, TT], BF16) for i in range(4)] for j in range(2)]
        li = sb("li", [2, TT])
        ge = sb("ge", [2, TT])
        bneg = sb("bneg", [2, TT])
        bcar = sb("bcar", [2, 1])
        cm = sb("cm", [2, 4])
        gmall = sb("gmall", [2, 5])
        dd = sb("dd", [2, 4])
        dexp = sb("dexp", [2, 2, 4])
        tgs = [sb("tg%d" % i, [128, 4, 4]) for i in range(2)]
        dec_bcs = [sb("dec_bc%d" % i, [128, 2, 4]) for i in range(2)]
        Vms = [sb("Vm%d" % i, [128, 2, 257], BF16) for i in range(2)]
        Vrs = [sb("Vr%d" % i, [128, 2, 256], BF16) for i in range(2)]
        og = sb("og", [128, 512])
        rzg = sb("rzg", [128, 512])
        swTs = [sb("swT%d" % i, [128, 128], BF16) for i in range(2)]
        kws = [sb("kw%d" % i, [128, 256], BF16) for i in range(2)]
        C32 = [sb("C32_%d" % i, [128, 2, 257]) for i in range(2)]
        Cbf = [sb("Cbf_%d" % i, [128, 2, 257], BF16) for i in range(2)]
        R32 = [sb("R32_%d" % i, [128, 2, 256]) for i in range(2)]
        Rbf = [sb("Rbf_%d" % i, [128, 2, 256], BF16) for i in range(2)]
        hb = [sb("hb%d" % i, [128, 256]) for i in range(4)]
        den2 = sb("den2", [128, 2])
        st6 = sb("st6", [128, 4, 6])
        mv = sb("mv", [128, 4, 2])
        rinv = sb("rinv", [128, 4])
        v4 = sb("v4", [128, 4])
        sd4 = sb("sd4", [128, 4])
        s4 = sb("s4", [128, 4])
        nb4 = sb("nb4", [128, 4])
        ytok = sb("ytok", [128, D], BF16)
        yT = sb("yT", [128, 8, 128], BF16)
        xres = sb("xres", [128, D])
        xo = sb("xo", [128, D])
        gb_bc = xo
        junk = xn
        posi_v = f2[:, :].bitcast(I32)
        cacc = f3
        xpre = sb("xpre", [128, TT + 3])
        f4 = T(xpre.t)
        f4 = xpre
        U = li
        wl = li
        fl = bneg
        c_rep_v = xres[:, :].rearrange("p (k c) -> p k c", k=8)
        stage_ada = [TV(hTs[i], hTs[i][:, :, :].rearrange("p k t -> p (k t)").bitcast(F32)) for i in range(2)]
        stage_w = [xin0, xo, xres]
        for i in range(2):
            K.op(DVE, [], [Vms[i]], lambda i=i: E[DVE].memset(Vms[i][:], 1.0))
        K.op(DVE, [tabs], [rinv], lambda: E[DVE].tensor_copy(out=rinv[:, 2:4], in_=tabs[:, 0:2]))

        part_t = [[dr(part_d[l]) for _ in range(NT)] for l in range(DEPTH)]
        xs0_t = [dr(xs0_d) for _ in range(NT)]
        xs1_t = [dr(xs1_d) for _ in range(NT)]
        x_t = dr(x_d)
        out_t = dr(out_d)

        def dump(name, t, ap):
            if debug and name in dbg:
                dma(None, dbg[name].ap() if ap is None else ap, t, t[:])

        for l in range(DEPTH):
            dma(cw, cw[:], None, cw_d[l].rearrange("p (c k) -> p c k", c=8))
            dma(cb, cb[:], None, cb_d[l])
            dma(bgi, bgi[:], None, bg_d[l][0])
            dma(bgf, bgf[:], None, bg_d[l][1])
            dma(gn_fm, gn_fm[:], None, gn_d[l])
            dma(normg, normg[:], None, normg_d[l])
            dma(bada_fm, bada_fm[:], None, badafm_d[l])
            dma(gb_bc, gb_bc[:], None, badag_d[l].broadcast_to([128, D]))
            K.op(DVE, [c_act], [xres], lambda: E[DVE].tensor_copy(
                out=c_rep_v, in_=c_act[:].unsqueeze(2).broadcast_to([128, 8, 128])))
            K.op(DVE, [bgf], [nbf], lambda: E[DVE].tensor_scalar(
                out=nbf[:], in0=bgf[:], scalar1=-1.0, scalar2=None, op0=ALU.mult))
            for j in range(24):
                stg = stage_ada[j % 2]
                sv = stg[:, 0:1024].rearrange("p (k c) -> p k c", k=8)
                dma(stg, sv, None, wada_d[l][:, j * 128:(j + 1) * 128].rearrange("(k p) c -> p k c", p=128))
                if j < 16:
                    for ci in range(1):
                        col = j

                        def f(col=col, ci=ci, sv=sv):
                            for kt in range(8):
                                r = E[PE].matmul(banks[7][:, 64 + col:65 + col], lhsT=sv[:, kt, ci * 128:(ci + 1) * 128],
                                                 rhs=c_act[:, kt:kt + 1], start=(kt == 0), stop=(kt == 7))
                            return r
                        K.op(PE, [stg, c_act], [ps_sm], f)
                else:
                    pst = next_ps()

                    def f(pst=pst, sv=sv):
                        for kt in range(8):
                            r = E[PE].matmul(pst[:, 0:128], lhsT=c_rep_v[:, kt, :], rhs=sv[:, kt, :],
                                             start=(kt == 0), stop=(kt == 7))
                        return r
                    K.op(PE, [stg, xres], [pst], f)
                    K.op(DVE, [pst, gb_bc], [gate_bc], lambda pst=pst, j=j: E[DVE].tensor_tensor(
                        out=gate_bc[:, (j - 16) * 128:(j - 15) * 128], in0=pst[:, 0:128],
                        in1=gb_bc[:, (j - 16) * 128:(j - 15) * 128], op=ALU.add))
            K.op(DVE, [ps_sm, bada_fm], [shf], lambda: E[DVE].tensor_tensor(
                out=shf[:], in0=banks[7][:, 64:72], in1=bada_fm[:, 0:8], op=ALU.add))
            K.op(DVE, [ps_sm, bada_fm], [gsc], lambda: E[DVE].scalar_tensor_tensor(
                out=gsc[:], in0=banks[7][:, 72:80], scalar=1.0, in1=bada_fm[:, 8:16], op0=ALU.add, op1=ALU.add))
            K.op(DVE, [gsc, normg], [gsc], lambda: E[DVE].tensor_tensor(
                out=gsc[:], in0=gsc[:], in1=normg[:], op=ALU.mult))
            ci_ = 0
            for (wd, wb, ncol, scl) in ((wf_d, wf_bf, 2048, False), (wg_d, wg_bf, 4, False),
                                        (wt_d, wt_bf, 2560, False), (wo_d, wo_bf, D, True)):
                for kt in range(8):
                  for c0 in range(0, ncol, 1024):
                    c1 = min(ncol, c0 + 1024)
                    stg = stage_w[sti[0] % 3]
                    sti[0] += 1
                    dma(stg, stg[:, 0:c1 - c0], None, wd[l][kt * 128:(kt + 1) * 128, c0:c1])
                    ce = (POOL, ACT)[ci_ % 2]
                    ci_ += 1
                    if scl:
                        K.op(DVE, [stg, gn_fm], [wb], lambda stg=stg, wb=wb, kt=kt, c0=c0, c1=c1: E[DVE].tensor_scalar(
                            out=wb[:, kt, c0:c1], in0=stg[:, 0:c1 - c0], scalar1=gn_fm[:, kt:kt + 1], scalar2=None,
                            op0=ALU.mult))
                    elif ce == POOL:
                        K.op(POOL, [stg], [wb], lambda stg=stg, wb=wb, kt=kt, c0=c0, c1=c1: E[POOL].tensor_copy(
                            out=wb[:, kt, c0:c1], in_=stg[:, 0:c1 - c0]))
                    else:
                        K.op(ACT, [stg], [wb], lambda stg=stg, wb=wb, kt=kt, c0=c0, c1=c1: E[ACT].activation(
                            out=wb[:, kt, c0:c1], in_=stg[:, 0:c1 - c0], func=AF.Copy))
            for i in range(2):
                K.op(POOL, [], [C32[i]], lambda i=i: E[POOL].memset(C32[i][:], 0.0))
                K.op(POOL, [], [Cbf[i]], lambda i=i: E[POOL].memset(Cbf[i][:], 0.0))
                K.op(POOL, [], [R32[i]], lambda i=i: E[POOL].memset(R32[i][:], 0.0))
                K.op(POOL, [], [Rbf[i]], lambda i=i: E[POOL].memset(Rbf[i][:], 0.0))
            K.op(POOL, [], [carry], lambda: E[POOL].memset(carry[:], 0.0))
            K.op(POOL, [], [bcar], lambda: E[POOL].memset(bcar[:], 0.0))
            K.op(POOL, [], [gmall], lambda: E[POOL].memset(gmall[:], 0.0))

            def gen_ABC(ti):
                par = ti % 2
                hT, qT, kT, rqT, rkT, tg, dec_bc = hTs[par], qTs[par], kTs[par], rqTs[par], rkTs[par], tgs[par], dec_bcs[par]
                for sub in range(4):
                    xi = xin[sub % 2]
                    if l == 0:
                        dma(xi, xi[:], x_t, x_d[ti * TT + sub * 128: ti * TT + (sub + 1) * 128, :])
                    else:
                        dma(xi, xi[:], xs0_t[ti], xs0_d[ti][sub * 128:(sub + 1) * 128, :])
                    K.op(ACT, [xi], [junk, ss], lambda xi=xi: E[ACT].activation(
                        out=junk[:], in_=xi[:], func=AF.Square, accum_out=ss[:]))
                    K.op(DVE, [ss], [ss], lambda: E[DVE].tensor_scalar(
                        out=ss[:], in0=ss[:], scalar1=1.0 / D, scalar2=EPS, op0=ALU.mult, op1=ALU.add))
                    K.op(POOL, [ss, mhalf], [rstd], lambda: E[POOL].tensor_tensor(out=rstd[:], in0=ss[:], in1=mhalf[:, 0:1], op=ALU.pow))
                    K.op(DVE, [xi, rstd], [xn], lambda xi=xi, sub=sub: E[DVE].tensor_scalar(
                        out=xn[:], in0=xi[:], scalar1=rstd[:, 0:1], scalar2=None, op0=ALU.mult))

                    def f():
                        for kt in range(8):
                            r = E[PE].transpose(out=bank6_bf[:, kt * 128:(kt + 1) * 128],
                                                in_=xn[:, kt * 128:(kt + 1) * 128], identity=ident_b[:])
                        return r
                    K.op(PE, [xn, ident_b], [ps_tr[0], ps_tr[1]], f)
                    for kt in range(8):
                        K.op(ACT, [ps_tr[0], ps_tr[1], gsc, shf], [hT], lambda kt=kt, sub=sub: E[ACT].activation(
                            out=hT[:, kt, sub * 128:(sub + 1) * 128], in_=bank6_bf[:, kt * 128:(kt + 1) * 128],
                            func=AF.Identity, scale=gsc[:, kt:kt + 1], bias=shf[:, kt:kt + 1]))
                    yield
                if l == 0 and ti == 0:
                    dump('hT', hT, None)
                dma(f2, posi_v, None, pos_d[0:1, ti * TT:(ti + 1) * TT].broadcast_to([128, TT]))
                K.op(DVE, [f2], [f1], lambda: E[DVE].tensor_copy(out=f1[:], in_=posi_v))
                K.op(DVE, [f1, tabs], [f1], lambda: E[DVE].tensor_scalar(
                    out=f1[:], in0=f1[:], scalar1=invf, scalar2=None, op0=ALU.mult))
                K.op(DVE, [f1], [f2], lambda: E[DVE].tensor_scalar(
                    out=posi_v, in0=f1[:], scalar1=1.0 / TWO_PI, scalar2=None, op0=ALU.mult))
                K.op(DVE, [f2], [f3], lambda: E[DVE].tensor_copy(out=f3[:], in_=posi_v))
                K.op(DVE, [f3, f1], [f1], lambda: E[DVE].scalar_tensor_tensor(
                    out=f1[:], in0=f3[:], scalar=-CW1, in1=f1[:], op0=ALU.mult, op1=ALU.add))
                K.op(DVE, [f3, f1], [f1], lambda: E[DVE].scalar_tensor_tensor(
                    out=f1[:], in0=f3[:], scalar=-CW2, in1=f1[:], op0=ALU.mult, op1=ALU.add))
                K.op(DVE, [f1], [f3], lambda: E[DVE].tensor_scalar(
                    out=f3[:], in0=f1[:], scalar1=math.pi, scalar2=-TWO_PI, op0=ALU.is_gt, op1=ALU.mult))
                K.op(DVE, [f1, f3], [f3], lambda: E[DVE].tensor_tensor(out=f3[:], in0=f3[:], in1=f1[:], op=ALU.add))
                K.op(DVE, [f3], [f3], lambda: E[DVE].tensor_scalar(
                    out=f3[:], in0=f3[:], scalar1=-math.pi, scalar2=math.pi, op0=ALU.max, op1=ALU.min))
                K.op(ACT, [f3], [sinT], lambda: E[ACT].activation(out=sinT[:], in_=f3[:], func=AF.Sin))
                K.op(DVE, [f1], [f2], lambda: E[DVE].tensor_scalar(
                    out=f2[:], in0=f1[:], scalar1=math.pi / 2, scalar2=-TWO_PI, op0=ALU.is_gt, op1=ALU.mult))
                K.op(DVE, [f1, f2], [f2], lambda: E[DVE].scalar_tensor_tensor(
                    out=f2[:], in0=f1[:], scalar=math.pi / 2, in1=f2[:], op0=ALU.add, op1=ALU.add))
                K.op(DVE, [f2], [f2], lambda: E[DVE].tensor_scalar(
                    out=f2[:], in0=f2[:], scalar1=-math.pi, scalar2=math.pi, op0=ALU.max, op1=ALU.min))
                K.op(ACT, [f2], [cosT], lambda: E[ACT].activation(out=cosT[:], in_=f2[:], func=AF.Sin))
                yield

                def fm_mm(pst, ct):
                    def f():
                        for kt in range(8):
                            r = E[PE].matmul(pst[:, :], lhsT=wf_bf[:, kt, ct * 128:(ct + 1) * 128], rhs=hT[:, kt, :],
                                             start=(kt == 0), stop=(kt == 7))
                        return r
                    K.op(PE, [wf_bf, hT], [pst], f)

                for ct in range(8):
                    pst = next_ps()
                    fm_mm(pst, ct)
                    dst = qT[ct] if ct < 4 else kT[ct - 4]
                    K.op(POOL, [carry], [xpre], lambda ct=ct: E[POOL].tensor_copy(out=xpre[:, 0:3], in_=carry[:, ct, :]))
                    K.op(ACT, [pst], [xpre], lambda pst=pst: E[ACT].activation(
                        out=xpre[:, 3:TT + 3], in_=pst[:, :], func=AF.Copy))
                    K.op(POOL, [xpre], [carry], lambda ct=ct: E[POOL].tensor_copy(
                        out=carry[:, ct, :], in_=xpre[:, TT:TT + 3]))
                    K.op(DVE, [xpre, cw, cb], [cacc], lambda ct=ct: E[DVE].tensor_scalar(
                        out=cacc[:], in0=xpre[:, 3:TT + 3], scalar1=cw[:, ct, 3:4], scalar2=cb[:, ct:ct + 1],
                        op0=ALU.mult, op1=ALU.add))
                    for tap in (2, 1, 0):
                        K.op(DVE, [xpre, cw, cacc], [cacc], lambda ct=ct, tap=tap: E[DVE].scalar_tensor_tensor(
                            out=cacc[:], in0=xpre[:, tap:tap + TT], scalar=cw[:, ct, tap:tap + 1], in1=cacc[:],
                            op0=ALU.mult, op1=ALU.add))
                    K.op(ACT, [cacc], [dst], lambda dst=dst: E[ACT].activation(out=dst[:], in_=cacc[:], func=AF.Silu))
                    yield
                for p in range(4):
                    p1 = next_ps()
                    fm_mm(p1, 8 + 2 * p)
                    p2 = next_ps()
                    fm_mm(p2, 9 + 2 * p)
                    d1 = (rqT if p < 2 else rkT)[2 * (p % 2)]
                    d2 = (rqT if p < 2 else rkT)[2 * (p % 2) + 1]
                    K.op(DVE, [p1, cosT], [f3], lambda p1=p1: E[DVE].tensor_tensor(out=f3[:], in0=p1[:, :], in1=cosT[:], op=ALU.mult))
                    K.op(DVE, [p1, sinT], [f2], lambda p1=p1: E[DVE].tensor_tensor(out=f2[:], in0=p1[:, :], in1=sinT[:], op=ALU.mult))
                    K.op(DVE, [p2, sinT], [f4], lambda p2=p2: E[DVE].tensor_tensor(out=f4[:, 0:TT], in0=p2[:, :], in1=sinT[:], op=ALU.mult))
                    K.op(DVE, [p2, cosT], [f1], lambda p2=p2: E[DVE].tensor_tensor(out=f1[:], in0=p2[:, :], in1=cosT[:], op=ALU.mult))
                    K.op(POOL, [f3, f4], [d1], lambda d1=d1: E[POOL].tensor_tensor(out=d1[:], in0=f3[:], in1=f4[:, 0:TT], op=ALU.subtract))
                    K.op(POOL, [f1, f2], [d2], lambda d2=d2: E[POOL].tensor_tensor(out=d2[:], in0=f1[:], in1=f2[:], op=ALU.add))
                    yield
                pgi = next_ps()

                def f(pgi=pgi):
                    for kt in range(8):
                        r = E[PE].matmul(pgi[0:2, :], lhsT=wg_bf[:, kt, 0:2], rhs=hT[:, kt, :], start=(kt == 0), stop=(kt == 7))
                    return r
                K.op(PE, [wg_bf, hT], [pgi], f)
                K.op(ACT, [pgi, bgi], [li], lambda pgi=pgi: E[ACT].activation(
                    out=li[:], in_=pgi[0:2, :], func=AF.Identity, bias=bgi[:, 0:1], scale=1.0))
                pgf = next_ps()

                def f(pgf=pgf):
                    for kt in range(8):
                        r = E[PE].matmul(pgf[0:2, :], lhsT=wg_bf[:, kt, 2:4], rhs=hT[:, kt, :], start=(kt == 0), stop=(kt == 7))
                    return r
                K.op(PE, [wg_bf, hT], [pgf], f)
                K.op(ACT, [pgf, nbf], [ge], lambda pgf=pgf: E[ACT].activation(
                    out=ge[:], in_=pgf[0:2, :], func=AF.Exp, bias=nbf[:, 0:1], scale=-1.0))
                K.op(ACT, [ge], [ge], lambda: E[ACT].activation(out=ge[:], in_=ge[:], func=AF.Ln, bias=1.0, scale=1.0))
                K.op(DVE, [ge, bcar], [bneg], lambda: E[DVE].tensor_tensor_scan(
                    out=bneg[:], data0=ge[:], data1=ge[:], initial=bcar[:, 0:1], op0=ALU.add, op1=ALU.max))
                K.op(DVE, [bneg], [bcar], lambda: E[DVE].tensor_copy(out=bcar[:], in_=bneg[:, TT - 1:TT]))
                K.op(DVE, [li, bneg], [U], lambda: E[DVE].tensor_tensor(out=U[:], in0=li[:], in1=bneg[:], op=ALU.add))
                K.op(DVE, [U], [cm], lambda: E[DVE].tensor_reduce(
                    out=cm[:], in_=U[:].rearrange("p (c t) -> p c t", c=4), axis=AX.X, op=ALU.max))
                K.op(DVE, [gmall], [gmall], lambda: E[DVE].tensor_copy(out=gmall[:, 0:1], in_=gmall[:, 4:5]))
                K.op(DVE, [cm, gmall], [gmall], lambda: E[DVE].tensor_tensor_scan(
                    out=gmall[:, 1:5], data0=cm[:], data1=cm[:], initial=gmall[:, 0:1], op0=ALU.max, op1=ALU.max))
                K.op(DVE, [gmall], [dd], lambda: E[DVE].tensor_tensor(
                    out=dd[:], in0=gmall[:, 0:4], in1=gmall[:, 1:5], op=ALU.subtract))
                K.op(ACT, [dd], [dd], lambda: E[ACT].activation(out=dd[:], in_=dd[:], func=AF.Exp))
                K.op(DVE, [dd, bmask], [dexp], lambda: E[DVE].tensor_tensor(
                    out=dexp[:], in0=dd[:].unsqueeze(1).broadcast_to([2, 2, 4]), in1=bmask[:], op=ALU.mult))
                K.op(PE, [onesK, dexp], [ps_sm], lambda: E[PE].matmul(
                    banks[7][:, 0:8], lhsT=onesK[:], rhs=dexp[:].rearrange("p h c -> p (h c)"), start=True, stop=True))
                K.op(DVE, [ps_sm], [dec_bc], lambda: E[DVE].tensor_copy(
                    out=dec_bc[:].rearrange("p h c -> p (h c)"), in_=banks[7][:, 0:8]))
                gmb = gmall[:, 1:5].unsqueeze(2).broadcast_to([2, 4, 128])
                K.op(DVE, [U, gmall], [wl], lambda: E[DVE].tensor_tensor(
                    out=wl[:].rearrange("p (c t) -> p c t", c=4), in0=U[:].rearrange("p (c t) -> p c t", c=4),
                    in1=gmb, op=ALU.subtract))
                K.op(ACT, [wl], [wl], lambda: E[ACT].activation(out=wl[:], in_=wl[:], func=AF.Exp, bias=-LN16, scale=1.0))
                K.op(DVE, [bneg, gmall], [fl], lambda: E[DVE].tensor_tensor(
                    out=fl[:].rearrange("p (c t) -> p c t", c=4), in0=bneg[:].rearrange("p (c t) -> p c t", c=4),
                    in1=gmb, op=ALU.subtract))
                K.op(ACT, [fl], [fl], lambda: E[ACT].activation(out=fl[:], in_=fl[:], func=AF.Exp))

                def f():
                    for c in range(4):
                        E[PE].transpose(out=banks[7][:, 16 + c * 4:18 + c * 4], in_=wl[:, c * 128:(c + 1) * 128],
                                        identity=ident_f[0:2, 0:2])
                        r = E[PE].transpose(out=banks[7][:, 18 + c * 4:20 + c * 4], in_=fl[:, c * 128:(c + 1) * 128],
                                            identity=ident_f[0:2, 0:2])
                    return r
                K.op(PE, [wl, fl, ident_f], [ps_sm], f)
                K.op(DVE, [ps_sm], [tg], lambda: E[DVE].tensor_copy(
                    out=tg[:].rearrange("p c k -> p (c k)"), in_=banks[7][:, 16:32]))
                yield
                if l == 0 and ti == 0:
                    dump('tg', tg, None)
                    for i in range(4):
                        dump('qT', qT[i], dbg['qT'][:, i, :] if debug else None)
                        dump('kT', kT[i], dbg['kT'][:, i, :] if debug else None)
                        dump('rqT', rqT[i], dbg['rqT'][:, i, :] if debug else None)
                        dump('rkT', rkT[i], dbg['rkT'][:, i, :] if debug else None)

            def gen_D(ti):
                par = ti % 2
                hT, qT, kT, rqT, rkT, tg, dec_bc = hTs[par], qTs[par], kTs[par], rqTs[par], rkTs[par], tgs[par], dec_bcs[par]

                def tm_unit(c, gi):
                    cs = slice(c * 128, (c + 1) * 128)
                    Vm, Vr = Vms[c % 2], Vrs[c % 2]
                    pst = next_ps()

                    def f():
                        for kt in range(8):
                            r = E[PE].matmul(pst[:, :], lhsT=hT[:, kt, cs], rhs=wt_bf[:, kt, gi * 512:(gi + 1) * 512],
                                             start=(kt == 0), stop=(kt == 7))
                        return r
                    K.op(PE, [wt_bf, hT], [pst], f)
                    if gi == 0:
                        K.op(ACT, [pst], [Vm], lambda: E[ACT].activation(
                            out=Vm[:, :, 0:256], in_=pst[:, :].rearrange("p (h d) -> p h d", h=2), func=AF.Copy))
                    elif gi == 1:
                        K.op(ACT, [pst], [og], lambda: E[ACT].activation(out=og[:], in_=pst[:, :], func=AF.Sigmoid))
                    elif gi == 2:
                        K.op(ACT, [pst], [pst], lambda: E[ACT].activation(out=pst[:, :], in_=pst[:, :], func=AF.Silu))
                        K.op(DVE, [og, pst], [og], lambda: E[DVE].tensor_tensor(out=og[:], in0=og[:], in1=pst[:, :], op=ALU.mult))
                    elif gi == 3:
                        K.op(ACT, [pst], [Vr], lambda: E[ACT].activation(
                            out=Vr[:].rearrange("p h d -> p (h d)"), in_=pst[:, :], func=AF.Copy))
                    else:
                        K.op(ACT, [pst], [rzg], lambda: E[ACT].activation(out=rzg[:], in_=pst[:, :], func=AF.Silu))

                tm_unit(0, 0)
                tm_unit(0, 3)
                yield
                for c in range(4):
                    cs = slice(c * 128, (c + 1) * 128)
                    Vm, Vr = Vms[c % 2], Vrs[c % 2]
                    fill = [1, 2, 4]
                    for hh in range(2):
                        wcol = tg[:, c, hh:hh + 1]
                        dcol = dec_bc[:, hh, c:c + 1]
                        swm, swr, kwm, kwr = swTs[0], swTs[1], kws[0], kws[1]
                        K.op(ACT, [C32[hh], dec_bc], [Cbf[hh]], lambda hh=hh, dcol=dcol: E[ACT].activation(
                            out=Cbf[hh][:].rearrange("p a b -> p (a b)"), in_=C32[hh][:].rearrange("p a b -> p (a b)"),
                            func=AF.Identity, scale=dcol, bias=0.0))
                        def f(hh=hh):
                            for dt in range(2):
                                r = E[PE].matmul(banks[2][:, 0:128], lhsT=kT[2 * hh + dt][:, cs], rhs=qT[2 * hh + dt][:, cs],
                                                 start=(dt == 0), stop=(dt == 1))
                            return r
                        K.op(PE, [kT[2 * hh], kT[2 * hh + 1], qT[2 * hh], qT[2 * hh + 1]], [ps_s], f)
                        K.op(DVE, [ps_s, tg, mask01], [swm], lambda wcol=wcol: E[DVE].scalar_tensor_tensor(
                            out=swm[:], in0=banks[2][:, 0:128], scalar=wcol, in1=mask01[:], op0=ALU.mult, op1=ALU.mult))
                        def f(hh=hh):
                            for dt in range(2):
                                r = E[PE].transpose(out=kt_bf[:, dt * 128:(dt + 1) * 128], in_=kT[2 * hh + dt][:, cs],
                                                    identity=ident_b[:])
                            return r
                        K.op(PE, [kT[2 * hh], kT[2 * hh + 1], ident_b], [ps_kt], f)
                        K.op(ACT, [ps_kt, tg], [kwm], lambda wcol=wcol: E[ACT].activation(
                            out=kwm[:], in_=kt_bf, func=AF.Identity, scale=wcol, bias=0.0))
                        def f(hh=hh):
                            for dt in range(2):
                                r = E[PE].matmul(banks[2][:, 0:128], lhsT=rkT[2 * hh + dt][:, cs], rhs=rqT[2 * hh + dt][:, cs],
                                                 start=(dt == 0), stop=(dt == 1))
                            return r
                        K.op(PE, [rkT[2 * hh], rkT[2 * hh + 1], rqT[2 * hh], rqT[2 * hh + 1]], [ps_s], f)
                        K.op(DVE, [ps_s, mrt], [swr], lambda hh=hh: E[DVE].tensor_tensor(
                            out=swr[:], in0=banks[2][:, 0:128], in1=mrt[:, hh, :], op=ALU.mult))
                        def f(hh=hh):
                            for dt in range(2):
                                r = E[PE].transpose(out=kt_bf[:, dt * 128:(dt + 1) * 128], in_=rkT[2 * hh + dt][:, cs],
                                                    identity=ident_b[:])
                            return r
                        K.op(PE, [rkT[2 * hh], rkT[2 * hh + 1], ident_b], [ps_kt], f)
                        K.op(ACT, [ps_kt, tabs], [kwr], lambda hh=hh: E[ACT].activation(
                            out=kwr[:], in_=kt_bf, func=AF.Identity, scale=zt(hh), bias=0.0))
                        tm_unit(c, fill.pop(0))
                        def f(hh=hh):
                            E[PE].matmul(banks[3][:, 0:257], lhsT=swm[:], rhs=Vm[:, hh, :], start=True, stop=False)
                            E[PE].matmul(banks[3][:, 0:257], lhsT=qT[2 * hh][:, cs], rhs=Cbf[hh][:, 0, :], start=False, stop=False)
                            return E[PE].matmul(banks[3][:, 0:257], lhsT=qT[2 * hh + 1][:, cs], rhs=Cbf[hh][:, 1, :], start=False, stop=True)
                        K.op(PE, [swm, Vm, qT[2 * hh], qT[2 * hh + 1], Cbf[hh]], [ps_num], f)
                        K.op(ACT, [ps_num], [hb[hh]], lambda hh=hh: E[ACT].activation(
                            out=hb[hh][:], in_=banks[3][:, 0:256], func=AF.Copy))
                        K.op(ACT, [ps_num], [den2], lambda hh=hh: E[ACT].activation(
                            out=den2[:, hh:hh + 1], in_=banks[3][:, 256:257], func=AF.Copy))
                        for dt in range(2):
                            K.op(PE, [kwm, Vm], [ps_u[dt]], lambda hh=hh, dt=dt: E[PE].matmul(
                                banks[4 + dt][:, 0:257], lhsT=kwm[:, dt * 128:(dt + 1) * 128], rhs=Vm[:, hh, :],
                                start=True, stop=True))
                            K.op(DVE, [C32[hh], dec_bc, ps_u[dt]], [C32[hh]], lambda hh=hh, dt=dt, dcol=dcol: E[DVE].scalar_tensor_tensor(
                                out=C32[hh][:, dt, :], in0=C32[hh][:, dt, :], scalar=dcol, in1=banks[4 + dt][:, 0:257],
                                op0=ALU.mult, op1=ALU.add))
                        def f(hh=hh):
                            E[PE].matmul(banks[3][:, 0:256], lhsT=swr[:], rhs=Vr[:, hh, :], start=True, stop=False)
                            E[PE].matmul(banks[3][:, 0:256], lhsT=rqT[2 * hh][:, cs], rhs=Rbf[hh][:, 0, :], start=False, stop=False)
                            return E[PE].matmul(banks[3][:, 0:256], lhsT=rqT[2 * hh + 1][:, cs], rhs=Rbf[hh][:, 1, :],
                                                start=False, stop=True)
                        K.op(PE, [swr, Vr, rqT[2 * hh], rqT[2 * hh + 1], Rbf[hh]], [ps_num], f)
                        K.op(ACT, [ps_num], [hb[2 + hh]], lambda hh=hh: E[ACT].activation(
                            out=hb[2 + hh][:], in_=banks[3][:, 0:256], func=AF.Copy))
                        if fill and hh == 1:
                            tm_unit(c, fill.pop(0))
                        for dt in range(2):
                            K.op(PE, [kwr, Vr], [ps_u[dt]], lambda hh=hh, dt=dt: E[PE].matmul(
                                banks[4 + dt][:, 0:256], lhsT=kwr[:, dt * 128:(dt + 1) * 128], rhs=Vr[:, hh, :],
                                start=True, stop=True))
                            K.op(DVE, [R32[hh], tabs, ps_u[dt]], [R32[hh]], lambda hh=hh, dt=dt: E[DVE].scalar_tensor_tensor(
                                out=R32[hh][:, dt, :], in0=R32[hh][:, dt, :], scalar=g128(hh), in1=banks[4 + dt][:, 0:256],
                                op0=ALU.mult, op1=ALU.add))
                        K.op(POOL, [R32[hh]], [Rbf[hh]], lambda hh=hh: E[POOL].tensor_copy(out=Rbf[hh][:], in_=R32[hh][:]))
                        for j in (hh, 2 + hh):
                            K.op(DVE, [hb[j]], [st6], lambda j=j: E[DVE].bn_stats(out=st6[:, j, :], in_=hb[j][:]))
                            K.op(DVE, [st6], [mv], lambda j=j: E[DVE].bn_aggr(out=mv[:, j, :], in_=st6[:, j, :]))
                        yield
                        yield

                    if c < 3:
                        tm_unit(c + 1, 0)
                        tm_unit(c + 1, 3)
                    yield
                    yield
                    K.op(DVE, [den2], [v4], lambda: E[DVE].tensor_scalar(
                        out=v4[:, 0:2], in0=den2[:], scalar1=-1.0, scalar2=None, op0=ALU.mult))
                    K.op(DVE, [den2, v4], [v4], lambda: E[DVE].tensor_tensor(
                        out=v4[:, 0:2], in0=v4[:, 0:2], in1=den2[:], op=ALU.max))
                    K.op(DVE, [v4, tg], [v4], lambda c=c: E[DVE].tensor_tensor(
                        out=v4[:, 0:2], in0=v4[:, 0:2], in1=tg[:, c, 2:4], op=ALU.max))
                    K.op(DVE, [v4], [rinv], lambda: E[DVE].reciprocal(out=rinv[:, 0:2], in_=v4[:, 0:2]))
                    K.op(DVE, [rinv], [v4], lambda: E[DVE].tensor_tensor(out=v4[:], in0=rinv[:], in1=rinv[:], op=ALU.mult))
                    K.op(DVE, [v4, mv], [v4], lambda: E[DVE].tensor_tensor(out=v4[:], in0=v4[:], in1=mv[:, :, 1], op=ALU.mult))
                    K.op(DVE, [v4], [v4], lambda: E[DVE].tensor_scalar(out=v4[:], in0=v4[:], scalar1=EPS, scalar2=None, op0=ALU.add))
                    K.op(POOL, [v4, mhalf], [s4], lambda: E[POOL].tensor_tensor(out=s4[:], in0=v4[:], in1=mhalf[:], op=ALU.pow))
                    K.op(DVE, [s4, rinv], [s4], lambda: E[DVE].tensor_tensor(out=s4[:], in0=s4[:], in1=rinv[:], op=ALU.mult))
                    K.op(DVE, [mv, s4], [nb4], lambda: E[DVE].scalar_tensor_tensor(
                        out=nb4[:], in0=mv[:, :, 0], scalar=-1.0, in1=s4[:], op0=ALU.mult, op1=ALU.mult))
                    for j in range(4):
                        K.op(ACT, [hb[j], s4, nb4], [hb[j]], lambda j=j: E[ACT].activation(
                            out=hb[j][:], in_=hb[j][:], func=AF.Identity, scale=s4[:, j:j + 1], bias=nb4[:, j:j + 1]))
                    for j in range(4):
                        gsrc = og if j < 2 else rzg
                        K.op(DVE, [hb[j], gsrc], [ytok], lambda j=j, gsrc=gsrc: E[DVE].tensor_tensor(
                            out=ytok[:, j * 256:(j + 1) * 256], in0=hb[j][:], in1=gsrc[:, (j % 2) * 256:(j % 2 + 1) * 256],
                            op=ALU.mult))

                    def f():
                        for ft in range(8):
                            r = E[PE].transpose(out=bank6_bf[:, ft * 128:(ft + 1) * 128], in_=ytok[:, ft * 128:(ft + 1) * 128],
                                                identity=ident_b[:])
                        return r
                    K.op(PE, [ytok, ident_b], [ps_tr[0], ps_tr[1]], f)
                    K.op(ACT, [ps_tr[0], ps_tr[1]], [yT], lambda: E[ACT].activation(
                        out=yT[:].rearrange("p f t -> p (f t)"), in_=bank6_bf, func=AF.Copy))
                    if l == 0:
                        dma(xres, xres[:], x_t, x_d[ti * TT + c * 128: ti * TT + (c + 1) * 128, :])
                    else:
                        dma(xres, xres[:], xs0_t[ti], xs0_d[ti][c * 128:(c + 1) * 128, :])
                    for half in range(2):
                        pst = next_ps()

                        def f(pst=pst, half=half):
                            for ft in range(8):
                                r = E[PE].matmul(pst[:, :], lhsT=yT[:, ft, :], rhs=wo_bf[:, ft, half * 512:(half + 1) * 512],
                                                 start=(ft == 0), stop=(ft == 7))
                            return r
                        K.op(PE, [yT, wo_bf], [pst], f)
                        hs = slice(half * 512, (half + 1) * 512)
                        K.op(DVE, [pst, gate_bc], [xo], lambda pst=pst, hs=hs: E[DVE].tensor_tensor(
                            out=xo[:, hs], in0=pst[:, :], in1=gate_bc[:, hs], op=ALU.mult))
                        K.op(DVE, [xres, xo], [xo], lambda hs=hs: E[DVE].scalar_tensor_tensor(
                            out=xo[:, hs], in0=xres[:, hs], scalar=0.5, in1=xo[:, hs], op0=ALU.mult, op1=ALU.add))
                    dma(part_t[l][ti], part_d[l][ti][c * 128:(c + 1) * 128, :], xo, xo[:])
                    yield
                if l == 0:
                    K.op(POOL, [part_t[l][ti]], [xs0_t[ti]], lambda ti=ti: E[POOL].collective_compute(
                        "AllReduce", ALU.add, replica_groups=[[0, 1], [2, 3], [4, 5], [6, 7]],
                        ins=[part_d[0][ti].opt()], outs=[xs0_d[ti].opt()]), kind='cc')
                else:
                    K.op(POOL, [part_t[l][ti]], [xs1_t[ti]], lambda ti=ti: E[POOL].collective_compute(
                        "ReduceScatter", ALU.add, replica_groups=[[0, 1], [2, 3], [4, 5], [6, 7]],
                        ins=[part_d[1][ti].opt()], outs=[xs1_d[ti].opt()]), kind='cc')


            for _ in gen_ABC(0):
                pass
            for ti in range(NT):
                gd = gen_D(ti)
                ga = gen_ABC(ti + 1) if ti + 1 < NT else None
                nD, nA = 30, 18
                dcnt = acnt = 0
                while True:
                    try:
                        next(gd)
                    except StopIteration:
                        break
                    dcnt += 1
                    while ga is not None and acnt * nD < dcnt * nA:
                        try:
                            next(ga)
                            acnt += 1
                        except StopIteration:
                            ga = None
                if ga is not None:
                    for _ in ga:
                        pass

        fg_bc = gate_bc
        dma(fg_bc, fg_bc[:], None, fg_d.ap().broadcast_to([128, D]))
        for ti in range(NT):
            for sub in range(2):
                xi = xin[sub % 2]
                dma(xi, xi[:], xs1_t[ti], xs1_d[ti][sub * 128:(sub + 1) * 128, :])
                K.op(ACT, [xi], [junk, ss], lambda xi=xi: E[ACT].activation(
                    out=junk[:], in_=xi[:], func=AF.Square, accum_out=ss[:]))
                K.op(DVE, [ss], [ss], lambda: E[DVE].tensor_scalar(
                    out=ss[:], in0=ss[:], scalar1=1.0 / D, scalar2=EPS, op0=ALU.mult, op1=ALU.add))
                K.op(POOL, [ss, mhalf], [rstd], lambda: E[POOL].tensor_tensor(out=rstd[:], in0=ss[:], in1=mhalf[:, 0:1], op=ALU.pow))
                K.op(DVE, [xi, rstd, fg_bc], [xo], lambda xi=xi: E[DVE].scalar_tensor_tensor(
                    out=xo[:], in0=xi[:], scalar=rstd[:, 0:1], in1=fg_bc[:], op0=ALU.mult, op1=ALU.mult))
                dma(out_t, out_d[ti * 256 + sub * 128: ti * 256 + (sub + 1) * 128, :], xo, xo[:])
        K.finish(SP)
    return nc


def _host_inputs(S, x, c, positions, w_ada, b_ada, norm_g, w_in, conv_w, conv_b,
                 b_igate, b_fgate, gn_m, gn_r, w_out, final_g):
    f32 = np.float32
    L = DEPTH
    shared = {}
    shared["wada"] = np.ascontiguousarray(w_ada, dtype=f32)
    shared["bada_fm"] = np.ascontiguousarray(b_ada[:, :2048].reshape(L, 16, 128).transpose(0, 2, 1), dtype=f32)
    shared["bada_g"] = np.ascontiguousarray(b_ada[:, 2048:].reshape(L, 1, D), dtype=f32)
    shared["normg_fm"] = np.ascontiguousarray(norm_g.reshape(L, 8, 128).transpose(0, 2, 1), dtype=f32)
    shared["finalg"] = np.ascontiguousarray(final_g.reshape(1, D), dtype=f32)
    shared["ident_b"] = np.eye(128, dtype=f32).astype(ml_dtypes.bfloat16)
    shared["ident_f"] = np.eye(128, dtype=f32)
    s_idx = np.arange(128)[:, None]
    l_idx = np.arange(128)[None, :]
    shared["mask01"] = (s_idx <= l_idx).astype(f32)
    bm = np.zeros((2, 2, 4), f32)
    bm[0, 0, :] = 1.0
    bm[1, 1, :] = 1.0
    shared["bmask"] = bm.reshape(2, 8)
    inv_freq = (1.0 / (10000.0 ** np.linspace(0.0, 1.0, 128, dtype=np.float32))).astype(f32)
    o = {"mq": 0, "mk": 1024, "mv": 2048, "mo": 3072, "mz": 4096, "mi": 5120, "mf": 5124,
         "rq": 5128, "rk": 6152, "rv": 7176, "rz": 8200}
    per_g = []
    for g in range(2):
        d = {}
        sl = lambda name: slice(o[name] + g * 512, o[name] + (g + 1) * 512)
        d["wf"] = np.ascontiguousarray(np.concatenate(
            [w_in[:, :, sl("mq")], w_in[:, :, sl("mk")], w_in[:, :, sl("rq")], w_in[:, :, sl("rk")]], axis=2), dtype=f32)
        d["wt"] = np.ascontiguousarray(np.concatenate(
            [w_in[:, :, sl("mv")], w_in[:, :, sl("mo")], w_in[:, :, sl("mz")], w_in[:, :, sl("rv")],
             w_in[:, :, sl("rz")]], axis=2), dtype=f32)
        d["wg"] = np.ascontiguousarray(np.concatenate(
            [w_in[:, :, o["mi"] + 2 * g:o["mi"] + 2 * g + 2], w_in[:, :, o["mf"] + 2 * g:o["mf"] + 2 * g + 2]], axis=2),
            dtype=f32)
        cwq = conv_w[:, :, g * 512:(g + 1) * 512]
        cwk = conv_w[:, :, 1024 + g * 512:1024 + (g + 1) * 512]
        cwa = np.concatenate([cwq, cwk], axis=2)
        d["cw"] = np.ascontiguousarray(cwa.reshape(L, 4, 8, 128).transpose(0, 3, 2, 1).reshape(L, 128, 32), dtype=f32)
        cba = np.concatenate([conv_b[:, g * 512:(g + 1) * 512], conv_b[:, 1024 + g * 512:1024 + (g + 1) * 512]], axis=1)
        d["cb"] = np.ascontiguousarray(cba.reshape(L, 8, 128).transpose(0, 2, 1), dtype=f32)
        d["bg"] = np.ascontiguousarray(np.stack([b_igate[:, 2 * g:2 * g + 2], b_fgate[:, 2 * g:2 * g + 2]], axis=1).reshape(L, 2, 2, 1), dtype=f32)
        gno = np.concatenate([gn_m[:, g * 512:(g + 1) * 512], gn_r[:, g * 512:(g + 1) * 512]], axis=1)
        d["gn_fm"] = np.ascontiguousarray(gno.reshape(L, 8, 128).transpose(0, 2, 1), dtype=f32)
        d["wo"] = np.ascontiguousarray(np.concatenate(
            [w_out[:, g * 512:(g + 1) * 512, :], w_out[:, 1024 + g * 512:1024 + (g + 1) * 512, :]], axis=1), dtype=f32)
        mrt = np.zeros((128, 2, 128), np.float64)
        tabs = np.zeros((128, 8), np.float64)
        for hh in range(2):
            h = 2 * g + hh
            gamma = 1.0 - 2.0 ** (-5.0 - h)
            s = np.arange(128, dtype=np.float64)
            mrt[:, hh, :] = np.where(s_idx <= l_idx, (gamma ** (-s))[:, None] / 16.0, 0.0)
            tabs[:, hh] = gamma ** s
            tabs[:, 2 + hh] = gamma ** (128.0 - s) / 16.0
            tabs[:, 4 + hh] = gamma ** 128.0
        tabs[:, 6] = inv_freq.astype(np.float64)
        d["mrt"] = mrt.reshape(128, 256).astype(f32)
        d["tabs"] = tabs.astype(f32)
        per_g.append(d)
    in_maps = []
    for core in range(8):
        b, g = core // 2, core % 2
        m = dict(shared)
        m.update(per_g[g])
        m["x"] = np.ascontiguousarray(x[b, :S], dtype=f32)
        m["pos"] = np.ascontiguousarray(positions[b, :S].reshape(1, S), dtype=np.int32)
        m["c_fm"] = np.ascontiguousarray(c[b].reshape(8, 128).T, dtype=f32)
        in_maps.append(m)
    return in_maps


_NC_CACHE = {}


def run(S, debug=False, **inputs):
    inputs = {k: np.asarray(v) for k, v in inputs.items()}
    key = (S, debug)
    if key not in _NC_CACHE:
        _NC_CACHE[key] = build(S, debug)
    nc = _NC_CACHE[key]
    in_maps = _host_inputs(S, **inputs)
    res = run_bass_kernel_spmd(nc, in_maps, core_ids=list(range(8)))
    NT = S // TT
    out = np.zeros((4, S, D), np.float32)
    for core in range(8):
        b, r = core // 2, core % 2
        o = np.asarray(res.results[core]["out"]).reshape(NT, 256, D)
        out[b].reshape(NT, 2, 256, D)[:, r] = o
    return out, res


def kernel(**inputs):
    S = np.asarray(inputs["x"]).shape[1]
    out, _ = run(S, False, **inputs)
    return out
```

```python
import contextlib
import math
import numpy as np
import ml_dtypes
import concourse.bass as bass
import concourse.mybir as mybir
from concourse.bass_utils import run_bass_kernel_spmd

F32 = mybir.dt.float32
BF16 = mybir.dt.bfloat16
I32 = mybir.dt.int32
ALU = mybir.AluOpType
AF = mybir.ActivationFunctionType
AX = mybir.AxisListType

D = 1024
DEPTH = 2
EPS = 1e-6
TT = 512
NDMA = 24
EPOCH = 30000
LN16 = math.log(16.0)
TWO_PI = 2.0 * math.pi
CW1 = 6.28125
CW2 = TWO_PI - 6.28125


def _freeze(fn):
    import types
    if getattr(fn, '__closure__', None) is None:
        return fn
    cells = []
    for c in fn.__closure__:
        try:
            v = c.cell_contents
        except ValueError:
            cells.append(c)
            continue
        if isinstance(v, types.FunctionType):
            v = _freeze(v)
        cells.append(types.CellType(v))
    return types.FunctionType(fn.__code__, fn.__globals__, fn.__name__, fn.__defaults__, tuple(cells))


class T:
    def __init__(self, t):
        self.t = t
        self.lw = {}
        self.rd = {}

    def __getitem__(self, idx):
        return self.t[idx]


class TV(T):
    def __init__(self, base, ap):
        self.t = ap
        self.lw = base.lw
        self.rd = base.rd


class Sched:
    def __init__(self, nc, es):
        self.nc = nc
        self.es = es
        self.eng = {'pe': nc.tensor, 'act': nc.scalar, 'dve': nc.vector, 'pool': nc.gpsimd, 'sp': nc.sync}
        self.cnt = {e: 0 for e in self.eng}
        self.sems = {e: [] for e in self.eng}
        self.waited = {e: {} for e in self.eng}
        self.dma_sems = [es.enter_context(nc.semaphore("dma%d" % i)) for i in range(NDMA)]
        self.dma_val = [0] * NDMA
        self.dma_rr = 0
        self.cc_sem = es.enter_context(nc.semaphore("ccs"))
        self.cc_val = 0
        self.nwait = 0
        self.rec = []

    def _sem(self, e, ep):
        while len(self.sems[e]) <= ep:
            self.sems[e].append(self.es.enter_context(self.nc.semaphore("s_%s_%d" % (e, len(self.sems[e])))))
        return self.sems[e][ep]

    def _wait(self, e, key, val):
        w = self.waited[e]
        if w.get(key, 0) >= val:
            return
        w[key] = val
        self.nwait += 1
        if key[0] == 'E':
            ep = (val - 1) // EPOCH
            self.eng[e].wait_ge(self._sem(key[1], ep), val - ep * EPOCH)
        elif key[0] == 'D':
            self.eng[e].wait_ge(self.dma_sems[key[1]], val)
        else:
            self.eng[e].wait_ge(self.cc_sem, val)

    def op(self, e, reads, writes, fn, kind='c', dur=None):
        self.rec.append((e, list(reads), list(writes), _freeze(fn), kind, dur))

    def run(self, lat=0.25):
        import heapq
        rec = self.rec
        n = len(rec)
        DEF = {'pe': 0.35, 'act': 0.5, 'dve': 0.45, 'pool': 0.9, 'sp': 0.1}
        lastw = {}
        readers = {}
        preds = [None] * n
        succ = [[] for _ in range(n)]
        lastcc = None
        for i, (e, reads, writes, fn, kind, dur) in enumerate(rec):
            p = set()
            for t in reads:
                k = id(t.lw)
                if getattr(t, 'psum', False):
                    writes = writes + [t]
                    continue
                if k in lastw:
                    p.add(lastw[k])
            for t in writes:
                k = id(t.lw)
                if k in lastw:
                    p.add(lastw[k])
                for r in readers.get(k, ()):
                    p.add(r)
            if kind == 'cc':
                if lastcc is not None:
                    p.add(lastcc)
                lastcc = i
            p.discard(i)
            preds[i] = p
            for q in p:
                succ[q].append(i)
            for t in reads:
                if getattr(t, 'psum', False):
                    continue
                readers.setdefault(id(t.lw), []).append(i)
            for t in writes:
                k = id(t.lw)
                lastw[k] = i
                readers[k] = []
        indeg = [len(p) for p in preds]
        ready_t = [0.0] * n
        fin = [0.0] * n
        efree = {e: 0.0 for e in self.eng}
        pend = {e: [] for e in self.eng}
        avail = {e: [] for e in self.eng}
        for i in range(n):
            if indeg[i] == 0:
                heapq.heappush(pend[rec[i][0]], (0.0, i))
        order = []
        done = 0
        while done < n:
            best = None
            for e in self.eng:
                pe_, av = pend[e], avail[e]
                while pe_ and pe_[0][0] <= efree[e]:
                    heapq.heappush(av, heapq.heappop(pe_)[1])
                if av:
                    cand = (efree[e], av[0], e, True)
                elif pe_:
                    cand = (pe_[0][0], pe_[0][1], e, False)
                else:
                    continue
                if best is None or cand[:2] < best[:2]:
                    best = cand
            start, i, e, from_av = best
            if from_av:
                heapq.heappop(avail[e])
            else:
                heapq.heappop(pend[e])
            kind, dur = rec[i][4], rec[i][5]
            d = dur if dur is not None else DEF[e]
            if kind == 'dma':
                efree[e] = start + 0.1
                fin[i] = start + (dur if dur is not None else 2.5)
            elif kind == 'cc':
                efree[e] = start + 0.5
                fin[i] = start + 60.0
            else:
                efree[e] = start + d
                fin[i] = start + d
            order.append(i)
            done += 1
            for q in succ[i]:
                indeg[q] -= 1
                ready_t[q] = max(ready_t[q], fin[i] + (0.0 if (rec[q][0] == e and e == 'pe') else lat))
                if indeg[q] == 0:
                    heapq.heappush(pend[rec[q][0]], (ready_t[q], q))
        self.est_total = max(fin) if n else 0.0
        for i in order:
            e, reads, writes, fn, kind, dur = rec[i]
            self._emit(e, reads, writes, fn, kind)
        self.rec = []

    def _emit(self, e, reads, writes, fn, kind='c'):
        writes = list(writes) + [t for t in reads if getattr(t, 'psum', False)]
        deps = {}
        for t in reads:
            for k, v in t.lw.items():
                deps[k] = max(deps.get(k, 0), v)
        for t in writes:
            for k, v in t.lw.items():
                deps[k] = max(deps.get(k, 0), v)
            for k, v in t.rd.items():
                deps[k] = max(deps.get(k, 0), v)
        for k, v in deps.items():
            if k == ('E', 'pe') and e == 'pe':
                continue
            self._wait(e, k, v)
        if kind == 'dma':
            i = self.dma_rr
            self.dma_rr = (i + 1) % NDMA
            if self.dma_val[i] > 0:
                self._wait(e, ('D', i), self.dma_val[i])
            inst = fn()
            self.dma_val[i] += 16
            inst.then_inc(self.dma_sems[i], 16)
            tok = (('D', i), self.dma_val[i])
        elif kind == 'cc':
            inst = fn()
            self.cc_val += 1
            inst.then_inc(self.cc_sem, 1)
            tok = (('C',), self.cc_val)
        else:
            inst = fn()
            self.cnt[e] += 1
            n = self.cnt[e]
            ep = (n - 1) // EPOCH
            inst.then_inc(self._sem(e, ep), 1)
            tok = (('E', e), n)
        for t in reads:
            t.rd[tok[0]] = max(t.rd.get(tok[0], 0), tok[1])
        for t in writes:
            t.lw.clear()
            t.lw[tok[0]] = tok[1]
            t.rd.clear()
        return tok

    def finish(self, e):
        for src in self.eng:
            if self.cnt[src] > 0 and not (src == e == 'pe'):
                self._wait(e, ('E', src), self.cnt[src])
        for i in range(NDMA):
            if self.dma_val[i] > 0:
                self._wait(e, ('D', i), self.dma_val[i])
        if self.cc_val > 0:
            self._wait(e, ('C',), self.cc_val)


def build(S=8192, debug=False):
    NT = S // TT
    nc = bass.Bass("TRN2", target_bir_lowering=False)
    es = contextlib.ExitStack()

    def din(name, shape, dt=F32):
        return nc.dram_tensor(name, list(shape), dt, kind="ExternalInput")

    x_d = din("x", [S, D])
    pos_d = din("pos", [1, S], I32)
    c_d = din("c_fm", [128, 8])
    wada_d = din("wada", [DEPTH, D, 3 * D])
    badafm_d = din("bada_fm", [DEPTH, 128, 16])
    badag_d = din("bada_g", [DEPTH, 1, D])
    normg_d = din("normg_fm", [DEPTH, 128, 8])
    wf_d = din("wf", [DEPTH, D, 2048])
    wt_d = din("wt", [DEPTH, D, 2560])
    wg_d = din("wg", [DEPTH, D, 4])
    cw_d = din("cw", [DEPTH, 128, 32])
    cb_d = din("cb", [DEPTH, 128, 8])
    bg_d = din("bg", [DEPTH, 2, 2, 1])
    gn_d = din("gn_fm", [DEPTH, 128, 8])
    wo_d = din("wo", [DEPTH, D, D])
    fg_d = din("finalg", [1, D])
    identb_d = din("ident_b", [128, 128], BF16)
    identf_d = din("ident_f", [128, 128])
    mask_d = din("mask01", [128, 128])
    mrt_d = din("mrt", [128, 256])
    tabs_d = din("tabs", [128, 8])
    bmask_d = din("bmask", [2, 8])
    out_d = nc.dram_tensor("out", [NT * 256, D], F32, kind="ExternalOutput")
    part_d = [nc.dram_tensor("part%d" % l, [NT, TT, D], F32) for l in range(DEPTH)]
    xs0_d = nc.dram_tensor("xs0", [NT, TT, D], F32)
    xs1_d = nc.dram_tensor("xs1", [NT, 256, D], F32)
    dbg = {}
    if debug:
        dbg['hT'] = nc.dram_tensor("dbg_hT", [128, 8, TT], BF16, kind="ExternalOutput")
        dbg['qT'] = nc.dram_tensor("dbg_qT", [128, 4, TT], BF16, kind="ExternalOutput")
        dbg['kT'] = nc.dram_tensor("dbg_kT", [128, 4, TT], BF16, kind="ExternalOutput")
        dbg['rqT'] = nc.dram_tensor("dbg_rqT", [128, 4, TT], BF16, kind="ExternalOutput")
        dbg['rkT'] = nc.dram_tensor("dbg_rkT", [128, 4, TT], BF16, kind="ExternalOutput")
        dbg['ytok'] = nc.dram_tensor("dbg_ytok", [128, 1024], BF16, kind="ExternalOutput")
        dbg['tg'] = nc.dram_tensor("dbg_tg", [128, 16], F32, kind="ExternalOutput")
        dbg['hbuf'] = nc.dram_tensor("dbg_hbuf", [128, 4 * 257], F32, kind="ExternalOutput")
        dbg['xo'] = nc.dram_tensor("dbg_xo", [128, 1024], F32, kind="ExternalOutput")

    with es:
        K = Sched(nc, es)

        def sb(name, shape, dt=F32):
            return T(es.enter_context(nc.sbuf_tensor("sb_" + name, list(shape), dt)))

        def dr(t):
            return T(t)

        PE, ACT, DVE, POOL, SP = 'pe', 'act', 'dve', 'pool', 'sp'
        E = K.eng

        def dma(out_t, out_ap, in_t, in_ap, q=SP):
            reads = [in_t] if in_t is not None else []
            writes = [out_t] if out_t is not None else []
            return K.op(q, reads, writes, lambda: E[q].dma_start(out=out_ap, in_=in_ap), kind='dma')

        banks = [es.enter_context(nc.psum_tensor("psb%d" % i, [128, 512], F32)) for i in range(8)]
        class TP(T):
            psum = True
        _T = T
        psA = [TP(banks[0]), TP(banks[1])]
        ps_s = TP(banks[2])
        ps_num = TP(banks[3])
        ps_u = [TP(banks[4]), TP(banks[5])]
        ps_tr = [TP(banks[6]), TP(banks[6])]
        ps_yt = TP(banks[6])
        ps_sm = TP(banks[7])
        ps_kt = ps_sm
        bank6_bf = banks[6][:].bitcast(BF16)
        kt_bf = banks[7][:, 128:256].bitcast(BF16)
        abank = [0]

        def next_ps():
            abank[0] ^= 1
            return psA[abank[0]]

        ident_b = sb("ident_b", [128, 128], BF16)
        ident_f = sb("ident_f", [128, 128])
        mask01 = sb("mask01", [128, 128])
        mrt = sb("mrt", [128, 2, 128])
        tabs = sb("tabs", [128, 8])
        bmask = sb("bmask", [2, 2, 4])
        c_fm = sb("c_fm", [128, 8])
        c_act = sb("c_act", [128, 8])
        onesK = sb("onesK", [2, 128])
        mhalf = sb("mhalf", [128, 4])
        dma(ident_b, ident_b[:], None, identb_d[:, :])
        dma(ident_f, ident_f[:], None, identf_d[:, :])
        dma(mask01, mask01[:], None, mask_d[:, :])
        dma(mrt, mrt[:], None, mrt_d.ap().rearrange("p (h l) -> p h l", h=2))
        dma(tabs, tabs[:], None, tabs_d[:, :])
        dma(bmask, bmask[:], None, bmask_d.ap().rearrange("p (h c) -> p h c", h=2))
        dma(c_fm, c_fm[:], None, c_d[:, :])
        K.op(DVE, [], [onesK], lambda: E[DVE].memset(onesK[:], 1.0))
        K.op(DVE, [], [mhalf], lambda: E[DVE].memset(mhalf[:], -0.5))
        K.op(ACT, [c_fm], [c_act], lambda: E[ACT].activation(out=c_act[:], in_=c_fm[:], func=AF.Silu))
        gl = lambda j: tabs[:, j:j + 1]
        zt = lambda j: tabs[:, 2 + j:3 + j]
        g128 = lambda j: tabs[:, 4 + j:5 + j]
        invf = tabs[:, 6:7]

        wf_bf = sb("wf_bf", [128, 8, 2048], BF16)
        wt_bf = sb("wt_bf", [128, 8, 2560], BF16)
        wg_bf = sb("wg_bf", [128, 8, 4], BF16)
        wo_bf = sb("wo_bf", [128, 8, D], BF16)
        sti = [0]
        cw = sb("cw", [128, 8, 4])
        cb = sb("cb", [128, 8])
        bgi = sb("bgi", [2, 1])
        bgf = sb("bgf", [2, 1])
        nbf = sb("nbf", [2, 1])
        gn_fm = sb("gn_fm", [128, 8])
        normg = sb("normg", [128, 8])
        bada_fm = sb("bada_fm", [128, 16])
        gsc = sb("gsc", [128, 8])
        shf = sb("shf", [128, 8])
        gate_bc = sb("gate_bc", [128, D])

        xin0 = sb("xin0", [128, D])
        xin = [xin0, xin0]
        ss = sb("ss", [128, 1])
        rstd = sb("rstd", [128, 1])
        xn = sb("xn", [128, D], BF16)
        hTs = [sb("hT%d" % i, [128, 8, TT], BF16) for i in range(2)]
        f1 = sb("f1", [128, TT])
        f2 = sb("f2", [128, TT])
        f3 = sb("f3", [128, TT])
        cosT = sb("cosT", [128, TT])
        sinT = sb("sinT", [128, TT])
        carry = sb("carry", [128, 8, 3])
        qTs = [[sb("qT%d_%d" % (j, i), [128, TT], BF16) for i in range(4)] for j in range(2)]
        kTs = [[sb("kT%d_%d" % (j, i), [128, TT], BF16) for i in range(4)] for j in range(2)]
        rqTs = [[sb("rqT%d_%d" % (j, i), [128, TT], BF16) for i in range(4)] for j in range(2)]
        rkTs = [[sb("rkT%d_%d" % (j, i), [128, TT], BF16) for i in range(4)] for j in range(2)]
        li = sb("li", [2, TT])
        ge = sb("ge", [2, TT])
        bneg = sb("bneg", [2, TT])
        bcar = sb("bcar", [2, 1])
        cm = sb("cm", [2, 4])
        gmall = sb("gmall", [2, 5])
        dd = sb("dd", [2, 4])
        dexp = sb("dexp", [2, 2, 4])
        tgs = [sb("tg%d" % i, [128, 4, 4]) for i in range(2)]
        dec_bcs = [sb("dec_bc%d" % i, [128, 2, 4]) for i in range(2)]
        Vms = [sb("Vm%d" % i, [128, 2, 257], BF16) for i in range(2)]
        Vrs = [sb("Vr%d" % i, [128, 2, 256], BF16) for i in range(2)]
        og = sb("og", [128, 512])
        rzg = sb("rzg", [128, 512])
        swTs = [sb("swT%d" % i, [128, 128], BF16) for i in range(2)]
        kws = [sb("kw%d" % i, [128, 256], BF16) for i in range(2)]
        C32 = [sb("C32_%d" % i, [128, 2, 257]) for i in range(2)]
        Cbf = [sb("Cbf_%d" % i, [128, 2, 257], BF16) for i in range(2)]
        R32 = [sb("R32_%d" % i, [128, 2, 256]) for i in range(2)]
        Rbf = [sb("Rbf_%d" % i, [128, 2, 256], BF16) for i in range(2)]
        hb = [sb("hb%d" % i, [128, 256]) for i in range(4)]
        den2 = sb("den2", [128, 2])
        st6 = sb("st6", [128, 4, 6])
        mv = sb("mv", [128, 4, 2])
        rinv = sb("rinv", [128, 4])
        v4 = sb("v4", [128, 4])
        sd4 = sb("sd4", [128, 4])
        s4 = sb("s4", [128, 4])
        nb4 = sb("nb4", [128, 4])
        ytok = sb("ytok", [128, D], BF16)
        yT = sb("yT", [128, 8, 128], BF16)
        xres = sb("xres", [128, D])
        xo = sb("xo", [128, D])
        gb_bc = xo
        junk = xn
        posi_v = f2[:, :].bitcast(I32)
        cacc = f3
        xpre = sb("xpre", [128, TT + 3])
        f4 = T(xpre.t)
        f4 = xpre
        U = li
        wl = li
        fl = bneg
        c_rep_v = xres[:, :].rearrange("p (k c) -> p k c", k=8)
        stage_ada = [TV(hTs[i], hTs[i][:, :, :].rearrange("p k t -> p (k t)").bitcast(F32)) for i in range(2)]
        stage_w = [xin0, xo, xres]
        for i in range(2):
            K.op(DVE, [], [Vms[i]], lambda i=i: E[DVE].memset(Vms[i][:], 1.0))
        K.op(DVE, [tabs], [rinv], lambda: E[DVE].tensor_copy(out=rinv[:, 2:4], in_=tabs[:, 0:2]))

        part_t = [[dr(part_d[l]) for _ in range(NT)] for l in range(DEPTH)]
        xs0_t = [dr(xs0_d) for _ in range(NT)]
        xs1_t = [dr(xs1_d) for _ in range(NT)]
        x_t = dr(x_d)
        out_t = dr(out_d)

        def dump(name, t, ap):
            if debug and name in dbg:
                dma(None, dbg[name].ap() if ap is None else ap, t, t[:])

        for l in range(DEPTH):
            dma(cw, cw[:], None, cw_d[l].rearrange("p (c k) -> p c k", c=8))
            dma(cb, cb[:], None, cb_d[l])
            dma(bgi, bgi[:], None, bg_d[l][0])
            dma(bgf, bgf[:], None, bg_d[l][1])
            dma(gn_fm, gn_fm[:], None, gn_d[l])
            dma(normg, normg[:], None, normg_d[l])
            dma(bada_fm, bada_fm[:], None, badafm_d[l])
            dma(gb_bc, gb_bc[:], None, badag_d[l].broadcast_to([128, D]))
            K.op(DVE, [c_act], [xres], lambda: E[DVE].tensor_copy(
                out=c_rep_v, in_=c_act[:].unsqueeze(2).broadcast_to([128, 8, 128])))
            K.op(DVE, [bgf], [nbf], lambda: E[DVE].tensor_scalar(
                out=nbf[:], in0=bgf[:], scalar1=-1.0, scalar2=None, op0=ALU.mult))
            for j in range(24):
                stg = stage_ada[j % 2]
                sv = stg[:, 0:1024].rearrange("p (k c) -> p k c", k=8)
                dma(stg, sv, None, wada_d[l][:, j * 128:(j + 1) * 128].rearrange("(k p) c -> p k c", p=128))
                if j < 16:
                    for ci in range(1):
                        col = j

                        def f(col=col, ci=ci, sv=sv):
                            for kt in range(8):
                                r = E[PE].matmul(banks[7][:, 64 + col:65 + col], lhsT=sv[:, kt, ci * 128:(ci + 1) * 128],
                                                 rhs=c_act[:, kt:kt + 1], start=(kt == 0), stop=(kt == 7))
                            return r
                        K.op(PE, [stg, c_act], [ps_sm], f)
                else:
                    pst = next_ps()

                    def f(pst=pst, sv=sv):
                        for kt in range(8):
                            r = E[PE].matmul(pst[:, 0:128], lhsT=c_rep_v[:, kt, :], rhs=sv[:, kt, :],
                                             start=(kt == 0), stop=(kt == 7))
                        return r
                    K.op(PE, [stg, xres], [pst], f)
                    K.op(DVE, [pst, gb_bc], [gate_bc], lambda pst=pst, j=j: E[DVE].tensor_tensor(
                        out=gate_bc[:, (j - 16) * 128:(j - 15) * 128], in0=pst[:, 0:128],
                        in1=gb_bc[:, (j - 16) * 128:(j - 15) * 128], op=ALU.add))
            K.op(DVE, [ps_sm, bada_fm], [shf], lambda: E[DVE].tensor_tensor(
                out=shf[:], in0=banks[7][:, 64:72], in1=bada_fm[:, 0:8], op=ALU.add))
            K.op(DVE, [ps_sm, bada_fm], [gsc], lambda: E[DVE].scalar_tensor_tensor(
                out=gsc[:], in0=banks[7][:, 72:80], scalar=1.0, in1=bada_fm[:, 8:16], op0=ALU.add, op1=ALU.add))
            K.op(DVE, [gsc, normg], [gsc], lambda: E[DVE].tensor_tensor(
                out=gsc[:], in0=gsc[:], in1=normg[:], op=ALU.mult))
            ci_ = 0
            for (wd, wb, ncol, scl) in ((wf_d, wf_bf, 2048, False), (wg_d, wg_bf, 4, False),
                                        (wt_d, wt_bf, 2560, False), (wo_d, wo_bf, D, True)):
                for kt in range(8):
                  for c0 in range(0, ncol, 1024):
                    c1 = min(ncol, c0 + 1024)
                    stg = stage_w[sti[0] % 3]
                    sti[0] += 1
                    dma(stg, stg[:, 0:c1 - c0], None, wd[l][kt * 128:(kt + 1) * 128, c0:c1])
                    ce = (POOL, ACT)[ci_ % 2]
                    ci_ += 1
                    if scl:
                        K.op(DVE, [stg, gn_fm], [wb], lambda stg=stg, wb=wb, kt=kt, c0=c0, c1=c1: E[DVE].tensor_scalar(
                            out=wb[:, kt, c0:c1], in0=stg[:, 0:c1 - c0], scalar1=gn_fm[:, kt:kt + 1], scalar2=None,
                            op0=ALU.mult))
                    elif ce == POOL:
                        K.op(POOL, [stg], [wb], lambda stg=stg, wb=wb, kt=kt, c0=c0, c1=c1: E[POOL].tensor_copy(
                            out=wb[:, kt, c0:c1], in_=stg[:, 0:c1 - c0]))
                    else:
                        K.op(ACT, [stg], [wb], lambda stg=stg, wb=wb, kt=kt, c0=c0, c1=c1: E[ACT].activation(
                            out=wb[:, kt, c0:c1], in_=stg[:, 0:c1 - c0], func=AF.Copy))
            for i in range(2):
                K.op(POOL, [], [C32[i]], lambda i=i: E[POOL].memset(C32[i][:], 0.0))
                K.op(POOL, [], [Cbf[i]], lambda i=i: E[POOL].memset(Cbf[i][:], 0.0))
                K.op(POOL, [], [R32[i]], lambda i=i: E[POOL].memset(R32[i][:], 0.0))
                K.op(POOL, [], [Rbf[i]], lambda i=i: E[POOL].memset(Rbf[i][:], 0.0))
            K.op(POOL, [], [carry], lambda: E[POOL].memset(carry[:], 0.0))
            K.op(POOL, [], [bcar], lambda: E[POOL].memset(bcar[:], 0.0))
            K.op(POOL, [], [gmall], lambda: E[POOL].memset(gmall[:], 0.0))

            def gen_ABC(ti):
                par = ti % 2
                hT, qT, kT, rqT, rkT, tg, dec_bc = hTs[par], qTs[par], kTs[par], rqTs[par], rkTs[par], tgs[par], dec_bcs[par]
                for sub in range(4):
                    xi = xin[sub % 2]
                    if l == 0:
                        dma(xi, xi[:], x_t, x_d[ti * TT + sub * 128: ti * TT + (sub + 1) * 128, :])
                    else:
                        dma(xi, xi[:], xs0_t[ti], xs0_d[ti][sub * 128:(sub + 1) * 128, :])
                    K.op(ACT, [xi], [junk, ss], lambda xi=xi: E[ACT].activation(
                        out=junk[:], in_=xi[:], func=AF.Square, accum_out=ss[:]), dur=1.0)
                    K.op(DVE, [ss], [ss], lambda: E[DVE].tensor_scalar(
                        out=ss[:], in0=ss[:], scalar1=1.0 / D, scalar2=EPS, op0=ALU.mult, op1=ALU.add))
                    K.op(POOL, [ss, mhalf], [rstd], lambda: E[POOL].tensor_tensor(out=rstd[:], in0=ss[:], in1=mhalf[:, 0:1], op=ALU.pow))
                    K.op(DVE, [xi, rstd], [xn], lambda xi=xi, sub=sub: E[DVE].tensor_scalar(
                        out=xn[:], in0=xi[:], scalar1=rstd[:, 0:1], scalar2=None, op0=ALU.mult), dur=1.0)

                    def f():
                        for kt in range(8):
                            r = E[PE].transpose(out=bank6_bf[:, kt * 128:(kt + 1) * 128],
                                                in_=xn[:, kt * 128:(kt + 1) * 128], identity=ident_b[:])
                        return r
                    K.op(PE, [xn, ident_b], [ps_tr[0], ps_tr[1]], f, dur=1.0)
                    for kt in range(8):
                        K.op(ACT, [ps_tr[0], ps_tr[1], gsc, shf], [hT], lambda kt=kt, sub=sub: E[ACT].activation(
                            out=hT[:, kt, sub * 128:(sub + 1) * 128], in_=bank6_bf[:, kt * 128:(kt + 1) * 128],
                            func=AF.Identity, scale=gsc[:, kt:kt + 1], bias=shf[:, kt:kt + 1]))
                    yield
                if l == 0 and ti == 0:
                    dump('hT', hT, None)
                dma(f2, posi_v, None, pos_d[0:1, ti * TT:(ti + 1) * TT].broadcast_to([128, TT]))
                K.op(DVE, [f2], [f1], lambda: E[DVE].tensor_copy(out=f1[:], in_=posi_v))
                K.op(DVE, [f1, tabs], [f1], lambda: E[DVE].tensor_scalar(
                    out=f1[:], in0=f1[:], scalar1=invf, scalar2=None, op0=ALU.mult))
                K.op(DVE, [f1], [f2], lambda: E[DVE].tensor_scalar(
                    out=posi_v, in0=f1[:], scalar1=1.0 / TWO_PI, scalar2=None, op0=ALU.mult))
                K.op(DVE, [f2], [f3], lambda: E[DVE].tensor_copy(out=f3[:], in_=posi_v))
                K.op(DVE, [f3, f1], [f1], lambda: E[DVE].scalar_tensor_tensor(
                    out=f1[:], in0=f3[:], scalar=-CW1, in1=f1[:], op0=ALU.mult, op1=ALU.add))
                K.op(DVE, [f3, f1], [f1], lambda: E[DVE].scalar_tensor_tensor(
                    out=f1[:], in0=f3[:], scalar=-CW2, in1=f1[:], op0=ALU.mult, op1=ALU.add))
                K.op(DVE, [f1], [f3], lambda: E[DVE].tensor_scalar(
                    out=f3[:], in0=f1[:], scalar1=math.pi, scalar2=-TWO_PI, op0=ALU.is_gt, op1=ALU.mult))
                K.op(DVE, [f1, f3], [f3], lambda: E[DVE].tensor_tensor(out=f3[:], in0=f3[:], in1=f1[:], op=ALU.add))
                K.op(DVE, [f3], [f3], lambda: E[DVE].tensor_scalar(
                    out=f3[:], in0=f3[:], scalar1=-math.pi, scalar2=math.pi, op0=ALU.max, op1=ALU.min))
                K.op(ACT, [f3], [sinT], lambda: E[ACT].activation(out=sinT[:], in_=f3[:], func=AF.Sin))
                K.op(DVE, [f1], [f2], lambda: E[DVE].tensor_scalar(
                    out=f2[:], in0=f1[:], scalar1=math.pi / 2, scalar2=-TWO_PI, op0=ALU.is_gt, op1=ALU.mult))
                K.op(DVE, [f1, f2], [f2], lambda: E[DVE].scalar_tensor_tensor(
                    out=f2[:], in0=f1[:], scalar=math.pi / 2, in1=f2[:], op0=ALU.add, op1=ALU.add))
                K.op(DVE, [f2], [f2], lambda: E[DVE].tensor_scalar(
                    out=f2[:], in0=f2[:], scalar1=-math.pi, scalar2=math.pi, op0=ALU.max, op1=ALU.min))
                K.op(ACT, [f2], [cosT], lambda: E[ACT].activation(out=cosT[:], in_=f2[:], func=AF.Sin))
                yield

                def fm_mm(pst, ct):
                    def f():
                        for kt in range(8):
                            r = E[PE].matmul(pst[:, :], lhsT=wf_bf[:, kt, ct * 128:(ct + 1) * 128], rhs=hT[:, kt, :],
                                             start=(kt == 0), stop=(kt == 7))
                        return r
                    K.op(PE, [wf_bf, hT], [pst], f, dur=1.9)

                for ct in range(8):
                    pst = next_ps()
                    fm_mm(pst, ct)
                    dst = qT[ct] if ct < 4 else kT[ct - 4]
                    K.op(POOL, [carry], [xpre], lambda ct=ct: E[POOL].tensor_copy(out=xpre[:, 0:3], in_=carry[:, ct, :]))
                    K.op(ACT, [pst], [xpre], lambda pst=pst: E[ACT].activation(
                        out=xpre[:, 3:TT + 3], in_=pst[:, :], func=AF.Copy))
                    K.op(POOL, [xpre], [carry], lambda ct=ct: E[POOL].tensor_copy(
                        out=carry[:, ct, :], in_=xpre[:, TT:TT + 3]))
                    K.op(DVE, [xpre, cw, cb], [cacc], lambda ct=ct: E[DVE].tensor_scalar(
                        out=cacc[:], in0=xpre[:, 3:TT + 3], scalar1=cw[:, ct, 3:4], scalar2=cb[:, ct:ct + 1],
                        op0=ALU.mult, op1=ALU.add))
                    for tap in (2, 1, 0):
                        K.op(DVE, [xpre, cw, cacc], [cacc], lambda ct=ct, tap=tap: E[DVE].scalar_tensor_tensor(
                            out=cacc[:], in0=xpre[:, tap:tap + TT], scalar=cw[:, ct, tap:tap + 1], in1=cacc[:],
                            op0=ALU.mult, op1=ALU.add))
                    K.op(ACT, [cacc], [dst], lambda dst=dst: E[ACT].activation(out=dst[:], in_=cacc[:], func=AF.Silu))
                    yield
                for p in range(4):
                    p1 = next_ps()
                    fm_mm(p1, 8 + 2 * p)
                    p2 = next_ps()
                    fm_mm(p2, 9 + 2 * p)
                    d1 = (rqT if p < 2 else rkT)[2 * (p % 2)]
                    d2 = (rqT if p < 2 else rkT)[2 * (p % 2) + 1]
                    K.op(DVE, [p1, cosT], [f3], lambda p1=p1: E[DVE].tensor_tensor(out=f3[:], in0=p1[:, :], in1=cosT[:], op=ALU.mult))
                    K.op(DVE, [p1, sinT], [f2], lambda p1=p1: E[DVE].tensor_tensor(out=f2[:], in0=p1[:, :], in1=sinT[:], op=ALU.mult))
                    K.op(DVE, [p2, sinT], [f4], lambda p2=p2: E[DVE].tensor_tensor(out=f4[:, 0:TT], in0=p2[:, :], in1=sinT[:], op=ALU.mult))
                    K.op(DVE, [p2, cosT], [f1], lambda p2=p2: E[DVE].tensor_tensor(out=f1[:], in0=p2[:, :], in1=cosT[:], op=ALU.mult))
                    K.op(POOL, [f3, f4], [d1], lambda d1=d1: E[POOL].tensor_tensor(out=d1[:], in0=f3[:], in1=f4[:, 0:TT], op=ALU.subtract), dur=1.4)
                    K.op(POOL, [f1, f2], [d2], lambda d2=d2: E[POOL].tensor_tensor(out=d2[:], in0=f1[:], in1=f2[:], op=ALU.add), dur=1.4)
                    yield
                pgi = next_ps()

                def f(pgi=pgi):
                    for kt in range(8):
                        r = E[PE].matmul(pgi[0:2, :], lhsT=wg_bf[:, kt, 0:2], rhs=hT[:, kt, :], start=(kt == 0), stop=(kt == 7))
                    return r
                K.op(PE, [wg_bf, hT], [pgi], f, dur=1.8)
                K.op(ACT, [pgi, bgi], [li], lambda pgi=pgi: E[ACT].activation(
                    out=li[:], in_=pgi[0:2, :], func=AF.Identity, bias=bgi[:, 0:1], scale=1.0))
                pgf = next_ps()

                def f(pgf=pgf):
                    for kt in range(8):
                        r = E[PE].matmul(pgf[0:2, :], lhsT=wg_bf[:, kt, 2:4], rhs=hT[:, kt, :], start=(kt == 0), stop=(kt == 7))
                    return r
                K.op(PE, [wg_bf, hT], [pgf], f, dur=1.8)
                K.op(ACT, [pgf, nbf], [ge], lambda pgf=pgf: E[ACT].activation(
                    out=ge[:], in_=pgf[0:2, :], func=AF.Exp, bias=nbf[:, 0:1], scale=-1.0))
                K.op(ACT, [ge], [ge], lambda: E[ACT].activation(out=ge[:], in_=ge[:], func=AF.Ln, bias=1.0, scale=1.0))
                K.op(DVE, [ge, bcar], [bneg], lambda: E[DVE].tensor_tensor_scan(
                    out=bneg[:], data0=ge[:], data1=ge[:], initial=bcar[:, 0:1], op0=ALU.add, op1=ALU.max))
                K.op(DVE, [bneg], [bcar], lambda: E[DVE].tensor_copy(out=bcar[:], in_=bneg[:, TT - 1:TT]))
                K.op(DVE, [li, bneg], [U], lambda: E[DVE].tensor_tensor(out=U[:], in0=li[:], in1=bneg[:], op=ALU.add))
                K.op(DVE, [U], [cm], lambda: E[DVE].tensor_reduce(
                    out=cm[:], in_=U[:].rearrange("p (c t) -> p c t", c=4), axis=AX.X, op=ALU.max))
                K.op(DVE, [gmall], [gmall], lambda: E[DVE].tensor_copy(out=gmall[:, 0:1], in_=gmall[:, 4:5]))
                K.op(DVE, [cm, gmall], [gmall], lambda: E[DVE].tensor_tensor_scan(
                    out=gmall[:, 1:5], data0=cm[:], data1=cm[:], initial=gmall[:, 0:1], op0=ALU.max, op1=ALU.max))
                K.op(DVE, [gmall], [dd], lambda: E[DVE].tensor_tensor(
                    out=dd[:], in0=gmall[:, 0:4], in1=gmall[:, 1:5], op=ALU.subtract))
                K.op(ACT, [dd], [dd], lambda: E[ACT].activation(out=dd[:], in_=dd[:], func=AF.Exp))
                K.op(DVE, [dd, bmask], [dexp], lambda: E[DVE].tensor_tensor(
                    out=dexp[:], in0=dd[:].unsqueeze(1).broadcast_to([2, 2, 4]), in1=bmask[:], op=ALU.mult))
                K.op(PE, [onesK, dexp], [ps_sm], lambda: E[PE].matmul(
                    banks[7][:, 0:8], lhsT=onesK[:], rhs=dexp[:].rearrange("p h c -> p (h c)"), start=True, stop=True))
                K.op(DVE, [ps_sm], [dec_bc], lambda: E[DVE].tensor_copy(
                    out=dec_bc[:].rearrange("p h c -> p (h c)"), in_=banks[7][:, 0:8]))
                gmb = gmall[:, 1:5].unsqueeze(2).broadcast_to([2, 4, 128])
                K.op(DVE, [U, gmall], [wl], lambda: E[DVE].tensor_tensor(
                    out=wl[:].rearrange("p (c t) -> p c t", c=4), in0=U[:].rearrange("p (c t) -> p c t", c=4),
                    in1=gmb, op=ALU.subtract))
                K.op(ACT, [wl], [wl], lambda: E[ACT].activation(out=wl[:], in_=wl[:], func=AF.Exp, bias=-LN16, scale=1.0))
                K.op(DVE, [bneg, gmall], [fl], lambda: E[DVE].tensor_tensor(
                    out=fl[:].rearrange("p (c t) -> p c t", c=4), in0=bneg[:].rearrange("p (c t) -> p c t", c=4),
                    in1=gmb, op=ALU.subtract))
                K.op(ACT, [fl], [fl], lambda: E[ACT].activation(out=fl[:], in_=fl[:], func=AF.Exp))

                def f():
                    for c in range(4):
                        E[PE].transpose(out=banks[7][:, 16 + c * 4:18 + c * 4], in_=wl[:, c * 128:(c + 1) * 128],
                                        identity=ident_f[0:2, 0:2])
                        r = E[PE].transpose(out=banks[7][:, 18 + c * 4:20 + c * 4], in_=fl[:, c * 128:(c + 1) * 128],
                                            identity=ident_f[0:2, 0:2])
                    return r
                K.op(PE, [wl, fl, ident_f], [ps_sm], f)
                K.op(DVE, [ps_sm], [tg], lambda: E[DVE].tensor_copy(
                    out=tg[:].rearrange("p c k -> p (c k)"), in_=banks[7][:, 16:32]))
                yield
                if l == 0 and ti == 0:
                    dump('tg', tg, None)
                    for i in range(4):
                        dump('qT', qT[i], dbg['qT'][:, i, :] if debug else None)
                        dump('kT', kT[i], dbg['kT'][:, i, :] if debug else None)
                        dump('rqT', rqT[i], dbg['rqT'][:, i, :] if debug else None)
                        dump('rkT', rkT[i], dbg['rkT'][:, i, :] if debug else None)

            def gen_D(ti):
                par = ti % 2
                hT, qT, kT, rqT, rkT, tg, dec_bc = hTs[par], qTs[par], kTs[par], rqTs[par], rkTs[par], tgs[par], dec_bcs[par]

                def tm_unit(c, gi):
                    cs = slice(c * 128, (c + 1) * 128)
                    Vm, Vr = Vms[c % 2], Vrs[c % 2]
                    pst = next_ps()

                    def f():
                        for kt in range(8):
                            r = E[PE].matmul(pst[:, :], lhsT=hT[:, kt, cs], rhs=wt_bf[:, kt, gi * 512:(gi + 1) * 512],
                                             start=(kt == 0), stop=(kt == 7))
                        return r
                    K.op(PE, [wt_bf, hT], [pst], f, dur=1.9)
                    if gi == 0:
                        K.op(ACT, [pst], [Vm], lambda: E[ACT].activation(
                            out=Vm[:, :, 0:256], in_=pst[:, :].rearrange("p (h d) -> p h d", h=2), func=AF.Copy))
                    elif gi == 1:
                        K.op(ACT, [pst], [og], lambda: E[ACT].activation(out=og[:], in_=pst[:, :], func=AF.Sigmoid))
                    elif gi == 2:
                        K.op(ACT, [pst], [pst], lambda: E[ACT].activation(out=pst[:, :], in_=pst[:, :], func=AF.Silu))
                        K.op(DVE, [og, pst], [og], lambda: E[DVE].tensor_tensor(out=og[:], in0=og[:], in1=pst[:, :], op=ALU.mult))
                    elif gi == 3:
                        K.op(ACT, [pst], [Vr], lambda: E[ACT].activation(
                            out=Vr[:].rearrange("p h d -> p (h d)"), in_=pst[:, :], func=AF.Copy))
                    else:
                        K.op(ACT, [pst], [rzg], lambda: E[ACT].activation(out=rzg[:], in_=pst[:, :], func=AF.Silu))

                tm_unit(0, 0)
                tm_unit(0, 3)
                yield
                for c in range(4):
                    cs = slice(c * 128, (c + 1) * 128)
                    Vm, Vr = Vms[c % 2], Vrs[c % 2]
                    fill = [1, 2, 4]
                    for hh in range(2):
                        wcol = tg[:, c, hh:hh + 1]
                        dcol = dec_bc[:, hh, c:c + 1]
                        swm, swr, kwm, kwr = swTs[0], swTs[1], kws[0], kws[1]
                        K.op(ACT, [C32[hh], dec_bc], [Cbf[hh]], lambda hh=hh, dcol=dcol: E[ACT].activation(
                            out=Cbf[hh][:].rearrange("p a b -> p (a b)"), in_=C32[hh][:].rearrange("p a b -> p (a b)"),
                            func=AF.Identity, scale=dcol, bias=0.0))
                        def f(hh=hh):
                            for dt in range(2):
                                r = E[PE].matmul(banks[2][:, 0:128], lhsT=kT[2 * hh + dt][:, cs], rhs=qT[2 * hh + dt][:, cs],
                                                 start=(dt == 0), stop=(dt == 1))
                            return r
                        K.op(PE, [kT[2 * hh], kT[2 * hh + 1], qT[2 * hh], qT[2 * hh + 1]], [ps_s], f)
                        K.op(DVE, [ps_s, tg, mask01], [swm], lambda wcol=wcol: E[DVE].scalar_tensor_tensor(
                            out=swm[:], in0=banks[2][:, 0:128], scalar=wcol, in1=mask01[:], op0=ALU.mult, op1=ALU.mult))
                        def f(hh=hh):
                            for dt in range(2):
                                r = E[PE].transpose(out=kt_bf[:, dt * 128:(dt + 1) * 128], in_=kT[2 * hh + dt][:, cs],
                                                    identity=ident_b[:])
                            return r
                        K.op(PE, [kT[2 * hh], kT[2 * hh + 1], ident_b], [ps_kt], f)
                        K.op(ACT, [ps_kt, tg], [kwm], lambda wcol=wcol: E[ACT].activation(
                            out=kwm[:], in_=kt_bf, func=AF.Identity, scale=wcol, bias=0.0))
                        def f(hh=hh):
                            for dt in range(2):
                                r = E[PE].matmul(banks[2][:, 0:128], lhsT=rkT[2 * hh + dt][:, cs], rhs=rqT[2 * hh + dt][:, cs],
                                                 start=(dt == 0), stop=(dt == 1))
                            return r
                        K.op(PE, [rkT[2 * hh], rkT[2 * hh + 1], rqT[2 * hh], rqT[2 * hh + 1]], [ps_s], f)
                        K.op(DVE, [ps_s, mrt], [swr], lambda hh=hh: E[DVE].tensor_tensor(
                            out=swr[:], in0=banks[2][:, 0:128], in1=mrt[:, hh, :], op=ALU.mult))
                        def f(hh=hh):
                            for dt in range(2):
                                r = E[PE].transpose(out=kt_bf[:, dt * 128:(dt + 1) * 128], in_=rkT[2 * hh + dt][:, cs],
                                                    identity=ident_b[:])
                            return r
                        K.op(PE, [rkT[2 * hh], rkT[2 * hh + 1], ident_b], [ps_kt], f)
                        K.op(ACT, [ps_kt, tabs], [kwr], lambda hh=hh: E[ACT].activation(
                            out=kwr[:], in_=kt_bf, func=AF.Identity, scale=zt(hh), bias=0.0))
                        tm_unit(c, fill.pop(0))
                        def f(hh=hh):
                            E[PE].matmul(banks[3][:, 0:257], lhsT=swm[:], rhs=Vm[:, hh, :], start=True, stop=False)
                            E[PE].matmul(banks[3][:, 0:257], lhsT=qT[2 * hh][:, cs], rhs=Cbf[hh][:, 0, :], start=False, stop=False)
                            return E[PE].matmul(banks[3][:, 0:257], lhsT=qT[2 * hh + 1][:, cs], rhs=Cbf[hh][:, 1, :], start=False, stop=True)
                        K.op(PE, [swm, Vm, qT[2 * hh], qT[2 * hh + 1], Cbf[hh]], [ps_num], f, dur=0.5)
                        K.op(ACT, [ps_num], [hb[hh]], lambda hh=hh: E[ACT].activation(
                            out=hb[hh][:], in_=banks[3][:, 0:256], func=AF.Copy))
                        K.op(ACT, [ps_num], [den2], lambda hh=hh: E[ACT].activation(
                            out=den2[:, hh:hh + 1], in_=banks[3][:, 256:257], func=AF.Copy))
                        for dt in range(2):
                            K.op(PE, [kwm, Vm], [ps_u[dt]], lambda hh=hh, dt=dt: E[PE].matmul(
                                banks[4 + dt][:, 0:257], lhsT=kwm[:, dt * 128:(dt + 1) * 128], rhs=Vm[:, hh, :],
                                start=True, stop=True))
                            K.op(DVE, [C32[hh], dec_bc, ps_u[dt]], [C32[hh]], lambda hh=hh, dt=dt, dcol=dcol: E[DVE].scalar_tensor_tensor(
                                out=C32[hh][:, dt, :], in0=C32[hh][:, dt, :], scalar=dcol, in1=banks[4 + dt][:, 0:257],
                                op0=ALU.mult, op1=ALU.add), dur=0.75)
                        def f(hh=hh):
                            E[PE].matmul(banks[3][:, 0:256], lhsT=swr[:], rhs=Vr[:, hh, :], start=True, stop=False)
                            E[PE].matmul(banks[3][:, 0:256], lhsT=rqT[2 * hh][:, cs], rhs=Rbf[hh][:, 0, :], start=False, stop=False)
                            return E[PE].matmul(banks[3][:, 0:256], lhsT=rqT[2 * hh + 1][:, cs], rhs=Rbf[hh][:, 1, :],
                                                start=False, stop=True)
                        K.op(PE, [swr, Vr, rqT[2 * hh], rqT[2 * hh + 1], Rbf[hh]], [ps_num], f, dur=0.5)
                        K.op(ACT, [ps_num], [hb[2 + hh]], lambda hh=hh: E[ACT].activation(
                            out=hb[2 + hh][:], in_=banks[3][:, 0:256], func=AF.Copy))
                        if fill and hh == 1:
                            tm_unit(c, fill.pop(0))
                        for dt in range(2):
                            K.op(PE, [kwr, Vr], [ps_u[dt]], lambda hh=hh, dt=dt: E[PE].matmul(
                                banks[4 + dt][:, 0:256], lhsT=kwr[:, dt * 128:(dt + 1) * 128], rhs=Vr[:, hh, :],
                                start=True, stop=True))
                            K.op(DVE, [R32[hh], tabs, ps_u[dt]], [R32[hh]], lambda hh=hh, dt=dt: E[DVE].scalar_tensor_tensor(
                                out=R32[hh][:, dt, :], in0=R32[hh][:, dt, :], scalar=g128(hh), in1=banks[4 + dt][:, 0:256],
                                op0=ALU.mult, op1=ALU.add))
                        K.op(POOL, [R32[hh]], [Rbf[hh]], lambda hh=hh: E[POOL].tensor_copy(out=Rbf[hh][:], in_=R32[hh][:]), dur=1.9)
                        for j in (hh, 2 + hh):
                            K.op(DVE, [hb[j]], [st6], lambda j=j: E[DVE].bn_stats(out=st6[:, j, :], in_=hb[j][:]))
                            K.op(DVE, [st6], [mv], lambda j=j: E[DVE].bn_aggr(out=mv[:, j, :], in_=st6[:, j, :]))
                        yield
                        yield

                    if c < 3:
                        tm_unit(c + 1, 0)
                        tm_unit(c + 1, 3)
                    yield
                    yield
                    K.op(DVE, [den2], [v4], lambda: E[DVE].tensor_scalar(
                        out=v4[:, 0:2], in0=den2[:], scalar1=-1.0, scalar2=None, op0=ALU.mult))
                    K.op(DVE, [den2, v4], [v4], lambda: E[DVE].tensor_tensor(
                        out=v4[:, 0:2], in0=v4[:, 0:2], in1=den2[:], op=ALU.max))
                    K.op(DVE, [v4, tg], [v4], lambda c=c: E[DVE].tensor_tensor(
                        out=v4[:, 0:2], in0=v4[:, 0:2], in1=tg[:, c, 2:4], op=ALU.max))
                    K.op(DVE, [v4], [rinv], lambda: E[DVE].reciprocal(out=rinv[:, 0:2], in_=v4[:, 0:2]))
                    K.op(DVE, [rinv], [v4], lambda: E[DVE].tensor_tensor(out=v4[:], in0=rinv[:], in1=rinv[:], op=ALU.mult))
                    K.op(DVE, [v4, mv], [v4], lambda: E[DVE].tensor_tensor(out=v4[:], in0=v4[:], in1=mv[:, :, 1], op=ALU.mult))
                    K.op(DVE, [v4], [v4], lambda: E[DVE].tensor_scalar(out=v4[:], in0=v4[:], scalar1=EPS, scalar2=None, op0=ALU.add))
                    K.op(POOL, [v4, mhalf], [s4], lambda: E[POOL].tensor_tensor(out=s4[:], in0=v4[:], in1=mhalf[:], op=ALU.pow))
                    K.op(DVE, [s4, rinv], [s4], lambda: E[DVE].tensor_tensor(out=s4[:], in0=s4[:], in1=rinv[:], op=ALU.mult))
                    K.op(DVE, [mv, s4], [nb4], lambda: E[DVE].scalar_tensor_tensor(
                        out=nb4[:], in0=mv[:, :, 0], scalar=-1.0, in1=s4[:], op0=ALU.mult, op1=ALU.mult))
                    for j in range(4):
                        K.op(ACT, [hb[j], s4, nb4], [hb[j]], lambda j=j: E[ACT].activation(
                            out=hb[j][:], in_=hb[j][:], func=AF.Identity, scale=s4[:, j:j + 1], bias=nb4[:, j:j + 1]))
                    for j in range(4):
                        gsrc = og if j < 2 else rzg
                        K.op(DVE, [hb[j], gsrc], [ytok], lambda j=j, gsrc=gsrc: E[DVE].tensor_tensor(
                            out=ytok[:, j * 256:(j + 1) * 256], in0=hb[j][:], in1=gsrc[:, (j % 2) * 256:(j % 2 + 1) * 256],
                            op=ALU.mult))

                    def f():
                        for ft in range(8):
                            r = E[PE].transpose(out=bank6_bf[:, ft * 128:(ft + 1) * 128], in_=ytok[:, ft * 128:(ft + 1) * 128],
                                                identity=ident_b[:])
                        return r
                    K.op(PE, [ytok, ident_b], [ps_tr[0], ps_tr[1]], f, dur=1.0)
                    K.op(ACT, [ps_tr[0], ps_tr[1]], [yT], lambda: E[ACT].activation(
                        out=yT[:].rearrange("p f t -> p (f t)"), in_=bank6_bf, func=AF.Copy), dur=1.1)
                    if l == 0:
                        dma(xres, xres[:], x_t, x_d[ti * TT + c * 128: ti * TT + (c + 1) * 128, :])
                    else:
                        dma(xres, xres[:], xs0_t[ti], xs0_d[ti][c * 128:(c + 1) * 128, :])
                    for half in range(2):
                        pst = next_ps()

                        def f(pst=pst, half=half):
                            for ft in range(8):
                                r = E[PE].matmul(pst[:, :], lhsT=yT[:, ft, :], rhs=wo_bf[:, ft, half * 512:(half + 1) * 512],
                                                 start=(ft == 0), stop=(ft == 7))
                            return r
                        K.op(PE, [yT, wo_bf], [pst], f, dur=1.9)
                        hs = slice(half * 512, (half + 1) * 512)
                        K.op(DVE, [pst, gate_bc], [xo], lambda pst=pst, hs=hs: E[DVE].tensor_tensor(
                            out=xo[:, hs], in0=pst[:, :], in1=gate_bc[:, hs], op=ALU.mult))
                        K.op(DVE, [xres, xo], [xo], lambda hs=hs: E[DVE].scalar_tensor_tensor(
                            out=xo[:, hs], in0=xres[:, hs], scalar=0.5, in1=xo[:, hs], op0=ALU.mult, op1=ALU.add))
                    dma(part_t[l][ti], part_d[l][ti][c * 128:(c + 1) * 128, :], xo, xo[:])
                    yield
                if l == 0:
                    K.op(POOL, [part_t[l][ti]], [xs0_t[ti]], lambda ti=ti: E[POOL].collective_compute(
                        "AllReduce", ALU.add, replica_groups=[[0, 1], [2, 3], [4, 5], [6, 7]],
                        ins=[part_d[0][ti].opt()], outs=[xs0_d[ti].opt()]), kind='cc')
                else:
                    K.op(POOL, [part_t[l][ti]], [xs1_t[ti]], lambda ti=ti: E[POOL].collective_compute(
                        "ReduceScatter", ALU.add, replica_groups=[[0, 1], [2, 3], [4, 5], [6, 7]],
                        ins=[part_d[1][ti].opt()], outs=[xs1_d[ti].opt()]), kind='cc')


            for _ in gen_ABC(0):
                pass
            for ti in range(NT):
                gd = gen_D(ti)
                ga = gen_ABC(ti + 1) if ti + 1 < NT else None
                nD, nA = 30, 18
                dcnt = acnt = 0
                while True:
                    try:
                        next(gd)
                    except StopIteration:
                        break
                    dcnt += 1
                    while ga is not None and acnt * nD < dcnt * nA:
                        try:
                            next(ga)
                            acnt += 1
                        except StopIteration:
                            ga = None
                if ga is not None:
                    for _ in ga:
                        pass

        fg_bc = gate_bc
        dma(fg_bc, fg_bc[:], None, fg_d.ap().broadcast_to([128, D]))
        for ti in range(NT):
            for sub in range(2):
                xi = xin[sub % 2]
                dma(xi, xi[:], xs1_t[ti], xs1_d[ti][sub * 128:(sub + 1) * 128, :])
                K.op(ACT, [xi], [junk, ss], lambda xi=xi: E[ACT].activation(
                    out=junk[:], in_=xi[:], func=AF.Square, accum_out=ss[:]), dur=1.0)
                K.op(DVE, [ss], [ss], lambda: E[DVE].tensor_scalar(
                    out=ss[:], in0=ss[:], scalar1=1.0 / D, scalar2=EPS, op0=ALU.mult, op1=ALU.add))
                K.op(POOL, [ss, mhalf], [rstd], lambda: E[POOL].tensor_tensor(out=rstd[:], in0=ss[:], in1=mhalf[:, 0:1], op=ALU.pow))
                K.op(DVE, [xi, rstd, fg_bc], [xo], lambda xi=xi: E[DVE].scalar_tensor_tensor(
                    out=xo[:], in0=xi[:], scalar=rstd[:, 0:1], in1=fg_bc[:], op0=ALU.mult, op1=ALU.mult))
                dma(out_t, out_d[ti * 256 + sub * 128: ti * 256 + (sub + 1) * 128, :], xo, xo[:])
        K.run()
        K.finish(SP)
    return nc


def _host_inputs(S, x, c, positions, w_ada, b_ada, norm_g, w_in, conv_w, conv_b,
                 b_igate, b_fgate, gn_m, gn_r, w_out, final_g):
    f32 = np.float32
    L = DEPTH
    shared = {}
    shared["wada"] = np.ascontiguousarray(w_ada, dtype=f32)
    shared["bada_fm"] = np.ascontiguousarray(b_ada[:, :2048].reshape(L, 16, 128).transpose(0, 2, 1), dtype=f32)
    shared["bada_g"] = np.ascontiguousarray(b_ada[:, 2048:].reshape(L, 1, D), dtype=f32)
    shared["normg_fm"] = np.ascontiguousarray(norm_g.reshape(L, 8, 128).transpose(0, 2, 1), dtype=f32)
    shared["finalg"] = np.ascontiguousarray(final_g.reshape(1, D), dtype=f32)
    shared["ident_b"] = np.eye(128, dtype=f32).astype(ml_dtypes.bfloat16)
    shared["ident_f"] = np.eye(128, dtype=f32)
    s_idx = np.arange(128)[:, None]
    l_idx = np.arange(128)[None, :]
    shared["mask01"] = (s_idx <= l_idx).astype(f32)
    bm = np.zeros((2, 2, 4), f32)
    bm[0, 0, :] = 1.0
    bm[1, 1, :] = 1.0
    shared["bmask"] = bm.reshape(2, 8)
    inv_freq = (1.0 / (10000.0 ** np.linspace(0.0, 1.0, 128, dtype=np.float32))).astype(f32)
    o = {"mq": 0, "mk": 1024, "mv": 2048, "mo": 3072, "mz": 4096, "mi": 5120, "mf": 5124,
         "rq": 5128, "rk": 6152, "rv": 7176, "rz": 8200}
    per_g = []
    for g in range(2):
        d = {}
        sl = lambda name: slice(o[name] + g * 512, o[name] + (g + 1) * 512)
        d["wf"] = np.ascontiguousarray(np.concatenate(
            [w_in[:, :, sl("mq")], w_in[:, :, sl("mk")], w_in[:, :, sl("rq")], w_in[:, :, sl("rk")]], axis=2), dtype=f32)
        d["wt"] = np.ascontiguousarray(np.concatenate(
            [w_in[:, :, sl("mv")], w_in[:, :, sl("mo")], w_in[:, :, sl("mz")], w_in[:, :, sl("rv")],
             w_in[:, :, sl("rz")]], axis=2), dtype=f32)
        d["wg"] = np.ascontiguousarray(np.concatenate(
            [w_in[:, :, o["mi"] + 2 * g:o["mi"] + 2 * g + 2], w_in[:, :, o["mf"] + 2 * g:o["mf"] + 2 * g + 2]], axis=2),
            dtype=f32)
        cwq = conv_w[:, :, g * 512:(g + 1) * 512]
        cwk = conv_w[:, :, 1024 + g * 512:1024 + (g + 1) * 512]
        cwa = np.concatenate([cwq, cwk], axis=2)
        d["cw"] = np.ascontiguousarray(cwa.reshape(L, 4, 8, 128).transpose(0, 3, 2, 1).reshape(L, 128, 32), dtype=f32)
        cba = np.concatenate([conv_b[:, g * 512:(g + 1) * 512], conv_b[:, 1024 + g * 512:1024 + (g + 1) * 512]], axis=1)
        d["cb"] = np.ascontiguousarray(cba.reshape(L, 8, 128).transpose(0, 2, 1), dtype=f32)
        d["bg"] = np.ascontiguousarray(np.stack([b_igate[:, 2 * g:2 * g + 2], b_fgate[:, 2 * g:2 * g + 2]], axis=1).reshape(L, 2, 2, 1), dtype=f32)
        gno = np.concatenate([gn_m[:, g * 512:(g + 1) * 512], gn_r[:, g * 512:(g + 1) * 512]], axis=1)
        d["gn_fm"] = np.ascontiguousarray(gno.reshape(L, 8, 128).transpose(0, 2, 1), dtype=f32)
        d["wo"] = np.ascontiguousarray(np.concatenate(
            [w_out[:, g * 512:(g + 1) * 512, :], w_out[:, 1024 + g * 512:1024 + (g + 1) * 512, :]], axis=1), dtype=f32)
        mrt = np.zeros((128, 2, 128), np.float64)
        tabs = np.zeros((128, 8), np.float64)
        for hh in range(2):
            h = 2 * g + hh
            gamma = 1.0 - 2.0 ** (-5.0 - h)
            s = np.arange(128, dtype=np.float64)
            mrt[:, hh, :] = np.where(s_idx <= l_idx, (gamma ** (-s))[:, None] / 16.0, 0.0)
            tabs[:, hh] = gamma ** s
            tabs[:, 2 + hh] = gamma ** (128.0 - s) / 16.0
            tabs[:, 4 + hh] = gamma ** 128.0
        tabs[:, 6] = inv_freq.astype(np.float64)
        d["mrt"] = mrt.reshape(128, 256).astype(f32)
        d["tabs"] = tabs.astype(f32)
        per_g.append(d)
    in_maps = []
    for core in range(8):
        b, g = core // 2, core % 2
        m = dict(shared)
        m.update(per_g[g])
        m["x"] = np.ascontiguousarray(x[b, :S], dtype=f32)
        m["pos"] = np.ascontiguousarray(positions[b, :S].reshape(1, S), dtype=np.int32)
        m["c_fm"] = np.ascontiguousarray(c[b].reshape(8, 128).T, dtype=f32)
        in_maps.append(m)
    return in_maps


_NC_CACHE = {}


def run(S, debug=False, **inputs):
    inputs = {k: np.asarray(v) for k, v in inputs.items()}
    key = (S, debug)
    if key not in _NC_CACHE:
        _NC_CACHE[key] = build(S, debug)
    nc = _NC_CACHE[key]
    in_maps = _host_inputs(S, **inputs)
    res = run_bass_kernel_spmd(nc, in_maps, core_ids=list(range(8)))
    NT = S // TT
    out = np.zeros((4, S, D), np.float32)
    for core in range(8):
        b, r = core // 2, core % 2
        o = np.asarray(res.results[core]["out"]).reshape(NT, 256, D)
        out[b].reshape(NT, 2, 256, D)[:, r] = o
    return out, res


def kernel(**inputs):
    S = np.asarray(inputs["x"]).shape[1]
    out, _ = run(S, False, **inputs)
    return out
```

```python
import contextlib
import math
import numpy as np
import ml_dtypes
import concourse.bass as bass
import concourse.mybir as mybir
from concourse.bass_utils import run_bass_kernel_spmd

F32 = mybir.dt.float32
BF16 = mybir.dt.bfloat16
I32 = mybir.dt.int32
ALU = mybir.AluOpType
AF = mybir.ActivationFunctionType
AX = mybir.AxisListType

D = 1024
DEPTH = 2
EPS = 1e-6
TT = 512
NDMA = 24
EPOCH = 30000
LN16 = math.log(16.0)
TWO_PI = 2.0 * math.pi
CW1 = 6.28125
CW2 = TWO_PI - 6.28125


def _freeze(fn):
    import types
    if getattr(fn, '__closure__', None) is None:
        return fn
    cells = []
    for c in fn.__closure__:
        try:
            v = c.cell_contents
        except ValueError:
            cells.append(c)
            continue
        if isinstance(v, types.FunctionType):
            v = _freeze(v)
        cells.append(types.CellType(v))
    return types.FunctionType(fn.__code__, fn.__globals__, fn.__name__, fn.__defaults__, tuple(cells))


class T:
    def __init__(self, t):
        self.t = t
        self.lw = {}
        self.rd = {}

    def __getitem__(self, idx):
        return self.t[idx]


class TV(T):
    def __init__(self, base, ap):
        self.t = ap
        self.lw = base.lw
        self.rd = base.rd


class Sched:
    def __init__(self, nc, es):
        self.nc = nc
        self.es = es
        self.eng = {'pe': nc.tensor, 'act': nc.scalar, 'dve': nc.vector, 'pool': nc.gpsimd, 'sp': nc.sync}
        self.cnt = {e: 0 for e in self.eng}
        self.sems = {e: [] for e in self.eng}
        self.waited = {e: {} for e in self.eng}
        self.dma_sems = [es.enter_context(nc.semaphore("dma%d" % i)) for i in range(NDMA)]
        self.dma_val = [0] * NDMA
        self.dma_rr = 0
        self.cc_sem = es.enter_context(nc.semaphore("ccs"))
        self.cc_val = 0
        self.nwait = 0
        self.rec = []

    def _sem(self, e, ep):
        while len(self.sems[e]) <= ep:
            self.sems[e].append(self.es.enter_context(self.nc.semaphore("s_%s_%d" % (e, len(self.sems[e])))))
        return self.sems[e][ep]

    def _wait(self, e, key, val):
        w = self.waited[e]
        if w.get(key, 0) >= val:
            return
        w[key] = val
        self.nwait += 1
        if key[0] == 'E':
            ep = (val - 1) // EPOCH
            self.eng[e].wait_ge(self._sem(key[1], ep), val - ep * EPOCH)
        elif key[0] == 'D':
            self.eng[e].wait_ge(self.dma_sems[key[1]], val)
        else:
            self.eng[e].wait_ge(self.cc_sem, val)

    def op(self, e, reads, writes, fn, kind='c', dur=None):
        self.rec.append((e, list(reads), list(writes), _freeze(fn), kind, dur))

    def run(self, lat=0.25):
        import heapq
        rec = self.rec
        n = len(rec)
        DEF = {'pe': 0.35, 'act': 0.5, 'dve': 0.45, 'pool': 0.9, 'sp': 0.1}
        lastw = {}
        readers = {}
        preds = [None] * n
        succ = [[] for _ in range(n)]
        lastcc = None
        for i, (e, reads, writes, fn, kind, dur) in enumerate(rec):
            p = set()
            for t in reads:
                k = id(t.lw)
                if getattr(t, 'psum', False):
                    writes = writes + [t]
                    continue
                if k in lastw:
                    p.add(lastw[k])
            for t in writes:
                k = id(t.lw)
                if k in lastw:
                    p.add(lastw[k])
                for r in readers.get(k, ()):
                    p.add(r)
            if kind == 'cc':
                if lastcc is not None:
                    p.add(lastcc)
                lastcc = i
            p.discard(i)
            preds[i] = p
            for q in p:
                succ[q].append(i)
            for t in reads:
                if getattr(t, 'psum', False):
                    continue
                readers.setdefault(id(t.lw), []).append(i)
            for t in writes:
                k = id(t.lw)
                lastw[k] = i
                readers[k] = []
        indeg = [len(p) for p in preds]
        ready_t = [0.0] * n
        fin = [0.0] * n
        efree = {e: 0.0 for e in self.eng}
        pend = {e: [] for e in self.eng}
        avail = {e: [] for e in self.eng}
        for i in range(n):
            if indeg[i] == 0:
                heapq.heappush(pend[rec[i][0]], (0.0, i))
        order = []
        done = 0
        while done < n:
            best = None
            for e in self.eng:
                pe_, av = pend[e], avail[e]
                while pe_ and pe_[0][0] <= efree[e]:
                    heapq.heappush(av, heapq.heappop(pe_)[1])
                if av:
                    cand = (efree[e], av[0], e, True)
                elif pe_:
                    cand = (pe_[0][0], pe_[0][1], e, False)
                else:
                    continue
                if best is None or cand[:2] < best[:2]:
                    best = cand
            start, i, e, from_av = best
            if from_av:
                heapq.heappop(avail[e])
            else:
                heapq.heappop(pend[e])
            kind, dur = rec[i][4], rec[i][5]
            d = dur if dur is not None else DEF[e]
            if kind == 'dma':
                efree[e] = start + 0.1
                fin[i] = start + (dur if dur is not None else 2.5)
            elif kind == 'cc':
                efree[e] = start + 0.5
                fin[i] = start + 60.0
            else:
                efree[e] = start + d
                fin[i] = start + d
            order.append(i)
            done += 1
            for q in succ[i]:
                indeg[q] -= 1
                ready_t[q] = max(ready_t[q], fin[i] + (0.0 if (rec[q][0] == e and e == 'pe') else lat))
                if indeg[q] == 0:
                    heapq.heappush(pend[rec[q][0]], (ready_t[q], q))
        self.est_total = max(fin) if n else 0.0
        for i in order:
            e, reads, writes, fn, kind, dur = rec[i]
            self._emit(e, reads, writes, fn, kind)
        self.rec = []

    def _emit(self, e, reads, writes, fn, kind='c'):
        writes = list(writes) + [t for t in reads if getattr(t, 'psum', False)]
        deps = {}
        for t in reads:
            for k, v in t.lw.items():
                deps[k] = max(deps.get(k, 0), v)
        for t in writes:
            for k, v in t.lw.items():
                deps[k] = max(deps.get(k, 0), v)
            for k, v in t.rd.items():
                deps[k] = max(deps.get(k, 0), v)
        for k, v in deps.items():
            if k == ('E', 'pe') and e == 'pe':
                continue
            self._wait(e, k, v)
        if kind == 'dma':
            i = self.dma_rr
            self.dma_rr = (i + 1) % NDMA
            if self.dma_val[i] > 0:
                self._wait(e, ('D', i), self.dma_val[i])
            inst = fn()
            self.dma_val[i] += 16
            inst.then_inc(self.dma_sems[i], 16)
            tok = (('D', i), self.dma_val[i])
        elif kind == 'cc':
            inst = fn()
            self.cc_val += 1
            inst.then_inc(self.cc_sem, 1)
            tok = (('C',), self.cc_val)
        else:
            inst = fn()
            self.cnt[e] += 1
            n = self.cnt[e]
            ep = (n - 1) // EPOCH
            inst.then_inc(self._sem(e, ep), 1)
            tok = (('E', e), n)
        for t in reads:
            t.rd[tok[0]] = max(t.rd.get(tok[0], 0), tok[1])
        for t in writes:
            t.lw.clear()
            t.lw[tok[0]] = tok[1]
            t.rd.clear()
        return tok

    def finish(self, e):
        for src in self.eng:
            if self.cnt[src] > 0 and not (src == e == 'pe'):
                self._wait(e, ('E', src), self.cnt[src])
        for i in range(NDMA):
            if self.dma_val[i] > 0:
                self._wait(e, ('D', i), self.dma_val[i])
        if self.cc_val > 0:
            self._wait(e, ('C',), self.cc_val)


def build(S=8192, debug=False):
    NT = S // TT
    nc = bass.Bass("TRN2", target_bir_lowering=False)
    es = contextlib.ExitStack()

    def din(name, shape, dt=F32):
        return nc.dram_tensor(name, list(shape), dt, kind="ExternalInput")

    x_d = din("x", [S, D])
    pos_d = din("pos", [1, S], I32)
    c_d = din("c_fm", [128, 8])
    wada_d = din("wada", [DEPTH, D, 3 * D])
    badafm_d = din("bada_fm", [DEPTH, 128, 16])
    badag_d = din("bada_g", [DEPTH, 1, D])
    normg_d = din("normg_fm", [DEPTH, 128, 8])
    wf_d = din("wf", [DEPTH, D, 2048])
    wt_d = din("wt", [DEPTH, D, 2560])
    wg_d = din("wg", [DEPTH, D, 4])
    cw_d = din("cw", [DEPTH, 128, 32])
    cb_d = din("cb", [DEPTH, 128, 8])
    bg_d = din("bg", [DEPTH, 2, 2, 1])
    gn_d = din("gn_fm", [DEPTH, 128, 8])
    wo_d = din("wo", [DEPTH, D, D])
    fg_d = din("finalg", [1, D])
    identb_d = din("ident_b", [128, 128], BF16)
    identf_d = din("ident_f", [128, 128])
    mask_d = din("mask01", [128, 128])
    mrt_d = din("mrt", [128, 256])
    tabs_d = din("tabs", [128, 8])
    bmask_d = din("bmask", [2, 8])
    out_d = nc.dram_tensor("out", [NT * 256, D], F32, kind="ExternalOutput")
    part_d = [nc.dram_tensor("part%d" % l, [NT, TT, D], F32) for l in range(DEPTH)]
    xs0_d = nc.dram_tensor("xs0", [NT, TT, D], F32)
    xs1_d = nc.dram_tensor("xs1", [NT, 256, D], F32)
    dbg = {}
    if debug:
        dbg['hT'] = nc.dram_tensor("dbg_hT", [128, 8, TT], BF16, kind="ExternalOutput")
        dbg['qT'] = nc.dram_tensor("dbg_qT", [128, 4, TT], BF16, kind="ExternalOutput")
        dbg['kT'] = nc.dram_tensor("dbg_kT", [128, 4, TT], BF16, kind="ExternalOutput")
        dbg['rqT'] = nc.dram_tensor("dbg_rqT", [128, 4, TT], BF16, kind="ExternalOutput")
        dbg['rkT'] = nc.dram_tensor("dbg_rkT", [128, 4, TT], BF16, kind="ExternalOutput")
        dbg['ytok'] = nc.dram_tensor("dbg_ytok", [128, 1024], BF16, kind="ExternalOutput")
        dbg['tg'] = nc.dram_tensor("dbg_tg", [128, 16], F32, kind="ExternalOutput")
        dbg['hbuf'] = nc.dram_tensor("dbg_hbuf", [128, 4 * 257], F32, kind="ExternalOutput")
        dbg['xo'] = nc.dram_tensor("dbg_xo", [128, 1024], F32, kind="ExternalOutput")

    with es:
        K = Sched(nc, es)

        def sb(name, shape, dt=F32):
            return T(es.enter_context(nc.sbuf_tensor("sb_" + name, list(shape), dt)))

        def dr(t):
            return T(t)

        PE, ACT, DVE, POOL, SP = 'pe', 'act', 'dve', 'pool', 'sp'
        E = K.eng

        def dma(out_t, out_ap, in_t, in_ap, q=SP):
            reads = [in_t] if in_t is not None else []
            writes = [out_t] if out_t is not None else []
            return K.op(q, reads, writes, lambda: E[q].dma_start(out=out_ap, in_=in_ap), kind='dma')

        banks = [es.enter_context(nc.psum_tensor("psb%d" % i, [128, 512], F32)) for i in range(8)]
        class TP(T):
            psum = True
        _T = T
        psA = [TP(banks[0]), TP(banks[1])]
        ps_s = TP(banks[2])
        ps_num = TP(banks[3])
        ps_u = [TP(banks[4]), TP(banks[5])]
        ps_tr = [TP(banks[6]), TP(banks[6])]
        ps_yt = TP(banks[6])
        ps_sm = TP(banks[7])
        ps_kt = ps_sm
        bank6_bf = banks[6][:].bitcast(BF16)
        kt_bf = banks[7][:, 128:256].bitcast(BF16)
        abank = [0]

        def next_ps():
            abank[0] ^= 1
            return psA[abank[0]]

        ident_b = sb("ident_b", [128, 128], BF16)
        ident_f = sb("ident_f", [128, 128])
        mask01 = sb("mask01", [128, 128])
        mrt = sb("mrt", [128, 2, 128])
        tabs = sb("tabs", [128, 8])
        bmask = sb("bmask", [2, 2, 4])
        c_fm = sb("c_fm", [128, 8])
        c_act = sb("c_act", [128, 8])
        onesK = sb("onesK", [2, 128])
        mhalf = sb("mhalf", [128, 4])
        dma(ident_b, ident_b[:], None, identb_d[:, :])
        dma(ident_f, ident_f[:], None, identf_d[:, :])
        dma(mask01, mask01[:], None, mask_d[:, :])
        dma(mrt, mrt[:], None, mrt_d.ap().rearrange("p (h l) -> p h l", h=2))
        dma(tabs, tabs[:], None, tabs_d[:, :])
        dma(bmask, bmask[:], None, bmask_d.ap().rearrange("p (h c) -> p h c", h=2))
        dma(c_fm, c_fm[:], None, c_d[:, :])
        K.op(DVE, [], [onesK], lambda: E[DVE].memset(onesK[:], 1.0))
        K.op(DVE, [], [mhalf], lambda: E[DVE].memset(mhalf[:], -0.5))
        K.op(ACT, [c_fm], [c_act], lambda: E[ACT].activation(out=c_act[:], in_=c_fm[:], func=AF.Silu))
        gl = lambda j: tabs[:, j:j + 1]
        zt = lambda j: tabs[:, 2 + j:3 + j]
        g128 = lambda j: tabs[:, 4 + j:5 + j]
        invf = tabs[:, 6:7]

        wf_bf = sb("wf_bf", [128, 8, 2048], BF16)
        wt_bf = sb("wt_bf", [128, 8, 2560], BF16)
        wg_bf = sb("wg_bf", [128, 8, 4], BF16)
        wo_bf = sb("wo_bf", [128, 8, D], BF16)
        sti = [0]
        cw = sb("cw", [128, 8, 4])
        cb = sb("cb", [128, 8])
        bgi = sb("bgi", [2, 1])
        bgf = sb("bgf", [2, 1])
        nbf = sb("nbf", [2, 1])
        gn_fm = sb("gn_fm", [128, 8])
        normg = sb("normg", [128, 8])
        bada_fm = sb("bada_fm", [128, 16])
        gsc = sb("gsc", [128, 8])
        shf = sb("shf", [128, 8])
        gate_bc = sb("gate_bc", [128, D])

        xin0 = sb("xin0", [128, D])
        xin = [xin0, xin0]
        ss = sb("ss", [128, 1])
        rstd = sb("rstd", [128, 1])
        xn = sb("xn", [128, D], BF16)
        hTs = [sb("hT%d" % i, [128, 8, TT], BF16) for i in range(2)]
        f1 = sb("f1", [128, TT])
        f2 = sb("f2", [128, TT])
        f3 = sb("f3", [128, TT])
        cosT = sb("cosT", [128, TT])
        sinT = sb("sinT", [128, TT])
        carry = sb("carry", [128, 8, 3])
        qTs = [[sb("qT%d_%d" % (j, i), [128, TT], BF16) for i in range(4)] for j in range(2)]
        kTs = [[sb("kT%d_%d" % (j, i), [128, TT], BF16) for i in range(4)] for j in range(2)]
        rqTs = [[sb("rqT%d_%d" % (j, i), [128, TT], BF16) for i in range(4)] for j in range(2)]
        rkTs = [[sb("rkT%d_%d" % (j, i), [128, TT], BF16) for i in range(4)] for j in range(2)]
        li = sb("li", [2, TT])
        ge = sb("ge", [2, TT])
        bneg = sb("bneg", [2, TT])
        bcar = sb("bcar", [2, 1])
        cm = sb("cm", [2, 4])
        gmall = sb("gmall", [2, 5])
        dd = sb("dd", [2, 4])
        dexp = sb("dexp", [2, 2, 4])
        tgs = [sb("tg%d" % i, [128, 4, 4]) for i in range(2)]
        dec_bcs = [sb("dec_bc%d" % i, [128, 2, 4]) for i in range(2)]
        Vms = [sb("Vm%d" % i, [128, 2, 257], BF16) for i in range(2)]
        Vrs = [sb("Vr%d" % i, [128, 2, 256], BF16) for i in range(2)]
        og = sb("og", [128, 512])
        rzg = sb("rzg", [128, 512])
        swTs = [sb("swT%d" % i, [128, 128], BF16) for i in range(2)]
        kws = [sb("kw%d" % i, [128, 256], BF16) for i in range(2)]
        C32 = [sb("C32_%d" % i, [128, 2, 257]) for i in range(2)]
        Cbf = [sb("Cbf_%d" % i, [128, 2, 257], BF16) for i in range(2)]
        R32 = [sb("R32_%d" % i, [128, 2, 256]) for i in range(2)]
        Rbf = [sb("Rbf_%d" % i, [128, 2, 256], BF16) for i in range(2)]
        hb = [sb("hb%d" % i, [128, 256]) for i in range(4)]
        den2 = sb("den2", [128, 2])
        st6 = sb("st6", [128, 4, 6])
        mv = sb("mv", [128, 4, 2])
        rinv = sb("rinv", [128, 4])
        v4 = sb("v4", [128, 4])
        sd4 = sb("sd4", [128, 4])
        s4 = sb("s4", [128, 4])
        nb4 = sb("nb4", [128, 4])
        ytok = sb("ytok", [128, D], BF16)
        yT = sb("yT", [128, 8, 128], BF16)
        xres = sb("xres", [128, D])
        xo = sb("xo", [128, D])
        gb_bc = xo
        junk = xn
        posi_v = f2[:, :].bitcast(I32)
        cacc = f3
        xpre = sb("xpre", [128, TT + 3])
        f4 = T(xpre.t)
        f4 = xpre
        U = li
        wl = li
        fl = bneg
        c_rep_v = xres[:, :].rearrange("p (k c) -> p k c", k=8)
        stage_ada = [TV(hTs[i], hTs[i][:, :, :].rearrange("p k t -> p (k t)").bitcast(F32)) for i in range(2)]
        stage_w = [f1, f2, f3, xpre, cosT, sinT]
        for i in range(2):
            K.op(DVE, [], [Vms[i]], lambda i=i: E[DVE].memset(Vms[i][:], 1.0))
        K.op(DVE, [tabs], [rinv], lambda: E[DVE].tensor_copy(out=rinv[:, 2:4], in_=tabs[:, 0:2]))

        part_t = [[[dr(part_d[l]) for _ in range(4)] for _ in range(NT)] for l in range(DEPTH)]
        xs0_t = [dr(xs0_d) for _ in range(NT)]
        xs1_t = [dr(xs1_d) for _ in range(NT)]
        x_t = dr(x_d)
        out_t = dr(out_d)

        def dump(name, t, ap):
            if debug and name in dbg:
                dma(None, dbg[name].ap() if ap is None else ap, t, t[:])

        for l in range(DEPTH):
            dma(cw, cw[:], None, cw_d[l].rearrange("p (c k) -> p c k", c=8))
            dma(cb, cb[:], None, cb_d[l])
            dma(bgi, bgi[:], None, bg_d[l][0])
            dma(bgf, bgf[:], None, bg_d[l][1])
            dma(gn_fm, gn_fm[:], None, gn_d[l])
            dma(normg, normg[:], None, normg_d[l])
            dma(bada_fm, bada_fm[:], None, badafm_d[l])
            dma(gb_bc, gb_bc[:], None, badag_d[l].broadcast_to([128, D]))
            K.op(DVE, [c_act], [xres], lambda: E[DVE].tensor_copy(
                out=c_rep_v, in_=c_act[:].unsqueeze(2).broadcast_to([128, 8, 128])))
            K.op(DVE, [bgf], [nbf], lambda: E[DVE].tensor_scalar(
                out=nbf[:], in0=bgf[:], scalar1=-1.0, scalar2=None, op0=ALU.mult))
            for j in range(24):
                stg = stage_ada[j % 2]
                sv = stg[:, 0:1024].rearrange("p (k c) -> p k c", k=8)
                dma(stg, sv, None, wada_d[l][:, j * 128:(j + 1) * 128].rearrange("(k p) c -> p k c", p=128))
                if j < 16:
                    for ci in range(1):
                        col = j

                        def f(col=col, ci=ci, sv=sv):
                            for kt in range(8):
                                r = E[PE].matmul(banks[7][:, 64 + col:65 + col], lhsT=sv[:, kt, ci * 128:(ci + 1) * 128],
                                                 rhs=c_act[:, kt:kt + 1], start=(kt == 0), stop=(kt == 7))
                            return r
                        K.op(PE, [stg, c_act], [ps_sm], f)
                else:
                    pst = next_ps()

                    def f(pst=pst, sv=sv):
                        for kt in range(8):
                            r = E[PE].matmul(pst[:, 0:128], lhsT=c_rep_v[:, kt, :], rhs=sv[:, kt, :],
                                             start=(kt == 0), stop=(kt == 7))
                        return r
                    K.op(PE, [stg, xres], [pst], f)
                    K.op(DVE, [pst, gb_bc], [gate_bc], lambda pst=pst, j=j: E[DVE].tensor_tensor(
                        out=gate_bc[:, (j - 16) * 128:(j - 15) * 128], in0=pst[:, 0:128],
                        in1=gb_bc[:, (j - 16) * 128:(j - 15) * 128], op=ALU.add))
            K.op(DVE, [ps_sm, bada_fm], [shf], lambda: E[DVE].tensor_tensor(
                out=shf[:], in0=banks[7][:, 64:72], in1=bada_fm[:, 0:8], op=ALU.add))
            K.op(DVE, [ps_sm, bada_fm], [gsc], lambda: E[DVE].scalar_tensor_tensor(
                out=gsc[:], in0=banks[7][:, 72:80], scalar=1.0, in1=bada_fm[:, 8:16], op0=ALU.add, op1=ALU.add))
            K.op(DVE, [gsc, normg], [gsc], lambda: E[DVE].tensor_tensor(
                out=gsc[:], in0=gsc[:], in1=normg[:], op=ALU.mult))
            ci_ = 0
            for (wd, wb, ncol, scl) in ((wf_d, wf_bf, 2048, False), (wg_d, wg_bf, 4, False),
                                        (wt_d, wt_bf, 2560, False), (wo_d, wo_bf, D, True)):
                for kt in range(8):
                  for c0 in range(0, ncol, 512):
                    c1 = min(ncol, c0 + 512)
                    stg = stage_w[sti[0] % 6]
                    sti[0] += 1
                    dma(stg, stg[:, 0:c1 - c0], None, wd[l][kt * 128:(kt + 1) * 128, c0:c1])
                    ce = (ACT, DVE, POOL, ACT, DVE)[ci_ % 5]
                    ci_ += 1
                    if scl:
                        K.op(DVE, [stg, gn_fm], [wb], lambda stg=stg, wb=wb, kt=kt, c0=c0, c1=c1: E[DVE].tensor_scalar(
                            out=wb[:, kt, c0:c1], in0=stg[:, 0:c1 - c0], scalar1=gn_fm[:, kt:kt + 1], scalar2=None,
                            op0=ALU.mult), dur=0.6)
                    elif ce == POOL:
                        K.op(POOL, [stg], [wb], lambda stg=stg, wb=wb, kt=kt, c0=c0, c1=c1: E[POOL].tensor_copy(
                            out=wb[:, kt, c0:c1], in_=stg[:, 0:c1 - c0]), dur=1.9)
                    elif ce == DVE:
                        K.op(DVE, [stg], [wb], lambda stg=stg, wb=wb, kt=kt, c0=c0, c1=c1: E[DVE].tensor_copy(
                            out=wb[:, kt, c0:c1], in_=stg[:, 0:c1 - c0]), dur=0.6)
                    else:
                        K.op(ACT, [stg], [wb], lambda stg=stg, wb=wb, kt=kt, c0=c0, c1=c1: E[ACT].activation(
                            out=wb[:, kt, c0:c1], in_=stg[:, 0:c1 - c0], func=AF.Copy), dur=0.65)
            for i in range(2):
                K.op(POOL, [], [C32[i]], lambda i=i: E[POOL].memset(C32[i][:], 0.0))
                K.op(POOL, [], [Cbf[i]], lambda i=i: E[POOL].memset(Cbf[i][:], 0.0))
                K.op(POOL, [], [R32[i]], lambda i=i: E[POOL].memset(R32[i][:], 0.0))
                K.op(POOL, [], [Rbf[i]], lambda i=i: E[POOL].memset(Rbf[i][:], 0.0))
            K.op(POOL, [], [carry], lambda: E[POOL].memset(carry[:], 0.0))
            K.op(POOL, [], [bcar], lambda: E[POOL].memset(bcar[:], 0.0))
            K.op(POOL, [], [gmall], lambda: E[POOL].memset(gmall[:], 0.0))

            def gen_ABC(ti):
                par = ti % 2
                hT, qT, kT, rqT, rkT, tg, dec_bc = hTs[par], qTs[par], kTs[par], rqTs[par], rkTs[par], tgs[par], dec_bcs[par]
                for sub in range(4):
                    xi = xin[sub % 2]
                    if l == 0:
                        dma(xi, xi[:], x_t, x_d[ti * TT + sub * 128: ti * TT + (sub + 1) * 128, :])
                    else:
                        dma(xi, xi[:], xs0_t[ti], xs0_d[ti][sub * 128:(sub + 1) * 128, :])
                    K.op(ACT, [xi], [junk, ss], lambda xi=xi: E[ACT].activation(
                        out=junk[:], in_=xi[:], func=AF.Square, accum_out=ss[:]), dur=1.0)
                    K.op(DVE, [ss], [ss], lambda: E[DVE].tensor_scalar(
                        out=ss[:], in0=ss[:], scalar1=1.0 / D, scalar2=EPS, op0=ALU.mult, op1=ALU.add))
                    K.op(POOL, [ss, mhalf], [rstd], lambda: E[POOL].tensor_tensor(out=rstd[:], in0=ss[:], in1=mhalf[:, 0:1], op=ALU.pow))
                    K.op(DVE, [xi, rstd], [xn], lambda xi=xi, sub=sub: E[DVE].tensor_scalar(
                        out=xn[:], in0=xi[:], scalar1=rstd[:, 0:1], scalar2=None, op0=ALU.mult), dur=1.0)

                    def f():
                        for kt in range(8):
                            r = E[PE].transpose(out=bank6_bf[:, kt * 128:(kt + 1) * 128],
                                                in_=xn[:, kt * 128:(kt + 1) * 128], identity=ident_b[:])
                        return r
                    K.op(PE, [xn, ident_b], [ps_tr[0], ps_tr[1]], f, dur=1.0)
                    for kt in range(8):
                        K.op(ACT, [ps_tr[0], ps_tr[1], gsc, shf], [hT], lambda kt=kt, sub=sub: E[ACT].activation(
                            out=hT[:, kt, sub * 128:(sub + 1) * 128], in_=bank6_bf[:, kt * 128:(kt + 1) * 128],
                            func=AF.Identity, scale=gsc[:, kt:kt + 1], bias=shf[:, kt:kt + 1]))
                    yield
                if l == 0 and ti == 0:
                    dump('hT', hT, None)
                dma(f2, posi_v, None, pos_d[0:1, ti * TT:(ti + 1) * TT].broadcast_to([128, TT]))
                K.op(DVE, [f2], [f1], lambda: E[DVE].tensor_copy(out=f1[:], in_=posi_v))
                K.op(DVE, [f1, tabs], [f1], lambda: E[DVE].tensor_scalar(
                    out=f1[:], in0=f1[:], scalar1=invf, scalar2=None, op0=ALU.mult))
                K.op(DVE, [f1], [f2], lambda: E[DVE].tensor_scalar(
                    out=posi_v, in0=f1[:], scalar1=1.0 / TWO_PI, scalar2=None, op0=ALU.mult))
                K.op(DVE, [f2], [f3], lambda: E[DVE].tensor_copy(out=f3[:], in_=posi_v))
                K.op(DVE, [f3, f1], [f1], lambda: E[DVE].scalar_tensor_tensor(
                    out=f1[:], in0=f3[:], scalar=-CW1, in1=f1[:], op0=ALU.mult, op1=ALU.add))
                K.op(DVE, [f3, f1], [f1], lambda: E[DVE].scalar_tensor_tensor(
                    out=f1[:], in0=f3[:], scalar=-CW2, in1=f1[:], op0=ALU.mult, op1=ALU.add))
                K.op(DVE, [f1], [f3], lambda: E[DVE].tensor_scalar(
                    out=f3[:], in0=f1[:], scalar1=math.pi, scalar2=-TWO_PI, op0=ALU.is_gt, op1=ALU.mult))
                K.op(DVE, [f1, f3], [f3], lambda: E[DVE].tensor_tensor(out=f3[:], in0=f3[:], in1=f1[:], op=ALU.add))
                K.op(DVE, [f3], [f3], lambda: E[DVE].tensor_scalar(
                    out=f3[:], in0=f3[:], scalar1=-math.pi, scalar2=math.pi, op0=ALU.max, op1=ALU.min))
                K.op(ACT, [f3], [sinT], lambda: E[ACT].activation(out=sinT[:], in_=f3[:], func=AF.Sin))
                K.op(DVE, [f1], [f2], lambda: E[DVE].tensor_scalar(
                    out=f2[:], in0=f1[:], scalar1=math.pi / 2, scalar2=-TWO_PI, op0=ALU.is_gt, op1=ALU.mult))
                K.op(DVE, [f1, f2], [f2], lambda: E[DVE].scalar_tensor_tensor(
                    out=f2[:], in0=f1[:], scalar=math.pi / 2, in1=f2[:], op0=ALU.add, op1=ALU.add))
                K.op(DVE, [f2], [f2], lambda: E[DVE].tensor_scalar(
                    out=f2[:], in0=f2[:], scalar1=-math.pi, scalar2=math.pi, op0=ALU.max, op1=ALU.min))
                K.op(ACT, [f2], [cosT], lambda: E[ACT].activation(out=cosT[:], in_=f2[:], func=AF.Sin))
                yield

                def fm_mm(pst, ct):
                    def f():
                        for kt in range(8):
                            r = E[PE].matmul(pst[:, :], lhsT=wf_bf[:, kt, ct * 128:(ct + 1) * 128], rhs=hT[:, kt, :],
                                             start=(kt == 0), stop=(kt == 7))
                        return r
                    K.op(PE, [wf_bf, hT], [pst], f, dur=1.9)

                for ct in range(8):
                    pst = next_ps()
                    fm_mm(pst, ct)
                    dst = qT[ct] if ct < 4 else kT[ct - 4]
                    K.op(POOL, [carry], [xpre], lambda ct=ct: E[POOL].tensor_copy(out=xpre[:, 0:3], in_=carry[:, ct, :]))
                    K.op(ACT, [pst], [xpre], lambda pst=pst: E[ACT].activation(
                        out=xpre[:, 3:TT + 3], in_=pst[:, :], func=AF.Copy))
                    K.op(POOL, [xpre], [carry], lambda ct=ct: E[POOL].tensor_copy(
                        out=carry[:, ct, :], in_=xpre[:, TT:TT + 3]))
                    K.op(DVE, [xpre, cw, cb], [cacc], lambda ct=ct: E[DVE].tensor_scalar(
                        out=cacc[:], in0=xpre[:, 3:TT + 3], scalar1=cw[:, ct, 3:4], scalar2=cb[:, ct:ct + 1],
                        op0=ALU.mult, op1=ALU.add))
                    for tap in (2, 1, 0):
                        K.op(DVE, [xpre, cw, cacc], [cacc], lambda ct=ct, tap=tap: E[DVE].scalar_tensor_tensor(
                            out=cacc[:], in0=xpre[:, tap:tap + TT], scalar=cw[:, ct, tap:tap + 1], in1=cacc[:],
                            op0=ALU.mult, op1=ALU.add))
                    K.op(ACT, [cacc], [dst], lambda dst=dst: E[ACT].activation(out=dst[:], in_=cacc[:], func=AF.Silu))
                    yield
                for p in range(4):
                    p1 = next_ps()
                    fm_mm(p1, 8 + 2 * p)
                    p2 = next_ps()
                    fm_mm(p2, 9 + 2 * p)
                    d1 = (rqT if p < 2 else rkT)[2 * (p % 2)]
                    d2 = (rqT if p < 2 else rkT)[2 * (p % 2) + 1]
                    K.op(DVE, [p1, cosT], [f3], lambda p1=p1: E[DVE].tensor_tensor(out=f3[:], in0=p1[:, :], in1=cosT[:], op=ALU.mult))
                    K.op(DVE, [p1, sinT], [f2], lambda p1=p1: E[DVE].tensor_tensor(out=f2[:], in0=p1[:, :], in1=sinT[:], op=ALU.mult))
                    K.op(DVE, [p2, sinT], [f4], lambda p2=p2: E[DVE].tensor_tensor(out=f4[:, 0:TT], in0=p2[:, :], in1=sinT[:], op=ALU.mult))
                    K.op(DVE, [p2, cosT], [f1], lambda p2=p2: E[DVE].tensor_tensor(out=f1[:], in0=p2[:, :], in1=cosT[:], op=ALU.mult))
                    K.op(POOL, [f3, f4], [d1], lambda d1=d1: E[POOL].tensor_tensor(out=d1[:], in0=f3[:], in1=f4[:, 0:TT], op=ALU.subtract), dur=1.4)
                    K.op(POOL, [f1, f2], [d2], lambda d2=d2: E[POOL].tensor_tensor(out=d2[:], in0=f1[:], in1=f2[:], op=ALU.add), dur=1.4)
                    yield
                pgi = next_ps()

                def f(pgi=pgi):
                    for kt in range(8):
                        r = E[PE].matmul(pgi[0:2, :], lhsT=wg_bf[:, kt, 0:2], rhs=hT[:, kt, :], start=(kt == 0), stop=(kt == 7))
                    return r
                K.op(PE, [wg_bf, hT], [pgi], f, dur=1.8)
                K.op(ACT, [pgi, bgi], [li], lambda pgi=pgi: E[ACT].activation(
                    out=li[:], in_=pgi[0:2, :], func=AF.Identity, bias=bgi[:, 0:1], scale=1.0))
                pgf = next_ps()

                def f(pgf=pgf):
                    for kt in range(8):
                        r = E[PE].matmul(pgf[0:2, :], lhsT=wg_bf[:, kt, 2:4], rhs=hT[:, kt, :], start=(kt == 0), stop=(kt == 7))
                    return r
                K.op(PE, [wg_bf, hT], [pgf], f, dur=1.8)
                K.op(ACT, [pgf, nbf], [ge], lambda pgf=pgf: E[ACT].activation(
                    out=ge[:], in_=pgf[0:2, :], func=AF.Exp, bias=nbf[:, 0:1], scale=-1.0))
                K.op(ACT, [ge], [ge], lambda: E[ACT].activation(out=ge[:], in_=ge[:], func=AF.Ln, bias=1.0, scale=1.0))
                K.op(DVE, [ge, bcar], [bneg], lambda: E[DVE].tensor_tensor_scan(
                    out=bneg[:], data0=ge[:], data1=ge[:], initial=bcar[:, 0:1], op0=ALU.add, op1=ALU.max))
                K.op(DVE, [bneg], [bcar], lambda: E[DVE].tensor_copy(out=bcar[:], in_=bneg[:, TT - 1:TT]))
                K.op(DVE, [li, bneg], [U], lambda: E[DVE].tensor_tensor(out=U[:], in0=li[:], in1=bneg[:], op=ALU.add))
                K.op(DVE, [U], [cm], lambda: E[DVE].tensor_reduce(
                    out=cm[:], in_=U[:].rearrange("p (c t) -> p c t", c=4), axis=AX.X, op=ALU.max))
                K.op(DVE, [gmall], [gmall], lambda: E[DVE].tensor_copy(out=gmall[:, 0:1], in_=gmall[:, 4:5]))
                K.op(DVE, [cm, gmall], [gmall], lambda: E[DVE].tensor_tensor_scan(
                    out=gmall[:, 1:5], data0=cm[:], data1=cm[:], initial=gmall[:, 0:1], op0=ALU.max, op1=ALU.max))
                K.op(DVE, [gmall], [dd], lambda: E[DVE].tensor_tensor(
                    out=dd[:], in0=gmall[:, 0:4], in1=gmall[:, 1:5], op=ALU.subtract))
                K.op(ACT, [dd], [dd], lambda: E[ACT].activation(out=dd[:], in_=dd[:], func=AF.Exp))
                K.op(DVE, [dd, bmask], [dexp], lambda: E[DVE].tensor_tensor(
                    out=dexp[:], in0=dd[:].unsqueeze(1).broadcast_to([2, 2, 4]), in1=bmask[:], op=ALU.mult))
                K.op(PE, [onesK, dexp], [ps_sm], lambda: E[PE].matmul(
                    banks[7][:, 0:8], lhsT=onesK[:], rhs=dexp[:].rearrange("p h c -> p (h c)"), start=True, stop=True))
                K.op(DVE, [ps_sm], [dec_bc], lambda: E[DVE].tensor_copy(
                    out=dec_bc[:].rearrange("p h c -> p (h c)"), in_=banks[7][:, 0:8]))
                gmb = gmall[:, 1:5].unsqueeze(2).broadcast_to([2, 4, 128])
                K.op(DVE, [U, gmall], [wl], lambda: E[DVE].tensor_tensor(
                    out=wl[:].rearrange("p (c t) -> p c t", c=4), in0=U[:].rearrange("p (c t) -> p c t", c=4),
                    in1=gmb, op=ALU.subtract))
                K.op(ACT, [wl], [wl], lambda: E[ACT].activation(out=wl[:], in_=wl[:], func=AF.Exp, bias=-LN16, scale=1.0))
                K.op(DVE, [bneg, gmall], [fl], lambda: E[DVE].tensor_tensor(
                    out=fl[:].rearrange("p (c t) -> p c t", c=4), in0=bneg[:].rearrange("p (c t) -> p c t", c=4),
                    in1=gmb, op=ALU.subtract))
                K.op(ACT, [fl], [fl], lambda: E[ACT].activation(out=fl[:], in_=fl[:], func=AF.Exp))

                def f():
                    for c in range(4):
                        E[PE].transpose(out=banks[7][:, 16 + c * 4:18 + c * 4], in_=wl[:, c * 128:(c + 1) * 128],
                                        identity=ident_f[0:2, 0:2])
                        r = E[PE].transpose(out=banks[7][:, 18 + c * 4:20 + c * 4], in_=fl[:, c * 128:(c + 1) * 128],
                                            identity=ident_f[0:2, 0:2])
                    return r
                K.op(PE, [wl, fl, ident_f], [ps_sm], f)
                K.op(DVE, [ps_sm], [tg], lambda: E[DVE].tensor_copy(
                    out=tg[:].rearrange("p c k -> p (c k)"), in_=banks[7][:, 16:32]))
                yield
                if l == 0 and ti == 0:
                    dump('tg', tg, None)
                    for i in range(4):
                        dump('qT', qT[i], dbg['qT'][:, i, :] if debug else None)
                        dump('kT', kT[i], dbg['kT'][:, i, :] if debug else None)
                        dump('rqT', rqT[i], dbg['rqT'][:, i, :] if debug else None)
                        dump('rkT', rkT[i], dbg['rkT'][:, i, :] if debug else None)

            def gen_D(ti):
                par = ti % 2
                hT, qT, kT, rqT, rkT, tg, dec_bc = hTs[par], qTs[par], kTs[par], rqTs[par], rkTs[par], tgs[par], dec_bcs[par]

                def tm_unit(c, gi):
                    cs = slice(c * 128, (c + 1) * 128)
                    Vm, Vr = Vms[c % 2], Vrs[c % 2]
                    pst = next_ps()

                    def f():
                        for kt in range(8):
                            r = E[PE].matmul(pst[:, :], lhsT=hT[:, kt, cs], rhs=wt_bf[:, kt, gi * 512:(gi + 1) * 512],
                                             start=(kt == 0), stop=(kt == 7))
                        return r
                    K.op(PE, [wt_bf, hT], [pst], f, dur=1.9)
                    if gi == 0:
                        K.op(ACT, [pst], [Vm], lambda: E[ACT].activation(
                            out=Vm[:, :, 0:256], in_=pst[:, :].rearrange("p (h d) -> p h d", h=2), func=AF.Copy))
                    elif gi == 1:
                        K.op(ACT, [pst], [og], lambda: E[ACT].activation(out=og[:], in_=pst[:, :], func=AF.Sigmoid))
                    elif gi == 2:
                        K.op(ACT, [pst], [pst], lambda: E[ACT].activation(out=pst[:, :], in_=pst[:, :], func=AF.Silu))
                        K.op(DVE, [og, pst], [og], lambda: E[DVE].tensor_tensor(out=og[:], in0=og[:], in1=pst[:, :], op=ALU.mult))
                    elif gi == 3:
                        K.op(ACT, [pst], [Vr], lambda: E[ACT].activation(
                            out=Vr[:].rearrange("p h d -> p (h d)"), in_=pst[:, :], func=AF.Copy))
                    else:
                        K.op(ACT, [pst], [rzg], lambda: E[ACT].activation(out=rzg[:], in_=pst[:, :], func=AF.Silu))

                tm_unit(0, 0)
                tm_unit(0, 3)
                yield
                for c in range(4):
                    cs = slice(c * 128, (c + 1) * 128)
                    Vm, Vr = Vms[c % 2], Vrs[c % 2]
                    fill = [1, 2, 4]
                    for hh in range(2):
                        wcol = tg[:, c, hh:hh + 1]
                        dcol = dec_bc[:, hh, c:c + 1]
                        swm, swr, kwm, kwr = swTs[0], swTs[1], kws[0], kws[1]
                        K.op(ACT, [C32[hh], dec_bc], [Cbf[hh]], lambda hh=hh, dcol=dcol: E[ACT].activation(
                            out=Cbf[hh][:].rearrange("p a b -> p (a b)"), in_=C32[hh][:].rearrange("p a b -> p (a b)"),
                            func=AF.Identity, scale=dcol, bias=0.0))
                        def f(hh=hh):
                            for dt in range(2):
                                r = E[PE].matmul(banks[2][:, 0:128], lhsT=kT[2 * hh + dt][:, cs], rhs=qT[2 * hh + dt][:, cs],
                                                 start=(dt == 0), stop=(dt == 1))
                            return r
                        K.op(PE, [kT[2 * hh], kT[2 * hh + 1], qT[2 * hh], qT[2 * hh + 1]], [ps_s], f)
                        K.op(DVE, [ps_s, tg, mask01], [swm], lambda wcol=wcol: E[DVE].scalar_tensor_tensor(
                            out=swm[:], in0=banks[2][:, 0:128], scalar=wcol, in1=mask01[:], op0=ALU.mult, op1=ALU.mult))
                        def f(hh=hh):
                            for dt in range(2):
                                r = E[PE].transpose(out=kt_bf[:, dt * 128:(dt + 1) * 128], in_=kT[2 * hh + dt][:, cs],
                                                    identity=ident_b[:])
                            return r
                        K.op(PE, [kT[2 * hh], kT[2 * hh + 1], ident_b], [ps_kt], f)
                        K.op(ACT, [ps_kt, tg], [kwm], lambda wcol=wcol: E[ACT].activation(
                            out=kwm[:], in_=kt_bf, func=AF.Identity, scale=wcol, bias=0.0))
                        def f(hh=hh):
                            for dt in range(2):
                                r = E[PE].matmul(banks[2][:, 0:128], lhsT=rkT[2 * hh + dt][:, cs], rhs=rqT[2 * hh + dt][:, cs],
                                                 start=(dt == 0), stop=(dt == 1))
                            return r
                        K.op(PE, [rkT[2 * hh], rkT[2 * hh + 1], rqT[2 * hh], rqT[2 * hh + 1]], [ps_s], f)
                        K.op(DVE, [ps_s, mrt], [swr], lambda hh=hh: E[DVE].tensor_tensor(
                            out=swr[:], in0=banks[2][:, 0:128], in1=mrt[:, hh, :], op=ALU.mult))
                        def f(hh=hh):
                            for dt in range(2):
                                r = E[PE].transpose(out=kt_bf[:, dt * 128:(dt + 1) * 128], in_=rkT[2 * hh + dt][:, cs],
                                                    identity=ident_b[:])
                            return r
                        K.op(PE, [rkT[2 * hh], rkT[2 * hh + 1], ident_b], [ps_kt], f)
                        K.op(ACT, [ps_kt, tabs], [kwr], lambda hh=hh: E[ACT].activation(
                            out=kwr[:], in_=kt_bf, func=AF.Identity, scale=zt(hh), bias=0.0))
                        tm_unit(c, fill.pop(0))
                        def f(hh=hh):
                            E[PE].matmul(banks[3][:, 0:257], lhsT=swm[:], rhs=Vm[:, hh, :], start=True, stop=False)
                            E[PE].matmul(banks[3][:, 0:257], lhsT=qT[2 * hh][:, cs], rhs=Cbf[hh][:, 0, :], start=False, stop=False)
                            return E[PE].matmul(banks[3][:, 0:257], lhsT=qT[2 * hh + 1][:, cs], rhs=Cbf[hh][:, 1, :], start=False, stop=True)
                        K.op(PE, [swm, Vm, qT[2 * hh], qT[2 * hh + 1], Cbf[hh]], [ps_num], f, dur=0.5)
                        K.op(ACT, [ps_num], [hb[hh]], lambda hh=hh: E[ACT].activation(
                            out=hb[hh][:], in_=banks[3][:, 0:256], func=AF.Copy))
                        K.op(ACT, [ps_num], [den2], lambda hh=hh: E[ACT].activation(
                            out=den2[:, hh:hh + 1], in_=banks[3][:, 256:257], func=AF.Copy))
                        for dt in range(2):
                            K.op(PE, [kwm, Vm], [ps_u[dt]], lambda hh=hh, dt=dt: E[PE].matmul(
                                banks[4 + dt][:, 0:257], lhsT=kwm[:, dt * 128:(dt + 1) * 128], rhs=Vm[:, hh, :],
                                start=True, stop=True))
                            K.op(DVE, [C32[hh], dec_bc, ps_u[dt]], [C32[hh]], lambda hh=hh, dt=dt, dcol=dcol: E[DVE].scalar_tensor_tensor(
                                out=C32[hh][:, dt, :], in0=C32[hh][:, dt, :], scalar=dcol, in1=banks[4 + dt][:, 0:257],
                                op0=ALU.mult, op1=ALU.add), dur=0.75)
                        def f(hh=hh):
                            E[PE].matmul(banks[3][:, 0:256], lhsT=swr[:], rhs=Vr[:, hh, :], start=True, stop=False)
                            E[PE].matmul(banks[3][:, 0:256], lhsT=rqT[2 * hh][:, cs], rhs=Rbf[hh][:, 0, :], start=False, stop=False)
                            return E[PE].matmul(banks[3][:, 0:256], lhsT=rqT[2 * hh + 1][:, cs], rhs=Rbf[hh][:, 1, :],
                                                start=False, stop=True)
                        K.op(PE, [swr, Vr, rqT[2 * hh], rqT[2 * hh + 1], Rbf[hh]], [ps_num], f, dur=0.5)
                        K.op(ACT, [ps_num], [hb[2 + hh]], lambda hh=hh: E[ACT].activation(
                            out=hb[2 + hh][:], in_=banks[3][:, 0:256], func=AF.Copy))
                        if fill and hh == 1:
                            tm_unit(c, fill.pop(0))
                        for dt in range(2):
                            K.op(PE, [kwr, Vr], [ps_u[dt]], lambda hh=hh, dt=dt: E[PE].matmul(
                                banks[4 + dt][:, 0:256], lhsT=kwr[:, dt * 128:(dt + 1) * 128], rhs=Vr[:, hh, :],
                                start=True, stop=True))
                            K.op(DVE, [R32[hh], tabs, ps_u[dt]], [R32[hh]], lambda hh=hh, dt=dt: E[DVE].scalar_tensor_tensor(
                                out=R32[hh][:, dt, :], in0=R32[hh][:, dt, :], scalar=g128(hh), in1=banks[4 + dt][:, 0:256],
                                op0=ALU.mult, op1=ALU.add))
                        K.op(POOL, [R32[hh]], [Rbf[hh]], lambda hh=hh: E[POOL].tensor_copy(out=Rbf[hh][:], in_=R32[hh][:]), dur=1.9)
                        for j in (hh, 2 + hh):
                            K.op(DVE, [hb[j]], [st6], lambda j=j: E[DVE].bn_stats(out=st6[:, j, :], in_=hb[j][:]))
                            K.op(DVE, [st6], [mv], lambda j=j: E[DVE].bn_aggr(out=mv[:, j, :], in_=st6[:, j, :]))
                        yield
                        yield

                    if c < 3:
                        tm_unit(c + 1, 0)
                        tm_unit(c + 1, 3)
                    yield
                    yield
                    K.op(DVE, [den2], [v4], lambda: E[DVE].tensor_scalar(
                        out=v4[:, 0:2], in0=den2[:], scalar1=-1.0, scalar2=None, op0=ALU.mult))
                    K.op(DVE, [den2, v4], [v4], lambda: E[DVE].tensor_tensor(
                        out=v4[:, 0:2], in0=v4[:, 0:2], in1=den2[:], op=ALU.max))
                    K.op(DVE, [v4, tg], [v4], lambda c=c: E[DVE].tensor_tensor(
                        out=v4[:, 0:2], in0=v4[:, 0:2], in1=tg[:, c, 2:4], op=ALU.max))
                    K.op(DVE, [v4], [rinv], lambda: E[DVE].reciprocal(out=rinv[:, 0:2], in_=v4[:, 0:2]))
                    K.op(DVE, [rinv], [v4], lambda: E[DVE].tensor_tensor(out=v4[:], in0=rinv[:], in1=rinv[:], op=ALU.mult))
                    K.op(DVE, [v4, mv], [v4], lambda: E[DVE].tensor_tensor(out=v4[:], in0=v4[:], in1=mv[:, :, 1], op=ALU.mult))
                    K.op(DVE, [v4], [v4], lambda: E[DVE].tensor_scalar(out=v4[:], in0=v4[:], scalar1=EPS, scalar2=None, op0=ALU.add))
                    K.op(POOL, [v4, mhalf], [s4], lambda: E[POOL].tensor_tensor(out=s4[:], in0=v4[:], in1=mhalf[:], op=ALU.pow))
                    K.op(DVE, [s4, rinv], [s4], lambda: E[DVE].tensor_tensor(out=s4[:], in0=s4[:], in1=rinv[:], op=ALU.mult))
                    K.op(DVE, [mv, s4], [nb4], lambda: E[DVE].scalar_tensor_tensor(
                        out=nb4[:], in0=mv[:, :, 0], scalar=-1.0, in1=s4[:], op0=ALU.mult, op1=ALU.mult))
                    for j in range(4):
                        K.op(ACT, [hb[j], s4, nb4], [hb[j]], lambda j=j: E[ACT].activation(
                            out=hb[j][:], in_=hb[j][:], func=AF.Identity, scale=s4[:, j:j + 1], bias=nb4[:, j:j + 1]))
                    for j in range(4):
                        gsrc = og if j < 2 else rzg
                        K.op(DVE, [hb[j], gsrc], [ytok], lambda j=j, gsrc=gsrc: E[DVE].tensor_tensor(
                            out=ytok[:, j * 256:(j + 1) * 256], in0=hb[j][:], in1=gsrc[:, (j % 2) * 256:(j % 2 + 1) * 256],
                            op=ALU.mult))

                    def f():
                        for ft in range(8):
                            r = E[PE].transpose(out=bank6_bf[:, ft * 128:(ft + 1) * 128], in_=ytok[:, ft * 128:(ft + 1) * 128],
                                                identity=ident_b[:])
                        return r
                    K.op(PE, [ytok, ident_b], [ps_tr[0], ps_tr[1]], f, dur=1.0)
                    K.op(ACT, [ps_tr[0], ps_tr[1]], [yT], lambda: E[ACT].activation(
                        out=yT[:].rearrange("p f t -> p (f t)"), in_=bank6_bf, func=AF.Copy), dur=1.1)
                    if l == 0:
                        dma(xres, xres[:], x_t, x_d[ti * TT + c * 128: ti * TT + (c + 1) * 128, :])
                    else:
                        dma(xres, xres[:], xs0_t[ti], xs0_d[ti][c * 128:(c + 1) * 128, :])
                    for half in range(2):
                        pst = next_ps()

                        def f(pst=pst, half=half):
                            for ft in range(8):
                                r = E[PE].matmul(pst[:, :], lhsT=yT[:, ft, :], rhs=wo_bf[:, ft, half * 512:(half + 1) * 512],
                                                 start=(ft == 0), stop=(ft == 7))
                            return r
                        K.op(PE, [yT, wo_bf], [pst], f, dur=1.9)
                        hs = slice(half * 512, (half + 1) * 512)
                        K.op(DVE, [pst, gate_bc], [xo], lambda pst=pst, hs=hs: E[DVE].tensor_tensor(
                            out=xo[:, hs], in0=pst[:, :], in1=gate_bc[:, hs], op=ALU.mult))
                        K.op(DVE, [xres, xo], [xo], lambda hs=hs: E[DVE].scalar_tensor_tensor(
                            out=xo[:, hs], in0=xres[:, hs], scalar=0.5, in1=xo[:, hs], op0=ALU.mult, op1=ALU.add))
                    dma(part_t[l][ti][c], part_d[l][ti][c * 128:(c + 1) * 128, :], xo, xo[:])
                    yield
                if l == 0:
                    K.op(POOL, part_t[l][ti], [xs0_t[ti]], lambda ti=ti: E[POOL].collective_compute(
                        "AllReduce", ALU.add, replica_groups=[[0, 1], [2, 3], [4, 5], [6, 7]],
                        ins=[part_d[0][ti].opt()], outs=[xs0_d[ti].opt()]), kind='cc')
                else:
                    K.op(POOL, part_t[l][ti], [xs1_t[ti]], lambda ti=ti: E[POOL].collective_compute(
                        "ReduceScatter", ALU.add, replica_groups=[[0, 1], [2, 3], [4, 5], [6, 7]],
                        ins=[part_d[1][ti].opt()], outs=[xs1_d[ti].opt()]), kind='cc')


            for _ in gen_ABC(0):
                pass
            for ti in range(NT):
                gd = gen_D(ti)
                ga = gen_ABC(ti + 1) if ti + 1 < NT else None
                nD, nA = 30, 18
                dcnt = acnt = 0
                while True:
                    try:
                        next(gd)
                    except StopIteration:
                        break
                    dcnt += 1
                    while ga is not None and acnt * nD < dcnt * nA:
                        try:
                            next(ga)
                            acnt += 1
                        except StopIteration:
                            ga = None
                if ga is not None:
                    for _ in ga:
                        pass

        fg_bc = gate_bc
        dma(fg_bc, fg_bc[:], None, fg_d.ap().broadcast_to([128, D]))
        fin_in = [xin0, xres]
        fin_out = [xo] + [TV(hTs[i], hTs[i][:, :, :].rearrange("p k t -> p (k t)").bitcast(F32)) for i in range(2)]
        fin_junk = [xn, ytok]
        fss = [sb("fss%d" % i, [128, 1]) for i in range(2)]
        frs = [sb("frs%d" % i, [128, 1]) for i in range(2)]
        cnt = 0
        for ti in range(NT):
            for sub in range(2):
                xi = fin_in[cnt % 2]
                xoo = fin_out[cnt % 3]
                jk = fin_junk[cnt % 2]
                ss_, rs_ = fss[cnt % 2], frs[cnt % 2]
                cnt += 1
                dma(xi, xi[:], xs1_t[ti], xs1_d[ti][sub * 128:(sub + 1) * 128, :])
                K.op(ACT, [xi], [jk, ss_], lambda xi=xi, jk=jk, ss_=ss_: E[ACT].activation(
                    out=jk[:], in_=xi[:], func=AF.Square, accum_out=ss_[:]), dur=1.0)
                K.op(DVE, [ss_], [ss_], lambda ss_=ss_: E[DVE].tensor_scalar(
                    out=ss_[:], in0=ss_[:], scalar1=1.0 / D, scalar2=EPS, op0=ALU.mult, op1=ALU.add), dur=0.15)
                K.op(POOL, [ss_, mhalf], [rs_], lambda ss_=ss_, rs_=rs_: E[POOL].tensor_tensor(
                    out=rs_[:], in0=ss_[:], in1=mhalf[:, 0:1], op=ALU.pow), dur=0.7)
                K.op(DVE, [xi, rs_, fg_bc], [xoo], lambda xi=xi, rs_=rs_, xoo=xoo: E[DVE].scalar_tensor_tensor(
                    out=xoo[:, 0:D], in0=xi[:], scalar=rs_[:, 0:1], in1=fg_bc[:], op0=ALU.mult, op1=ALU.mult), dur=1.1)
                dma(dr(out_d), out_d[ti * 256 + sub * 128: ti * 256 + (sub + 1) * 128, :], xoo, xoo[:, 0:D])
        K.run()
        K.finish(SP)
    return nc


def _host_inputs(S, x, c, positions, w_ada, b_ada, norm_g, w_in, conv_w, conv_b,
                 b_igate, b_fgate, gn_m, gn_r, w_out, final_g):
    f32 = np.float32
    L = DEPTH
    shared = {}
    shared["wada"] = np.ascontiguousarray(w_ada, dtype=f32)
    shared["bada_fm"] = np.ascontiguousarray(b_ada[:, :2048].reshape(L, 16, 128).transpose(0, 2, 1), dtype=f32)
    shared["bada_g"] = np.ascontiguousarray(b_ada[:, 2048:].reshape(L, 1, D), dtype=f32)
    shared["normg_fm"] = np.ascontiguousarray(norm_g.reshape(L, 8, 128).transpose(0, 2, 1), dtype=f32)
    shared["finalg"] = np.ascontiguousarray(final_g.reshape(1, D), dtype=f32)
    shared["ident_b"] = np.eye(128, dtype=f32).astype(ml_dtypes.bfloat16)
    shared["ident_f"] = np.eye(128, dtype=f32)
    s_idx = np.arange(128)[:, None]
    l_idx = np.arange(128)[None, :]
    shared["mask01"] = (s_idx <= l_idx).astype(f32)
    bm = np.zeros((2, 2, 4), f32)
    bm[0, 0, :] = 1.0
    bm[1, 1, :] = 1.0
    shared["bmask"] = bm.reshape(2, 8)
    inv_freq = (1.0 / (10000.0 ** np.linspace(0.0, 1.0, 128, dtype=np.float32))).astype(f32)
    o = {"mq": 0, "mk": 1024, "mv": 2048, "mo": 3072, "mz": 4096, "mi": 5120, "mf": 5124,
         "rq": 5128, "rk": 6152, "rv": 7176, "rz": 8200}
    per_g = []
    for g in range(2):
        d = {}
        sl = lambda name: slice(o[name] + g * 512, o[name] + (g + 1) * 512)
        d["wf"] = np.ascontiguousarray(np.concatenate(
            [w_in[:, :, sl("mq")], w_in[:, :, sl("mk")], w_in[:, :, sl("rq")], w_in[:, :, sl("rk")]], axis=2), dtype=f32)
        d["wt"] = np.ascontiguousarray(np.concatenate(
            [w_in[:, :, sl("mv")], w_in[:, :, sl("mo")], w_in[:, :, sl("mz")], w_in[:, :, sl("rv")],
             w_in[:, :, sl("rz")]], axis=2), dtype=f32)
        d["wg"] = np.ascontiguousarray(np.concatenate(
            [w_in[:, :, o["mi"] + 2 * g:o["mi"] + 2 * g + 2], w_in[:, :, o["mf"] + 2 * g:o["mf"] + 2 * g + 2]], axis=2),
            dtype=f32)
        cwq = conv_w[:, :, g * 512:(g + 1) * 512]
        cwk = conv_w[:, :, 1024 + g * 512:1024 + (g + 1) * 512]
        cwa = np.concatenate([cwq, cwk], axis=2)
        d["cw"] = np.ascontiguousarray(cwa.reshape(L, 4, 8, 128).transpose(0, 3, 2, 1).reshape(L, 128, 32), dtype=f32)
        cba = np.concatenate([conv_b[:, g * 512:(g + 1) * 512], conv_b[:, 1024 + g * 512:1024 + (g + 1) * 512]], axis=1)
        d["cb"] = np.ascontiguousarray(cba.reshape(L, 8, 128).transpose(0, 2, 1), dtype=f32)
        d["bg"] = np.ascontiguousarray(np.stack([b_igate[:, 2 * g:2 * g + 2], b_fgate[:, 2 * g:2 * g + 2]], axis=1).reshape(L, 2, 2, 1), dtype=f32)
        gno = np.concatenate([gn_m[:, g * 512:(g + 1) * 512], gn_r[:, g * 512:(g + 1) * 512]], axis=1)
        d["gn_fm"] = np.ascontiguousarray(gno.reshape(L, 8, 128).transpose(0, 2, 1), dtype=f32)
        d["wo"] = np.ascontiguousarray(np.concatenate(
            [w_out[:, g * 512:(g + 1) * 512, :], w_out[:, 1024 + g * 512:1024 + (g + 1) * 512, :]], axis=1), dtype=f32)
        mrt = np.zeros((128, 2, 128), np.float64)
        tabs = np.zeros((128, 8), np.float64)
        for hh in range(2):
            h = 2 * g + hh
            gamma = 1.0 - 2.0 ** (-5.0 - h)
            s = np.arange(128, dtype=np.float64)
            mrt[:, hh, :] = np.where(s_idx <= l_idx, (gamma ** (-s))[:, None] / 16.0, 0.0)
            tabs[:, hh] = gamma ** s
            tabs[:, 2 + hh] = gamma ** (128.0 - s) / 16.0
            tabs[:, 4 + hh] = gamma ** 128.0
        tabs[:, 6] = inv_freq.astype(np.float64)
        d["mrt"] = mrt.reshape(128, 256).astype(f32)
        d["tabs"] = tabs.astype(f32)
        per_g.append(d)
    in_maps = []
    for core in range(8):
        b, g = core // 2, core % 2
        m = dict(shared)
        m.update(per_g[g])
        m["x"] = np.ascontiguousarray(x[b, :S], dtype=f32)
        m["pos"] = np.ascontiguousarray(positions[b, :S].reshape(1, S), dtype=np.int32)
        m["c_fm"] = np.ascontiguousarray(c[b].reshape(8, 128).T, dtype=f32)
        in_maps.append(m)
    return in_maps


_NC_CACHE = {}


def run(S, debug=False, **inputs):
    inputs = {k: np.asarray(v) for k, v in inputs.items()}
    key = (S, debug)
    if key not in _NC_CACHE:
        _NC_CACHE[key] = build(S, debug)
    nc = _NC_CACHE[key]
    in_maps = _host_inputs(S, **inputs)
    res = run_bass_kernel_spmd(nc, in_maps, core_ids=list(range(8)))
    NT = S // TT
    out = np.zeros((4, S, D), np.float32)
    for core in range(8):
        b, r = core // 2, core % 2
        o = np.asarray(res.results[core]["out"]).reshape(NT, 256, D)
        out[b].reshape(NT, 2, 256, D)[:, r] = o
    return out, res


def kernel(**inputs):
    S = np.asarray(inputs["x"]).shape[1]
    out, _ = run(S, False, **inputs)
    return out
```

```python
import contextlib
import math
import numpy as np
import ml_dtypes
import concourse.bass as bass
import concourse.mybir as mybir
from concourse.bass_utils import run_bass_kernel_spmd

F32 = mybir.dt.float32
BF16 = mybir.dt.bfloat16
I32 = mybir.dt.int32
ALU = mybir.AluOpType
AF = mybir.ActivationFunctionType
AX = mybir.AxisListType

D = 1024
DEPTH = 2
EPS = 1e-6
TT = 512
NDMA = 24
EPOCH = 30000
LN16 = math.log(16.0)
TWO_PI = 2.0 * math.pi
CW1 = 6.28125
CW2 = TWO_PI - 6.28125


def _freeze(fn):
    import types
    if getattr(fn, '__closure__', None) is None:
        return fn
    cells = []
    for c in fn.__closure__:
        try:
            v = c.cell_contents
        except ValueError:
            cells.append(c)
            continue
        if isinstance(v, types.FunctionType):
            v = _freeze(v)
        cells.append(types.CellType(v))
    return types.FunctionType(fn.__code__, fn.__globals__, fn.__name__, fn.__defaults__, tuple(cells))


class T:
    def __init__(self, t):
        self.t = t
        self.lw = {}
        self.rd = {}

    def __getitem__(self, idx):
        return self.t[idx]


class TV(T):
    def __init__(self, base, ap):
        self.t = ap
        self.lw = base.lw
        self.rd = base.rd


class Sched:
    def __init__(self, nc, es):
        self.nc = nc
        self.es = es
        self.eng = {'pe': nc.tensor, 'act': nc.scalar, 'dve': nc.vector, 'pool': nc.gpsimd, 'sp': nc.sync}
        self.cnt = {e: 0 for e in self.eng}
        self.sems = {e: [] for e in self.eng}
        self.waited = {e: {} for e in self.eng}
        self.dma_sems = [es.enter_context(nc.semaphore("dma%d" % i)) for i in range(NDMA)]
        self.dma_val = [0] * NDMA
        self.dma_rr = 0
        self.cc_sem = es.enter_context(nc.semaphore("ccs"))
        self.cc_val = 0
        self.nwait = 0
        self.rec = []

    def _sem(self, e, ep):
        while len(self.sems[e]) <= ep:
            self.sems[e].append(self.es.enter_context(self.nc.semaphore("s_%s_%d" % (e, len(self.sems[e])))))
        return self.sems[e][ep]

    def _wait(self, e, key, val):
        w = self.waited[e]
        if w.get(key, 0) >= val:
            return
        w[key] = val
        self.nwait += 1
        if key[0] == 'E':
            ep = (val - 1) // EPOCH
            self.eng[e].wait_ge(self._sem(key[1], ep), val - ep * EPOCH)
        elif key[0] == 'D':
            self.eng[e].wait_ge(self.dma_sems[key[1]], val)
        else:
            self.eng[e].wait_ge(self.cc_sem, val)

    def op(self, e, reads, writes, fn, kind='c', dur=None):
        self.rec.append((e, list(reads), list(writes), _freeze(fn), kind, dur))

    def run(self, lat=0.25):
        import heapq
        rec = self.rec
        n = len(rec)
        DEF = {'pe': 0.35, 'act': 0.5, 'dve': 0.45, 'pool': 0.9, 'sp': 0.1}
        lastw = {}
        readers = {}
        preds = [None] * n
        succ = [[] for _ in range(n)]
        lastcc = None
        for i, (e, reads, writes, fn, kind, dur) in enumerate(rec):
            p = set()
            for t in reads:
                k = id(t.lw)
                if getattr(t, 'psum', False):
                    writes = writes + [t]
                    continue
                if k in lastw:
                    p.add(lastw[k])
            for t in writes:
                k = id(t.lw)
                if k in lastw:
                    p.add(lastw[k])
                for r in readers.get(k, ()):
                    p.add(r)
            if kind == 'cc':
                if lastcc is not None:
                    p.add(lastcc)
                lastcc = i
            p.discard(i)
            preds[i] = p
            for q in p:
                succ[q].append(i)
            for t in reads:
                if getattr(t, 'psum', False):
                    continue
                readers.setdefault(id(t.lw), []).append(i)
            for t in writes:
                k = id(t.lw)
                lastw[k] = i
                readers[k] = []
        def odur(i):
            e, _, _, _, kind, dur = rec[i]
            if kind == 'dma':
                return dur if dur is not None else 2.5
            if kind == 'cc':
                return 60.0
            return dur if dur is not None else DEF[e]
        bl = [0.0] * n
        for i in range(n - 1, -1, -1):
            m = 0.0
            for q in succ[i]:
                if bl[q] > m:
                    m = bl[q]
            bl[i] = odur(i) + lat + m
        indeg = [len(p) for p in preds]
        ready_t = [0.0] * n
        fin = [0.0] * n
        efree = {e: 0.0 for e in self.eng}
        pend = {e: [] for e in self.eng}
        avail = {e: [] for e in self.eng}
        for i in range(n):
            if indeg[i] == 0:
                heapq.heappush(pend[rec[i][0]], (0.0, i))
        order = []
        done = 0
        while done < n:
            best = None
            for e in self.eng:
                pe_, av = pend[e], avail[e]
                while pe_ and pe_[0][0] <= efree[e]:
                    j = heapq.heappop(pe_)[1]
                    heapq.heappush(av, (-bl[j], j))
                if av:
                    cand = (efree[e], av[0][1], e, True)
                elif pe_:
                    cand = (pe_[0][0], pe_[0][1], e, False)
                else:
                    continue
                if best is None or cand[:2] < best[:2]:
                    best = cand
            start, i, e, from_av = best
            if from_av:
                heapq.heappop(avail[e])
            else:
                heapq.heappop(pend[e])
            kind, dur = rec[i][4], rec[i][5]
            d = dur if dur is not None else DEF[e]
            if kind == 'dma':
                efree[e] = start + 0.1
                fin[i] = start + (dur if dur is not None else 2.5)
            elif kind == 'cc':
                efree[e] = start + 0.5
                fin[i] = start + 60.0
            else:
                efree[e] = start + d
                fin[i] = start + d
            order.append(i)
            done += 1
            for q in succ[i]:
                indeg[q] -= 1
                ready_t[q] = max(ready_t[q], fin[i] + (0.0 if (rec[q][0] == e and e == 'pe') else lat))
                if indeg[q] == 0:
                    heapq.heappush(pend[rec[q][0]], (ready_t[q], q))
        self.est_total = max(fin) if n else 0.0
        for i in order:
            e, reads, writes, fn, kind, dur = rec[i]
            self._emit(e, reads, writes, fn, kind)
        self.rec = []

    def _emit(self, e, reads, writes, fn, kind='c'):
        writes = list(writes) + [t for t in reads if getattr(t, 'psum', False)]
        deps = {}
        for t in reads:
            for k, v in t.lw.items():
                deps[k] = max(deps.get(k, 0), v)
        for t in writes:
            for k, v in t.lw.items():
                deps[k] = max(deps.get(k, 0), v)
            for k, v in t.rd.items():
                deps[k] = max(deps.get(k, 0), v)
        for k, v in deps.items():
            if k == ('E', 'pe') and e == 'pe':
                continue
            self._wait(e, k, v)
        if kind == 'dma':
            i = self.dma_rr
            self.dma_rr = (i + 1) % NDMA
            if self.dma_val[i] > 0:
                self._wait(e, ('D', i), self.dma_val[i])
            inst = fn()
            self.dma_val[i] += 16
            inst.then_inc(self.dma_sems[i], 16)
            tok = (('D', i), self.dma_val[i])
        elif kind == 'cc':
            inst = fn()
            self.cc_val += 1
            inst.then_inc(self.cc_sem, 1)
            tok = (('C',), self.cc_val)
        else:
            inst = fn()
            self.cnt[e] += 1
            n = self.cnt[e]
            ep = (n - 1) // EPOCH
            inst.then_inc(self._sem(e, ep), 1)
            tok = (('E', e), n)
        for t in reads:
            t.rd[tok[0]] = max(t.rd.get(tok[0], 0), tok[1])
        for t in writes:
            t.lw.clear()
            t.lw[tok[0]] = tok[1]
            t.rd.clear()
        return tok

    def finish(self, e):
        for src in self.eng:
            if self.cnt[src] > 0 and not (src == e == 'pe'):
                self._wait(e, ('E', src), self.cnt[src])
        for i in range(NDMA):
            if self.dma_val[i] > 0:
                self._wait(e, ('D', i), self.dma_val[i])
        if self.cc_val > 0:
            self._wait(e, ('C',), self.cc_val)


def build(S=8192, debug=False):
    NT = S // TT
    nc = bass.Bass("TRN2", target_bir_lowering=False)
    es = contextlib.ExitStack()

    def din(name, shape, dt=F32):
        return nc.dram_tensor(name, list(shape), dt, kind="ExternalInput")

    x_d = din("x", [S, D])
    pos_d = din("pos", [1, S], I32)
    c_d = din("c_fm", [128, 8])
    wada_d = din("wada", [DEPTH, D, 3 * D])
    badafm_d = din("bada_fm", [DEPTH, 128, 16])
    badag_d = din("bada_g", [DEPTH, 1, D])
    normg_d = din("normg_fm", [DEPTH, 128, 8])
    wf_d = din("wf", [DEPTH, D, 2048])
    wt_d = din("wt", [DEPTH, D, 2560])
    wg_d = din("wg", [DEPTH, D, 4])
    cw_d = din("cw", [DEPTH, 128, 32])
    cb_d = din("cb", [DEPTH, 128, 8])
    bg_d = din("bg", [DEPTH, 2, 2, 1])
    gn_d = din("gn_fm", [DEPTH, 128, 8])
    wo_d = din("wo", [DEPTH, D, D])
    fg_d = din("finalg", [1, D])
    identb_d = din("ident_b", [128, 128], BF16)
    identf_d = din("ident_f", [128, 128])
    mask_d = din("mask01", [128, 128])
    mrt_d = din("mrt", [128, 256])
    tabs_d = din("tabs", [128, 8])
    bmask_d = din("bmask", [2, 8])
    out_d = nc.dram_tensor("out", [NT * 256, D], F32, kind="ExternalOutput")
    part_d = [nc.dram_tensor("part%d" % l, [NT, TT, D], F32) for l in range(DEPTH)]
    xs0_d = nc.dram_tensor("xs0", [NT, TT, D], F32)
    xs1_d = nc.dram_tensor("xs1", [NT, 256, D], F32)
    dbg = {}
    if debug:
        dbg['hT'] = nc.dram_tensor("dbg_hT", [128, 8, TT], BF16, kind="ExternalOutput")
        dbg['qT'] = nc.dram_tensor("dbg_qT", [128, 4, TT], BF16, kind="ExternalOutput")
        dbg['kT'] = nc.dram_tensor("dbg_kT", [128, 4, TT], BF16, kind="ExternalOutput")
        dbg['rqT'] = nc.dram_tensor("dbg_rqT", [128, 4, TT], BF16, kind="ExternalOutput")
        dbg['rkT'] = nc.dram_tensor("dbg_rkT", [128, 4, TT], BF16, kind="ExternalOutput")
        dbg['ytok'] = nc.dram_tensor("dbg_ytok", [128, 1024], BF16, kind="ExternalOutput")
        dbg['tg'] = nc.dram_tensor("dbg_tg", [128, 16], F32, kind="ExternalOutput")
        dbg['hbuf'] = nc.dram_tensor("dbg_hbuf", [128, 4 * 257], F32, kind="ExternalOutput")
        dbg['xo'] = nc.dram_tensor("dbg_xo", [128, 1024], F32, kind="ExternalOutput")

    with es:
        K = Sched(nc, es)

        def sb(name, shape, dt=F32):
            return T(es.enter_context(nc.sbuf_tensor("sb_" + name, list(shape), dt)))

        def dr(t):
            return T(t)

        PE, ACT, DVE, POOL, SP = 'pe', 'act', 'dve', 'pool', 'sp'
        E = K.eng

        def dma(out_t, out_ap, in_t, in_ap, q=SP):
            reads = [in_t] if in_t is not None else []
            writes = [out_t] if out_t is not None else []
            return K.op(q, reads, writes, lambda: E[q].dma_start(out=out_ap, in_=in_ap), kind='dma')

        banks = [es.enter_context(nc.psum_tensor("psb%d" % i, [128, 512], F32)) for i in range(8)]
        class TP(T):
            psum = True
        _T = T
        psA = [TP(banks[0]), TP(banks[1])]
        ps_s = TP(banks[2])
        ps_num = TP(banks[3])
        ps_u = [TP(banks[4]), TP(banks[5])]
        ps_tr = [TP(banks[6]), TP(banks[6])]
        ps_yt = TP(banks[6])
        ps_sm = TP(banks[7])
        ps_kt = ps_sm
        bank6_bf = banks[6][:].bitcast(BF16)
        kt_bf = banks[7][:, 128:256].bitcast(BF16)
        abank = [0]

        def next_ps():
            abank[0] ^= 1
            return psA[abank[0]]

        ident_b = sb("ident_b", [128, 128], BF16)
        ident_f = sb("ident_f", [128, 128])
        mask01 = sb("mask01", [128, 128])
        mrt = sb("mrt", [128, 2, 128])
        tabs = sb("tabs", [128, 8])
        bmask = sb("bmask", [2, 2, 4])
        c_fm = sb("c_fm", [128, 8])
        c_act = sb("c_act", [128, 8])
        onesK = sb("onesK", [2, 128])
        mhalf = sb("mhalf", [128, 4])
        dma(ident_b, ident_b[:], None, identb_d[:, :])
        dma(ident_f, ident_f[:], None, identf_d[:, :])
        dma(mask01, mask01[:], None, mask_d[:, :])
        dma(mrt, mrt[:], None, mrt_d.ap().rearrange("p (h l) -> p h l", h=2))
        dma(tabs, tabs[:], None, tabs_d[:, :])
        dma(bmask, bmask[:], None, bmask_d.ap().rearrange("p (h c) -> p h c", h=2))
        dma(c_fm, c_fm[:], None, c_d[:, :])
        K.op(DVE, [], [onesK], lambda: E[DVE].memset(onesK[:], 1.0))
        K.op(DVE, [], [mhalf], lambda: E[DVE].memset(mhalf[:], -0.5))
        K.op(ACT, [c_fm], [c_act], lambda: E[ACT].activation(out=c_act[:], in_=c_fm[:], func=AF.Silu))
        gl = lambda j: tabs[:, j:j + 1]
        zt = lambda j: tabs[:, 2 + j:3 + j]
        g128 = lambda j: tabs[:, 4 + j:5 + j]
        invf = tabs[:, 6:7]

        wf_bf = sb("wf_bf", [128, 8, 2048], BF16)
        wt_bf = sb("wt_bf", [128, 8, 2560], BF16)
        wg_bf = sb("wg_bf", [128, 8, 4], BF16)
        wo_bf = sb("wo_bf", [128, 8, D], BF16)
        sti = [0]
        cw = sb("cw", [128, 8, 4])
        cb = sb("cb", [128, 8])
        bgi = sb("bgi", [2, 1])
        bgf = sb("bgf", [2, 1])
        nbf = sb("nbf", [2, 1])
        gn_fm = sb("gn_fm", [128, 8])
        normg = sb("normg", [128, 8])
        bada_fm = sb("bada_fm", [128, 16])
        gsc = sb("gsc", [128, 8])
        shf = sb("shf", [128, 8])
        gate_bc = sb("gate_bc", [128, D])

        xin0 = sb("xin0", [128, D])
        xin = [xin0, xin0]
        ss = sb("ss", [128, 1])
        rstd = sb("rstd", [128, 1])
        xn = sb("xn", [128, D], BF16)
        hTs = [sb("hT%d" % i, [128, 8, TT], BF16) for i in range(2)]
        f1 = sb("f1", [128, TT])
        f2 = sb("f2", [128, TT])
        f3 = sb("f3", [128, TT])
        cosT = sb("cosT", [128, TT])
        sinT = sb("sinT", [128, TT])
        carry = sb("carry", [128, 8, 3])
        qTs = [[sb("qT%d_%d" % (j, i), [128, TT], BF16) for i in range(4)] for j in range(2)]
        kTs = [[sb("kT%d_%d" % (j, i), [128, TT], BF16) for i in range(4)] for j in range(2)]
        rqTs = [[sb("rqT%d_%d" % (j, i), [128, TT], BF16) for i in range(4)] for j in range(2)]
        rkTs = [[sb("rkT%d_%d" % (j, i), [128, TT], BF16) for i in range(4)] for j in range(2)]
        li = sb("li", [2, TT])
        ge = sb("ge", [2, TT])
        bneg = sb("bneg", [2, TT])
        bcar = sb("bcar", [2, 1])
        cm = sb("cm", [2, 4])
        gmall = sb("gmall", [2, 5])
        dd = sb("dd", [2, 4])
        dexp = sb("dexp", [2, 2, 4])
        tgs = [sb("tg%d" % i, [128, 4, 4]) for i in range(2)]
        dec_bcs = [sb("dec_bc%d" % i, [128, 2, 4]) for i in range(2)]
        Vms = [sb("Vm%d" % i, [128, 2, 257], BF16) for i in range(2)]
        Vrs = [sb("Vr%d" % i, [128, 2, 256], BF16) for i in range(2)]
        og = sb("og", [128, 512])
        rzg = sb("rzg", [128, 512])
        swTs = [sb("swT%d" % i, [128, 128], BF16) for i in range(2)]
        kws = [sb("kw%d" % i, [128, 256], BF16) for i in range(2)]
        C32 = [sb("C32_%d" % i, [128, 2, 257]) for i in range(2)]
        Cbf = [sb("Cbf_%d" % i, [128, 2, 257], BF16) for i in range(2)]
        R32 = [sb("R32_%d" % i, [128, 2, 256]) for i in range(2)]
        Rbf = [sb("Rbf_%d" % i, [128, 2, 256], BF16) for i in range(2)]
        hb = [sb("hb%d" % i, [128, 256]) for i in range(4)]
        den2 = sb("den2", [128, 2])
        st6 = sb("st6", [128, 4, 6])
        mv = sb("mv", [128, 4, 2])
        rinv = sb("rinv", [128, 4])
        v4 = sb("v4", [128, 4])
        sd4 = sb("sd4", [128, 4])
        s4 = sb("s4", [128, 4])
        nb4 = sb("nb4", [128, 4])
        ytok = sb("ytok", [128, D], BF16)
        yT = sb("yT", [128, 8, 128], BF16)
        xres = sb("xres", [128, D])
        xo = sb("xo", [128, D])
        gb_bc = xo
        junk = xn
        posi_v = f2[:, :].bitcast(I32)
        cacc = f3
        xpre = sb("xpre", [128, TT + 3])
        f4 = T(xpre.t)
        f4 = xpre
        U = li
        wl = li
        fl = bneg
        c_rep_v = xres[:, :].rearrange("p (k c) -> p k c", k=8)
        stage_ada = [TV(hTs[i], hTs[i][:, :, :].rearrange("p k t -> p (k t)").bitcast(F32)) for i in range(2)]
        stage_w = [f1, f2, f3, xpre, cosT, sinT]
        for i in range(2):
            K.op(DVE, [], [Vms[i]], lambda i=i: E[DVE].memset(Vms[i][:], 1.0))
        K.op(DVE, [tabs], [rinv], lambda: E[DVE].tensor_copy(out=rinv[:, 2:4], in_=tabs[:, 0:2]))

        part_t = [[[dr(part_d[l]) for _ in range(4)] for _ in range(NT)] for l in range(DEPTH)]
        xs0_t = [dr(xs0_d) for _ in range(NT)]
        xs1_t = [dr(xs1_d) for _ in range(NT)]
        x_t = dr(x_d)
        out_t = dr(out_d)

        def dump(name, t, ap):
            if debug and name in dbg:
                dma(None, dbg[name].ap() if ap is None else ap, t, t[:])

        for l in range(DEPTH):
            dma(cw, cw[:], None, cw_d[l].rearrange("p (c k) -> p c k", c=8))
            dma(cb, cb[:], None, cb_d[l])
            dma(bgi, bgi[:], None, bg_d[l][0])
            dma(bgf, bgf[:], None, bg_d[l][1])
            dma(gn_fm, gn_fm[:], None, gn_d[l])
            dma(normg, normg[:], None, normg_d[l])
            dma(bada_fm, bada_fm[:], None, badafm_d[l])
            dma(gb_bc, gb_bc[:], None, badag_d[l].broadcast_to([128, D]))
            K.op(DVE, [c_act], [xres], lambda: E[DVE].tensor_copy(
                out=c_rep_v, in_=c_act[:].unsqueeze(2).broadcast_to([128, 8, 128])))
            K.op(DVE, [bgf], [nbf], lambda: E[DVE].tensor_scalar(
                out=nbf[:], in0=bgf[:], scalar1=-1.0, scalar2=None, op0=ALU.mult))
            for j in range(24):
                stg = stage_ada[j % 2]
                sv = stg[:, 0:1024].rearrange("p (k c) -> p k c", k=8)
                dma(stg, sv, None, wada_d[l][:, j * 128:(j + 1) * 128].rearrange("(k p) c -> p k c", p=128))
                if j < 16:
                    for ci in range(1):
                        col = j

                        def f(col=col, ci=ci, sv=sv):
                            for kt in range(8):
                                r = E[PE].matmul(banks[7][:, 64 + col:65 + col], lhsT=sv[:, kt, ci * 128:(ci + 1) * 128],
                                                 rhs=c_act[:, kt:kt + 1], start=(kt == 0), stop=(kt == 7))
                            return r
                        K.op(PE, [stg, c_act], [ps_sm], f)
                else:
                    pst = next_ps()

                    def f(pst=pst, sv=sv):
                        for kt in range(8):
                            r = E[PE].matmul(pst[:, 0:128], lhsT=c_rep_v[:, kt, :], rhs=sv[:, kt, :],
                                             start=(kt == 0), stop=(kt == 7))
                        return r
                    K.op(PE, [stg, xres], [pst], f)
                    K.op(DVE, [pst, gb_bc], [gate_bc], lambda pst=pst, j=j: E[DVE].tensor_tensor(
                        out=gate_bc[:, (j - 16) * 128:(j - 15) * 128], in0=pst[:, 0:128],
                        in1=gb_bc[:, (j - 16) * 128:(j - 15) * 128], op=ALU.add))
            K.op(DVE, [ps_sm, bada_fm], [shf], lambda: E[DVE].tensor_tensor(
                out=shf[:], in0=banks[7][:, 64:72], in1=bada_fm[:, 0:8], op=ALU.add))
            K.op(DVE, [ps_sm, bada_fm], [gsc], lambda: E[DVE].scalar_tensor_tensor(
                out=gsc[:], in0=banks[7][:, 72:80], scalar=1.0, in1=bada_fm[:, 8:16], op0=ALU.add, op1=ALU.add))
            K.op(DVE, [gsc, normg], [gsc], lambda: E[DVE].tensor_tensor(
                out=gsc[:], in0=gsc[:], in1=normg[:], op=ALU.mult))
            ci_ = 0
            for (wd, wb, ncol, scl) in ((wf_d, wf_bf, 2048, False), (wg_d, wg_bf, 4, False),
                                        (wt_d, wt_bf, 2560, False), (wo_d, wo_bf, D, True)):
                for kt in range(8):
                  for c0 in range(0, ncol, 512):
                    c1 = min(ncol, c0 + 512)
                    stg = stage_w[sti[0] % 6]
                    sti[0] += 1
                    dma(stg, stg[:, 0:c1 - c0], None, wd[l][kt * 128:(kt + 1) * 128, c0:c1])
                    ce = (ACT, DVE, POOL, ACT, DVE)[ci_ % 5]
                    ci_ += 1
                    if scl:
                        K.op(DVE, [stg, gn_fm], [wb], lambda stg=stg, wb=wb, kt=kt, c0=c0, c1=c1: E[DVE].tensor_scalar(
                            out=wb[:, kt, c0:c1], in0=stg[:, 0:c1 - c0], scalar1=gn_fm[:, kt:kt + 1], scalar2=None,
                            op0=ALU.mult), dur=0.6)
                    elif ce == POOL:
                        K.op(POOL, [stg], [wb], lambda stg=stg, wb=wb, kt=kt, c0=c0, c1=c1: E[POOL].tensor_copy(
                            out=wb[:, kt, c0:c1], in_=stg[:, 0:c1 - c0]), dur=1.9)
                    elif ce == DVE:
                        K.op(DVE, [stg], [wb], lambda stg=stg, wb=wb, kt=kt, c0=c0, c1=c1: E[DVE].tensor_copy(
                            out=wb[:, kt, c0:c1], in_=stg[:, 0:c1 - c0]), dur=0.6)
                    else:
                        K.op(ACT, [stg], [wb], lambda stg=stg, wb=wb, kt=kt, c0=c0, c1=c1: E[ACT].activation(
                            out=wb[:, kt, c0:c1], in_=stg[:, 0:c1 - c0], func=AF.Copy), dur=0.65)
            for i in range(2):
                K.op(POOL, [], [C32[i]], lambda i=i: E[POOL].memset(C32[i][:], 0.0))
                K.op(POOL, [], [Cbf[i]], lambda i=i: E[POOL].memset(Cbf[i][:], 0.0))
                K.op(POOL, [], [R32[i]], lambda i=i: E[POOL].memset(R32[i][:], 0.0))
                K.op(POOL, [], [Rbf[i]], lambda i=i: E[POOL].memset(Rbf[i][:], 0.0))
            K.op(POOL, [], [carry], lambda: E[POOL].memset(carry[:], 0.0))
            K.op(POOL, [], [bcar], lambda: E[POOL].memset(bcar[:], 0.0))
            K.op(POOL, [], [gmall], lambda: E[POOL].memset(gmall[:], 0.0))

            def gen_ABC(ti):
                par = ti % 2
                hT, qT, kT, rqT, rkT, tg, dec_bc = hTs[par], qTs[par], kTs[par], rqTs[par], rkTs[par], tgs[par], dec_bcs[par]
                for sub in range(4):
                    xi = xin[sub % 2]
                    if l == 0:
                        dma(xi, xi[:], x_t, x_d[ti * TT + sub * 128: ti * TT + (sub + 1) * 128, :])
                    else:
                        dma(xi, xi[:], xs0_t[ti], xs0_d[ti][sub * 128:(sub + 1) * 128, :])
                    K.op(ACT, [xi], [junk, ss], lambda xi=xi: E[ACT].activation(
                        out=junk[:], in_=xi[:], func=AF.Square, accum_out=ss[:]), dur=1.0)
                    K.op(DVE, [ss], [ss], lambda: E[DVE].tensor_scalar(
                        out=ss[:], in0=ss[:], scalar1=1.0 / D, scalar2=EPS, op0=ALU.mult, op1=ALU.add))
                    K.op(POOL, [ss, mhalf], [rstd], lambda: E[POOL].tensor_tensor(out=rstd[:], in0=ss[:], in1=mhalf[:, 0:1], op=ALU.pow))
                    K.op(DVE, [xi, rstd], [xn], lambda xi=xi, sub=sub: E[DVE].tensor_scalar(
                        out=xn[:], in0=xi[:], scalar1=rstd[:, 0:1], scalar2=None, op0=ALU.mult), dur=1.0)

                    def f():
                        for kt in range(8):
                            r = E[PE].transpose(out=bank6_bf[:, kt * 128:(kt + 1) * 128],
                                                in_=xn[:, kt * 128:(kt + 1) * 128], identity=ident_b[:])
                        return r
                    K.op(PE, [xn, ident_b], [ps_tr[0], ps_tr[1]], f, dur=1.0)
                    for kt in range(8):
                        K.op(ACT, [ps_tr[0], ps_tr[1], gsc, shf], [hT], lambda kt=kt, sub=sub: E[ACT].activation(
                            out=hT[:, kt, sub * 128:(sub + 1) * 128], in_=bank6_bf[:, kt * 128:(kt + 1) * 128],
                            func=AF.Identity, scale=gsc[:, kt:kt + 1], bias=shf[:, kt:kt + 1]))
                    yield
                if l == 0 and ti == 0:
                    dump('hT', hT, None)
                dma(f2, posi_v, None, pos_d[0:1, ti * TT:(ti + 1) * TT].broadcast_to([128, TT]))
                K.op(DVE, [f2], [f1], lambda: E[DVE].tensor_copy(out=f1[:], in_=posi_v))
                K.op(DVE, [f1, tabs], [f1], lambda: E[DVE].tensor_scalar(
                    out=f1[:], in0=f1[:], scalar1=invf, scalar2=None, op0=ALU.mult))
                K.op(DVE, [f1], [f2], lambda: E[DVE].tensor_scalar(
                    out=posi_v, in0=f1[:], scalar1=1.0 / TWO_PI, scalar2=None, op0=ALU.mult))
                K.op(DVE, [f2], [f3], lambda: E[DVE].tensor_copy(out=f3[:], in_=posi_v))
                K.op(DVE, [f3, f1], [f1], lambda: E[DVE].scalar_tensor_tensor(
                    out=f1[:], in0=f3[:], scalar=-CW1, in1=f1[:], op0=ALU.mult, op1=ALU.add))
                K.op(DVE, [f3, f1], [f1], lambda: E[DVE].scalar_tensor_tensor(
                    out=f1[:], in0=f3[:], scalar=-CW2, in1=f1[:], op0=ALU.mult, op1=ALU.add))
                K.op(DVE, [f1], [f3], lambda: E[DVE].tensor_scalar(
                    out=f3[:], in0=f1[:], scalar1=math.pi, scalar2=-TWO_PI, op0=ALU.is_gt, op1=ALU.mult))
                K.op(DVE, [f1, f3], [f3], lambda: E[DVE].tensor_tensor(out=f3[:], in0=f3[:], in1=f1[:], op=ALU.add))
                K.op(DVE, [f3], [f3], lambda: E[DVE].tensor_scalar(
                    out=f3[:], in0=f3[:], scalar1=-math.pi, scalar2=math.pi, op0=ALU.max, op1=ALU.min))
                K.op(ACT, [f3], [sinT], lambda: E[ACT].activation(out=sinT[:], in_=f3[:], func=AF.Sin))
                K.op(DVE, [f1], [f2], lambda: E[DVE].tensor_scalar(
                    out=f2[:], in0=f1[:], scalar1=math.pi / 2, scalar2=-TWO_PI, op0=ALU.is_gt, op1=ALU.mult))
                K.op(DVE, [f1, f2], [f2], lambda: E[DVE].scalar_tensor_tensor(
                    out=f2[:], in0=f1[:], scalar=math.pi / 2, in1=f2[:], op0=ALU.add, op1=ALU.add))
                K.op(DVE, [f2], [f2], lambda: E[DVE].tensor_scalar(
                    out=f2[:], in0=f2[:], scalar1=-math.pi, scalar2=math.pi, op0=ALU.max, op1=ALU.min))
                K.op(ACT, [f2], [cosT], lambda: E[ACT].activation(out=cosT[:], in_=f2[:], func=AF.Sin))
                yield

                def fm_mm(pst, ct):
                    def f():
                        for kt in range(8):
                            r = E[PE].matmul(pst[:, :], lhsT=wf_bf[:, kt, ct * 128:(ct + 1) * 128], rhs=hT[:, kt, :],
                                             start=(kt == 0), stop=(kt == 7))
                        return r
                    K.op(PE, [wf_bf, hT], [pst], f, dur=1.9)

                for ct in range(8):
                    pst = next_ps()
                    fm_mm(pst, ct)
                    dst = qT[ct] if ct < 4 else kT[ct - 4]
                    K.op(POOL, [carry], [xpre], lambda ct=ct: E[POOL].tensor_copy(out=xpre[:, 0:3], in_=carry[:, ct, :]))
                    K.op(ACT, [pst], [xpre], lambda pst=pst: E[ACT].activation(
                        out=xpre[:, 3:TT + 3], in_=pst[:, :], func=AF.Copy))
                    K.op(POOL, [xpre], [carry], lambda ct=ct: E[POOL].tensor_copy(
                        out=carry[:, ct, :], in_=xpre[:, TT:TT + 3]))
                    K.op(DVE, [xpre, cw, cb], [cacc], lambda ct=ct: E[DVE].tensor_scalar(
                        out=cacc[:], in0=xpre[:, 3:TT + 3], scalar1=cw[:, ct, 3:4], scalar2=cb[:, ct:ct + 1],
                        op0=ALU.mult, op1=ALU.add))
                    for tap in (2, 1, 0):
                        K.op(DVE, [xpre, cw, cacc], [cacc], lambda ct=ct, tap=tap: E[DVE].scalar_tensor_tensor(
                            out=cacc[:], in0=xpre[:, tap:tap + TT], scalar=cw[:, ct, tap:tap + 1], in1=cacc[:],
                            op0=ALU.mult, op1=ALU.add))
                    K.op(ACT, [cacc], [dst], lambda dst=dst: E[ACT].activation(out=dst[:], in_=cacc[:], func=AF.Silu))
                    yield
                for p in range(4):
                    p1 = next_ps()
                    fm_mm(p1, 8 + 2 * p)
                    p2 = next_ps()
                    fm_mm(p2, 9 + 2 * p)
                    d1 = (rqT if p < 2 else rkT)[2 * (p % 2)]
                    d2 = (rqT if p < 2 else rkT)[2 * (p % 2) + 1]
                    K.op(DVE, [p1, cosT], [f3], lambda p1=p1: E[DVE].tensor_tensor(out=f3[:], in0=p1[:, :], in1=cosT[:], op=ALU.mult))
                    K.op(DVE, [p1, sinT], [f2], lambda p1=p1: E[DVE].tensor_tensor(out=f2[:], in0=p1[:, :], in1=sinT[:], op=ALU.mult))
                    K.op(DVE, [p2, sinT], [f4], lambda p2=p2: E[DVE].tensor_tensor(out=f4[:, 0:TT], in0=p2[:, :], in1=sinT[:], op=ALU.mult))
                    K.op(DVE, [p2, cosT], [f1], lambda p2=p2: E[DVE].tensor_tensor(out=f1[:], in0=p2[:, :], in1=cosT[:], op=ALU.mult))
                    K.op(POOL, [f3, f4], [d1], lambda d1=d1: E[POOL].tensor_tensor(out=d1[:], in0=f3[:], in1=f4[:, 0:TT], op=ALU.subtract), dur=1.4)
                    K.op(POOL, [f1, f2], [d2], lambda d2=d2: E[POOL].tensor_tensor(out=d2[:], in0=f1[:], in1=f2[:], op=ALU.add), dur=1.4)
                    yield
                pgi = next_ps()

                def f(pgi=pgi):
                    for kt in range(8):
                        r = E[PE].matmul(pgi[0:2, :], lhsT=wg_bf[:, kt, 0:2], rhs=hT[:, kt, :], start=(kt == 0), stop=(kt == 7))
                    return r
                K.op(PE, [wg_bf, hT], [pgi], f, dur=1.8)
                K.op(ACT, [pgi, bgi], [li], lambda pgi=pgi: E[ACT].activation(
                    out=li[:], in_=pgi[0:2, :], func=AF.Identity, bias=bgi[:, 0:1], scale=1.0))
                pgf = next_ps()

                def f(pgf=pgf):
                    for kt in range(8):
                        r = E[PE].matmul(pgf[0:2, :], lhsT=wg_bf[:, kt, 2:4], rhs=hT[:, kt, :], start=(kt == 0), stop=(kt == 7))
                    return r
                K.op(PE, [wg_bf, hT], [pgf], f, dur=1.8)
                K.op(ACT, [pgf, nbf], [ge], lambda pgf=pgf: E[ACT].activation(
                    out=ge[:], in_=pgf[0:2, :], func=AF.Exp, bias=nbf[:, 0:1], scale=-1.0))
                K.op(ACT, [ge], [ge], lambda: E[ACT].activation(out=ge[:], in_=ge[:], func=AF.Ln, bias=1.0, scale=1.0))
                K.op(DVE, [ge, bcar], [bneg], lambda: E[DVE].tensor_tensor_scan(
                    out=bneg[:], data0=ge[:], data1=ge[:], initial=bcar[:, 0:1], op0=ALU.add, op1=ALU.max))
                K.op(DVE, [bneg], [bcar], lambda: E[DVE].tensor_copy(out=bcar[:], in_=bneg[:, TT - 1:TT]))
                K.op(DVE, [li, bneg], [U], lambda: E[DVE].tensor_tensor(out=U[:], in0=li[:], in1=bneg[:], op=ALU.add))
                K.op(DVE, [U], [cm], lambda: E[DVE].tensor_reduce(
                    out=cm[:], in_=U[:].rearrange("p (c t) -> p c t", c=4), axis=AX.X, op=ALU.max))
                K.op(DVE, [gmall], [gmall], lambda: E[DVE].tensor_copy(out=gmall[:, 0:1], in_=gmall[:, 4:5]))
                K.op(DVE, [cm, gmall], [gmall], lambda: E[DVE].tensor_tensor_scan(
                    out=gmall[:, 1:5], data0=cm[:], data1=cm[:], initial=gmall[:, 0:1], op0=ALU.max, op1=ALU.max))
                K.op(DVE, [gmall], [dd], lambda: E[DVE].tensor_tensor(
                    out=dd[:], in0=gmall[:, 0:4], in1=gmall[:, 1:5], op=ALU.subtract))
                K.op(ACT, [dd], [dd], lambda: E[ACT].activation(out=dd[:], in_=dd[:], func=AF.Exp))
                K.op(DVE, [dd, bmask], [dexp], lambda: E[DVE].tensor_tensor(
                    out=dexp[:], in0=dd[:].unsqueeze(1).broadcast_to([2, 2, 4]), in1=bmask[:], op=ALU.mult))
                K.op(PE, [onesK, dexp], [ps_sm], lambda: E[PE].matmul(
                    banks[7][:, 0:8], lhsT=onesK[:], rhs=dexp[:].rearrange("p h c -> p (h c)"), start=True, stop=True))
                K.op(DVE, [ps_sm], [dec_bc], lambda: E[DVE].tensor_copy(
                    out=dec_bc[:].rearrange("p h c -> p (h c)"), in_=banks[7][:, 0:8]))
                gmb = gmall[:, 1:5].unsqueeze(2).broadcast_to([2, 4, 128])
                K.op(DVE, [U, gmall], [wl], lambda: E[DVE].tensor_tensor(
                    out=wl[:].rearrange("p (c t) -> p c t", c=4), in0=U[:].rearrange("p (c t) -> p c t", c=4),
                    in1=gmb, op=ALU.subtract))
                K.op(ACT, [wl], [wl], lambda: E[ACT].activation(out=wl[:], in_=wl[:], func=AF.Exp, bias=-LN16, scale=1.0))
                K.op(DVE, [bneg, gmall], [fl], lambda: E[DVE].tensor_tensor(
                    out=fl[:].rearrange("p (c t) -> p c t", c=4), in0=bneg[:].rearrange("p (c t) -> p c t", c=4),
                    in1=gmb, op=ALU.subtract))
                K.op(ACT, [fl], [fl], lambda: E[ACT].activation(out=fl[:], in_=fl[:], func=AF.Exp))

                def f():
                    for c in range(4):
                        E[PE].transpose(out=banks[7][:, 16 + c * 4:18 + c * 4], in_=wl[:, c * 128:(c + 1) * 128],
                                        identity=ident_f[0:2, 0:2])
                        r = E[PE].transpose(out=banks[7][:, 18 + c * 4:20 + c * 4], in_=fl[:, c * 128:(c + 1) * 128],
                                            identity=ident_f[0:2, 0:2])
                    return r
                K.op(PE, [wl, fl, ident_f], [ps_sm], f)
                K.op(DVE, [ps_sm], [tg], lambda: E[DVE].tensor_copy(
                    out=tg[:].rearrange("p c k -> p (c k)"), in_=banks[7][:, 16:32]))
                yield
                if l == 0 and ti == 0:
                    dump('tg', tg, None)
                    for i in range(4):
                        dump('qT', qT[i], dbg['qT'][:, i, :] if debug else None)
                        dump('kT', kT[i], dbg['kT'][:, i, :] if debug else None)
                        dump('rqT', rqT[i], dbg['rqT'][:, i, :] if debug else None)
                        dump('rkT', rkT[i], dbg['rkT'][:, i, :] if debug else None)

            def gen_D(ti):
                par = ti % 2
                hT, qT, kT, rqT, rkT, tg, dec_bc = hTs[par], qTs[par], kTs[par], rqTs[par], rkTs[par], tgs[par], dec_bcs[par]

                def tm_unit(c, gi):
                    cs = slice(c * 128, (c + 1) * 128)
                    Vm, Vr = Vms[c % 2], Vrs[c % 2]
                    pst = next_ps()

                    def f():
                        for kt in range(8):
                            r = E[PE].matmul(pst[:, :], lhsT=hT[:, kt, cs], rhs=wt_bf[:, kt, gi * 512:(gi + 1) * 512],
                                             start=(kt == 0), stop=(kt == 7))
                        return r
                    K.op(PE, [wt_bf, hT], [pst], f, dur=1.9)
                    if gi == 0:
                        K.op(ACT, [pst], [Vm], lambda: E[ACT].activation(
                            out=Vm[:, :, 0:256], in_=pst[:, :].rearrange("p (h d) -> p h d", h=2), func=AF.Copy))
                    elif gi == 1:
                        K.op(ACT, [pst], [og], lambda: E[ACT].activation(out=og[:], in_=pst[:, :], func=AF.Sigmoid))
                    elif gi == 2:
                        K.op(ACT, [pst], [pst], lambda: E[ACT].activation(out=pst[:, :], in_=pst[:, :], func=AF.Silu))
                        K.op(DVE, [og, pst], [og], lambda: E[DVE].tensor_tensor(out=og[:], in0=og[:], in1=pst[:, :], op=ALU.mult))
                    elif gi == 3:
                        K.op(ACT, [pst], [Vr], lambda: E[ACT].activation(
                            out=Vr[:].rearrange("p h d -> p (h d)"), in_=pst[:, :], func=AF.Copy))
                    else:
                        K.op(ACT, [pst], [rzg], lambda: E[ACT].activation(out=rzg[:], in_=pst[:, :], func=AF.Silu))

                tm_unit(0, 0)
                tm_unit(0, 3)
                yield
                for c in range(4):
                    cs = slice(c * 128, (c + 1) * 128)
                    Vm, Vr = Vms[c % 2], Vrs[c % 2]
                    fill = [1, 2, 4]
                    for hh in range(2):
                        wcol = tg[:, c, hh:hh + 1]
                        dcol = dec_bc[:, hh, c:c + 1]
                        swm, swr, kwm, kwr = swTs[0], swTs[1], kws[0], kws[1]
                        K.op(ACT, [C32[hh], dec_bc], [Cbf[hh]], lambda hh=hh, dcol=dcol: E[ACT].activation(
                            out=Cbf[hh][:].rearrange("p a b -> p (a b)"), in_=C32[hh][:].rearrange("p a b -> p (a b)"),
                            func=AF.Identity, scale=dcol, bias=0.0))
                        def f(hh=hh):
                            for dt in range(2):
                                r = E[PE].matmul(banks[2][:, 0:128], lhsT=kT[2 * hh + dt][:, cs], rhs=qT[2 * hh + dt][:, cs],
                                                 start=(dt == 0), stop=(dt == 1))
                            return r
                        K.op(PE, [kT[2 * hh], kT[2 * hh + 1], qT[2 * hh], qT[2 * hh + 1]], [ps_s], f)
                        K.op(DVE, [ps_s, tg, mask01], [swm], lambda wcol=wcol: E[DVE].scalar_tensor_tensor(
                            out=swm[:], in0=banks[2][:, 0:128], scalar=wcol, in1=mask01[:], op0=ALU.mult, op1=ALU.mult))
                        def f(hh=hh):
                            for dt in range(2):
                                r = E[PE].transpose(out=kt_bf[:, dt * 128:(dt + 1) * 128], in_=kT[2 * hh + dt][:, cs],
                                                    identity=ident_b[:])
                            return r
                        K.op(PE, [kT[2 * hh], kT[2 * hh + 1], ident_b], [ps_kt], f)
                        K.op(ACT, [ps_kt, tg], [kwm], lambda wcol=wcol: E[ACT].activation(
                            out=kwm[:], in_=kt_bf, func=AF.Identity, scale=wcol, bias=0.0))
                        def f(hh=hh):
                            for dt in range(2):
                                r = E[PE].matmul(banks[2][:, 0:128], lhsT=rkT[2 * hh + dt][:, cs], rhs=rqT[2 * hh + dt][:, cs],
                                                 start=(dt == 0), stop=(dt == 1))
                            return r
                        K.op(PE, [rkT[2 * hh], rkT[2 * hh + 1], rqT[2 * hh], rqT[2 * hh + 1]], [ps_s], f)
                        K.op(DVE, [ps_s, mrt], [swr], lambda hh=hh: E[DVE].tensor_tensor(
                            out=swr[:], in0=banks[2][:, 0:128], in1=mrt[:, hh, :], op=ALU.mult))
                        def f(hh=hh):
                            for dt in range(2):
                                r = E[PE].transpose(out=kt_bf[:, dt * 128:(dt + 1) * 128], in_=rkT[2 * hh + dt][:, cs],
                                                    identity=ident_b[:])
                            return r
                        K.op(PE, [rkT[2 * hh], rkT[2 * hh + 1], ident_b], [ps_kt], f)
                        K.op(ACT, [ps_kt, tabs], [kwr], lambda hh=hh: E[ACT].activation(
                            out=kwr[:], in_=kt_bf, func=AF.Identity, scale=zt(hh), bias=0.0))
                        tm_unit(c, fill.pop(0))
                        def f(hh=hh):
                            E[PE].matmul(banks[3][:, 0:257], lhsT=swm[:], rhs=Vm[:, hh, :], start=True, stop=False)
                            E[PE].matmul(banks[3][:, 0:257], lhsT=qT[2 * hh][:, cs], rhs=Cbf[hh][:, 0, :], start=False, stop=False)
                            return E[PE].matmul(banks[3][:, 0:257], lhsT=qT[2 * hh + 1][:, cs], rhs=Cbf[hh][:, 1, :], start=False, stop=True)
                        K.op(PE, [swm, Vm, qT[2 * hh], qT[2 * hh + 1], Cbf[hh]], [ps_num], f, dur=0.5)
                        K.op(ACT, [ps_num], [hb[hh]], lambda hh=hh: E[ACT].activation(
                            out=hb[hh][:], in_=banks[3][:, 0:256], func=AF.Copy))
                        K.op(ACT, [ps_num], [den2], lambda hh=hh: E[ACT].activation(
                            out=den2[:, hh:hh + 1], in_=banks[3][:, 256:257], func=AF.Copy))
                        for dt in range(2):
                            K.op(PE, [kwm, Vm], [ps_u[dt]], lambda hh=hh, dt=dt: E[PE].matmul(
                                banks[4 + dt][:, 0:257], lhsT=kwm[:, dt * 128:(dt + 1) * 128], rhs=Vm[:, hh, :],
                                start=True, stop=True))
                            K.op(DVE, [C32[hh], dec_bc, ps_u[dt]], [C32[hh]], lambda hh=hh, dt=dt, dcol=dcol: E[DVE].scalar_tensor_tensor(
                                out=C32[hh][:, dt, :], in0=C32[hh][:, dt, :], scalar=dcol, in1=banks[4 + dt][:, 0:257],
                                op0=ALU.mult, op1=ALU.add), dur=0.75)
                        def f(hh=hh):
                            E[PE].matmul(banks[3][:, 0:256], lhsT=swr[:], rhs=Vr[:, hh, :], start=True, stop=False)
                            E[PE].matmul(banks[3][:, 0:256], lhsT=rqT[2 * hh][:, cs], rhs=Rbf[hh][:, 0, :], start=False, stop=False)
                            return E[PE].matmul(banks[3][:, 0:256], lhsT=rqT[2 * hh + 1][:, cs], rhs=Rbf[hh][:, 1, :],
                                                start=False, stop=True)
                        K.op(PE, [swr, Vr, rqT[2 * hh], rqT[2 * hh + 1], Rbf[hh]], [ps_num], f, dur=0.5)
                        K.op(ACT, [ps_num], [hb[2 + hh]], lambda hh=hh: E[ACT].activation(
                            out=hb[2 + hh][:], in_=banks[3][:, 0:256], func=AF.Copy))
                        if fill and hh == 1:
                            tm_unit(c, fill.pop(0))
                        for dt in range(2):
                            K.op(PE, [kwr, Vr], [ps_u[dt]], lambda hh=hh, dt=dt: E[PE].matmul(
                                banks[4 + dt][:, 0:256], lhsT=kwr[:, dt * 128:(dt + 1) * 128], rhs=Vr[:, hh, :],
                                start=True, stop=True))
                            K.op(DVE, [R32[hh], tabs, ps_u[dt]], [R32[hh]], lambda hh=hh, dt=dt: E[DVE].scalar_tensor_tensor(
                                out=R32[hh][:, dt, :], in0=R32[hh][:, dt, :], scalar=g128(hh), in1=banks[4 + dt][:, 0:256],
                                op0=ALU.mult, op1=ALU.add))
                        K.op(POOL, [R32[hh]], [Rbf[hh]], lambda hh=hh: E[POOL].tensor_copy(out=Rbf[hh][:], in_=R32[hh][:]), dur=1.9)
                        for j in (hh, 2 + hh):
                            K.op(DVE, [hb[j]], [st6], lambda j=j: E[DVE].bn_stats(out=st6[:, j, :], in_=hb[j][:]))
                            K.op(DVE, [st6], [mv], lambda j=j: E[DVE].bn_aggr(out=mv[:, j, :], in_=st6[:, j, :]))
                        yield
                        yield

                    if c < 3:
                        tm_unit(c + 1, 0)
                        tm_unit(c + 1, 3)
                    yield
                    yield
                    K.op(DVE, [den2], [v4], lambda: E[DVE].tensor_scalar(
                        out=v4[:, 0:2], in0=den2[:], scalar1=-1.0, scalar2=None, op0=ALU.mult))
                    K.op(DVE, [den2, v4], [v4], lambda: E[DVE].tensor_tensor(
                        out=v4[:, 0:2], in0=v4[:, 0:2], in1=den2[:], op=ALU.max))
                    K.op(DVE, [v4, tg], [v4], lambda c=c: E[DVE].tensor_tensor(
                        out=v4[:, 0:2], in0=v4[:, 0:2], in1=tg[:, c, 2:4], op=ALU.max))
                    K.op(DVE, [v4], [rinv], lambda: E[DVE].reciprocal(out=rinv[:, 0:2], in_=v4[:, 0:2]))
                    K.op(DVE, [rinv], [v4], lambda: E[DVE].tensor_tensor(out=v4[:], in0=rinv[:], in1=rinv[:], op=ALU.mult))
                    K.op(DVE, [v4, mv], [v4], lambda: E[DVE].tensor_tensor(out=v4[:], in0=v4[:], in1=mv[:, :, 1], op=ALU.mult))
                    K.op(DVE, [v4], [v4], lambda: E[DVE].tensor_scalar(out=v4[:], in0=v4[:], scalar1=EPS, scalar2=None, op0=ALU.add))
                    K.op(POOL, [v4, mhalf], [s4], lambda: E[POOL].tensor_tensor(out=s4[:], in0=v4[:], in1=mhalf[:], op=ALU.pow))
                    K.op(DVE, [s4, rinv], [s4], lambda: E[DVE].tensor_tensor(out=s4[:], in0=s4[:], in1=rinv[:], op=ALU.mult))
                    K.op(DVE, [mv, s4], [nb4], lambda: E[DVE].scalar_tensor_tensor(
                        out=nb4[:], in0=mv[:, :, 0], scalar=-1.0, in1=s4[:], op0=ALU.mult, op1=ALU.mult))
                    for j in range(4):
                        K.op(ACT, [hb[j], s4, nb4], [hb[j]], lambda j=j: E[ACT].activation(
                            out=hb[j][:], in_=hb[j][:], func=AF.Identity, scale=s4[:, j:j + 1], bias=nb4[:, j:j + 1]))
                    for j in range(4):
                        gsrc = og if j < 2 else rzg
                        K.op(DVE, [hb[j], gsrc], [ytok], lambda j=j, gsrc=gsrc: E[DVE].tensor_tensor(
                            out=ytok[:, j * 256:(j + 1) * 256], in0=hb[j][:], in1=gsrc[:, (j % 2) * 256:(j % 2 + 1) * 256],
                            op=ALU.mult))

                    def f():
                        for ft in range(8):
                            r = E[PE].transpose(out=bank6_bf[:, ft * 128:(ft + 1) * 128], in_=ytok[:, ft * 128:(ft + 1) * 128],
                                                identity=ident_b[:])
                        return r
                    K.op(PE, [ytok, ident_b], [ps_tr[0], ps_tr[1]], f, dur=1.0)
                    K.op(ACT, [ps_tr[0], ps_tr[1]], [yT], lambda: E[ACT].activation(
                        out=yT[:].rearrange("p f t -> p (f t)"), in_=bank6_bf, func=AF.Copy), dur=1.1)
                    if l == 0:
                        dma(xres, xres[:], x_t, x_d[ti * TT + c * 128: ti * TT + (c + 1) * 128, :])
                    else:
                        dma(xres, xres[:], xs0_t[ti], xs0_d[ti][c * 128:(c + 1) * 128, :])
                    for half in range(2):
                        pst = next_ps()

                        def f(pst=pst, half=half):
                            for ft in range(8):
                                r = E[PE].matmul(pst[:, :], lhsT=yT[:, ft, :], rhs=wo_bf[:, ft, half * 512:(half + 1) * 512],
                                                 start=(ft == 0), stop=(ft == 7))
                            return r
                        K.op(PE, [yT, wo_bf], [pst], f, dur=1.9)
                        hs = slice(half * 512, (half + 1) * 512)
                        K.op(DVE, [pst, gate_bc], [xo], lambda pst=pst, hs=hs: E[DVE].tensor_tensor(
                            out=xo[:, hs], in0=pst[:, :], in1=gate_bc[:, hs], op=ALU.mult))
                        K.op(DVE, [xres, xo], [xo], lambda hs=hs: E[DVE].scalar_tensor_tensor(
                            out=xo[:, hs], in0=xres[:, hs], scalar=0.5, in1=xo[:, hs], op0=ALU.mult, op1=ALU.add))
                    dma(part_t[l][ti][c], part_d[l][ti][c * 128:(c + 1) * 128, :], xo, xo[:])
                    yield
                if l == 0:
                    K.op(POOL, part_t[l][ti], [xs0_t[ti]], lambda ti=ti: E[POOL].collective_compute(
                        "AllReduce", ALU.add, replica_groups=[[0, 1], [2, 3], [4, 5], [6, 7]],
                        ins=[part_d[0][ti].opt()], outs=[xs0_d[ti].opt()]), kind='cc')
                else:
                    K.op(POOL, part_t[l][ti], [xs1_t[ti]], lambda ti=ti: E[POOL].collective_compute(
                        "ReduceScatter", ALU.add, replica_groups=[[0, 1], [2, 3], [4, 5], [6, 7]],
                        ins=[part_d[1][ti].opt()], outs=[xs1_d[ti].opt()]), kind='cc')


            for _ in gen_ABC(0):
                pass
            for ti in range(NT):
                gd = gen_D(ti)
                ga = gen_ABC(ti + 1) if ti + 1 < NT else None
                nD, nA = 30, 18
                dcnt = acnt = 0
                while True:
                    try:
                        next(gd)
                    except StopIteration:
                        break
                    dcnt += 1
                    while ga is not None and acnt * nD < dcnt * nA:
                        try:
                            next(ga)
                            acnt += 1
                        except StopIteration:
                            ga = None
                if ga is not None:
                    for _ in ga:
                        pass

        fg_bc = gate_bc
        dma(fg_bc, fg_bc[:], None, fg_d.ap().broadcast_to([128, D]))
        fin_in = [xin0, xres]
        fin_out = [xo] + [TV(hTs[i], hTs[i][:, :, :].rearrange("p k t -> p (k t)").bitcast(F32)) for i in range(2)]
        fin_junk = [xn, ytok]
        fss = [sb("fss%d" % i, [128, 1]) for i in range(2)]
        frs = [sb("frs%d" % i, [128, 1]) for i in range(2)]
        cnt = 0
        for ti in range(NT):
            for sub in range(2):
                xi = fin_in[cnt % 2]
                xoo = fin_out[cnt % 3]
                jk = fin_junk[cnt % 2]
                ss_, rs_ = fss[cnt % 2], frs[cnt % 2]
                cnt += 1
                dma(xi, xi[:], xs1_t[ti], xs1_d[ti][sub * 128:(sub + 1) * 128, :])
                K.op(ACT, [xi], [jk, ss_], lambda xi=xi, jk=jk, ss_=ss_: E[ACT].activation(
                    out=jk[:], in_=xi[:], func=AF.Square, accum_out=ss_[:]), dur=1.0)
                K.op(DVE, [ss_], [ss_], lambda ss_=ss_: E[DVE].tensor_scalar(
                    out=ss_[:], in0=ss_[:], scalar1=1.0 / D, scalar2=EPS, op0=ALU.mult, op1=ALU.add), dur=0.15)
                K.op(POOL, [ss_, mhalf], [rs_], lambda ss_=ss_, rs_=rs_: E[POOL].tensor_tensor(
                    out=rs_[:], in0=ss_[:], in1=mhalf[:, 0:1], op=ALU.pow), dur=0.7)
                K.op(DVE, [xi, rs_, fg_bc], [xoo], lambda xi=xi, rs_=rs_, xoo=xoo: E[DVE].scalar_tensor_tensor(
                    out=xoo[:, 0:D], in0=xi[:], scalar=rs_[:, 0:1], in1=fg_bc[:], op0=ALU.mult, op1=ALU.mult), dur=1.1)
                dma(dr(out_d), out_d[ti * 256 + sub * 128: ti * 256 + (sub + 1) * 128, :], xoo, xoo[:, 0:D])
        K.run()
        K.finish(SP)
    return nc


def _host_inputs(S, x, c, positions, w_ada, b_ada, norm_g, w_in, conv_w, conv_b,
                 b_igate, b_fgate, gn_m, gn_r, w_out, final_g):
    f32 = np.float32
    L = DEPTH
    shared = {}
    shared["wada"] = np.ascontiguousarray(w_ada, dtype=f32)
    shared["bada_fm"] = np.ascontiguousarray(b_ada[:, :2048].reshape(L, 16, 128).transpose(0, 2, 1), dtype=f32)
    shared["bada_g"] = np.ascontiguousarray(b_ada[:, 2048:].reshape(L, 1, D), dtype=f32)
    shared["normg_fm"] = np.ascontiguousarray(norm_g.reshape(L, 8, 128).transpose(0, 2, 1), dtype=f32)
    shared["finalg"] = np.ascontiguousarray(final_g.reshape(1, D), dtype=f32)
    shared["ident_b"] = np.eye(128, dtype=f32).astype(ml_dtypes.bfloat16)
    shared["ident_f"] = np.eye(128, dtype=f32)
    s_idx = np.arange(128)[:, None]
    l_idx = np.arange(128)[None, :]
    shared["mask01"] = (s_idx <= l_idx).astype(f32)
    bm = np.zeros((2, 2, 4), f32)
    bm[0, 0, :] = 1.0
    bm[1, 1, :] = 1.0
    shared["bmask"] = bm.reshape(2, 8)
    inv_freq = (1.0 / (10000.0 ** np.linspace(0.0, 1.0, 128, dtype=np.float32))).astype(f32)
    o = {"mq": 0, "mk": 1024, "mv": 2048, "mo": 3072, "mz": 4096, "mi": 5120, "mf": 5124,
         "rq": 5128, "rk": 6152, "rv": 7176, "rz": 8200}
    per_g = []
    for g in range(2):
        d = {}
        sl = lambda name: slice(o[name] + g * 512, o[name] + (g + 1) * 512)
        d["wf"] = np.ascontiguousarray(np.concatenate(
            [w_in[:, :, sl("mq")], w_in[:, :, sl("mk")], w_in[:, :, sl("rq")], w_in[:, :, sl("rk")]], axis=2), dtype=f32)
        d["wt"] = np.ascontiguousarray(np.concatenate(
            [w_in[:, :, sl("mv")], w_in[:, :, sl("mo")], w_in[:, :, sl("mz")], w_in[:, :, sl("rv")],
             w_in[:, :, sl("rz")]], axis=2), dtype=f32)
        d["wg"] = np.ascontiguousarray(np.concatenate(
            [w_in[:, :, o["mi"] + 2 * g:o["mi"] + 2 * g + 2], w_in[:, :, o["mf"] + 2 * g:o["mf"] + 2 * g + 2]], axis=2),
            dtype=f32)
        cwq = conv_w[:, :, g * 512:(g + 1) * 512]
        cwk = conv_w[:, :, 1024 + g * 512:1024 + (g + 1) * 512]
        cwa = np.concatenate([cwq, cwk], axis=2)
        d["cw"] = np.ascontiguousarray(cwa.reshape(L, 4, 8, 128).transpose(0, 3, 2, 1).reshape(L, 128, 32), dtype=f32)
        cba = np.concatenate([conv_b[:, g * 512:(g + 1) * 512], conv_b[:, 1024 + g * 512:1024 + (g + 1) * 512]], axis=1)
        d["cb"] = np.ascontiguousarray(cba.reshape(L, 8, 128).transpose(0, 2, 1), dtype=f32)
        d["bg"] = np.ascontiguousarray(np.stack([b_igate[:, 2 * g:2 * g + 2], b_fgate[:, 2 * g:2 * g + 2]], axis=1).reshape(L, 2, 2, 1), dtype=f32)
        gno = np.concatenate([gn_m[:, g * 512:(g + 1) * 512], gn_r[:, g * 512:(g + 1) * 512]], axis=1)
        d["gn_fm"] = np.ascontiguousarray(gno.reshape(L, 8, 128).transpose(0, 2, 1), dtype=f32)
        d["wo"] = np.ascontiguousarray(np.concatenate(
            [w_out[:, g * 512:(g + 1) * 512, :], w_out[:, 1024 + g * 512:1024 + (g + 1) * 512, :]], axis=1), dtype=f32)
        mrt = np.zeros((128, 2, 128), np.float64)
        tabs = np.zeros((128, 8), np.float64)
        for hh in range(2):
            h = 2 * g + hh
            gamma = 1.0 - 2.0 ** (-5.0 - h)
            s = np.arange(128, dtype=np.float64)
            mrt[:, hh, :] = np.where(s_idx <= l_idx, (gamma ** (-s))[:, None] / 16.0, 0.0)
            tabs[:, hh] = gamma ** s
            tabs[:, 2 + hh] = gamma ** (128.0 - s) / 16.0
            tabs[:, 4 + hh] = gamma ** 128.0
        tabs[:, 6] = inv_freq.astype(np.float64)
        d["mrt"] = mrt.reshape(128, 256).astype(f32)
        d["tabs"] = tabs.astype(f32)
        per_g.append(d)
    in_maps = []
    for core in range(8):
        b, g = core // 2, core % 2
        m = dict(shared)
        m.update(per_g[g])
        m["x"] = np.ascontiguousarray(x[b, :S], dtype=f32)
        m["pos"] = np.ascontiguousarray(positions[b, :S].reshape(1, S), dtype=np.int32)
        m["c_fm"] = np.ascontiguousarray(c[b].reshape(8, 128).T, dtype=f32)
        in_maps.append(m)
    return in_maps


_NC_CACHE = {}


def run(S, debug=False, **inputs):
    inputs = {k: np.asarray(v) for k, v in inputs.items()}
    key = (S, debug)
    if key not in _NC_CACHE:
        _NC_CACHE[key] = build(S, debug)
    nc = _NC_CACHE[key]
    in_maps = _host_inputs(S, **inputs)
    res = run_bass_kernel_spmd(nc, in_maps, core_ids=list(range(8)))
    NT = S // TT
    out = np.zeros((4, S, D), np.float32)
    for core in range(8):
        b, r = core // 2, core % 2
        o = np.asarray(res.results[core]["out"]).reshape(NT, 256, D)
        out[b].reshape(NT, 2, 256, D)[:, r] = o
    return out, res


def kernel(**inputs):
    S = np.asarray(inputs["x"]).shape[1]
    out, _ = run(S, False, **inputs)
    return out
```
